# Optimizing a Trainium2 kernel written in Bass

```python
import math
import jax, jax.numpy as jnp
from jax import lax
import numpy as np

D_MODEL = 2048
BATCH = 4
SEQ = 2048
DEPTH = 1
DEC_BATCH = 128
DEC_SEQ = 1
PAST_LEN = 16384
PAGE_SIZE = 128

RET_HEADS = 8
RET_QK_DIM = D_MODEL // RET_HEADS
RET_V_DIM = 2 * RET_QK_DIM
D_RET_QK = RET_HEADS * RET_QK_DIM
D_RET_V = RET_HEADS * RET_V_DIM
RET_CHUNK = 128
ROPE_BASE = 10000.0
POOL_WINDOWS = (2, 4, 8, 16)
POOL_GROUPS = 4
D_POOL = D_MODEL
POOL_GROUP_DIM = D_POOL // POOL_GROUPS
POOL_STATE = 15
D_FF = 5632
CONV_W = 3
N_MOD = 6
EPS = 1e-6
GN_EPS = 1e-5
IN_SIZES = (D_RET_QK, D_RET_QK, D_RET_V, D_RET_V, D_POOL, D_MODEL, D_MODEL)
D_IN = D_RET_QK * 2 + D_RET_V * 2 + D_POOL + 2 * D_MODEL

kernel_name = 'retention_pool_convffn_hybrid_step'


def _rmsnorm(x, g):
    xf = x.astype(jnp.float32)
    y = xf * lax.rsqrt(jnp.mean(xf * xf, axis=-1, keepdims=True) + EPS)
    return (y * g.astype(jnp.float32)).astype(x.dtype)


def _rotary(x, pos):
    half = x.shape[-1] // 2
    inv = ROPE_BASE ** (-jnp.arange(half, dtype=jnp.float32) / half)
    ang = pos.astype(jnp.float32)[:, None] * inv[None, :]
    cos, sin = jnp.cos(ang), jnp.sin(ang)
    x1, x2 = x[..., :half], x[..., half:]
    return jnp.concatenate([x1 * cos - x2 * sin, x2 * cos + x1 * sin], axis=-1)


def _retention_chunk(S, q, k, v, log_g):
    L = q.shape[2]
    idx = jnp.arange(L, dtype=jnp.float32)
    diff = idx[:, None] - idx[None, :]
    decay = jnp.where(diff >= 0, jnp.exp(jnp.maximum(diff, 0.0)[None] * log_g[:, None, None]), 0.0)
    scores = jnp.einsum('bhld,bhmd->bhlm', q, k) * decay[None]
    q_dec = q * jnp.exp((idx[None, :] + 1.0) * log_g[:, None])[None, :, :, None]
    o = jnp.einsum('bhlm,bhmv->bhlv', scores, v) + jnp.einsum('bhld,bhdv->bhlv', q_dec, S)
    k_dec = k * jnp.exp((L - 1.0 - idx[None, :]) * log_g[:, None])[None, :, :, None]
    S_new = jnp.exp(L * log_g)[None, :, None, None] * S + jnp.einsum('bhmd,bhmv->bhdv', k_dec, v)
    return S_new, o


def _retention(q, k, v, S0):
    B, H, T, _ = q.shape
    chunk = RET_CHUNK if T % RET_CHUNK == 0 else T
    nc = T // chunk
    log_g = jnp.log(1.0 - 2.0 ** (-5.0 - jnp.arange(H, dtype=jnp.float32)))

    def blocks(a):
        return jnp.moveaxis(a.reshape(B, H, nc, chunk, a.shape[-1]), 2, 0)

    def step(S, blk):
        qc, kc, vc = blk
        return _retention_chunk(S, qc, kc, vc, log_g)

    S_fin, o = lax.scan(step, S0, (blocks(q), blocks(k), blocks(v)))
    o = jnp.moveaxis(o, 0, 2).reshape(B, H, T, v.shape[-1])
    return o, S_fin


def _group_norm_heads(o, g):
    mu = jnp.mean(o, axis=-1, keepdims=True)
    var = jnp.mean(jnp.square(o - mu), axis=-1, keepdims=True)
    y = (o - mu) * lax.rsqrt(var + GN_EPS)
    B, H, T, DV = o.shape
    return y.transpose(0, 2, 1, 3).reshape(B, T, H * DV) * g.astype(jnp.float32)


def _pool_mix(u_ext, pos0, T):
    P = POOL_STATE
    uf = u_ext.astype(jnp.float32)
    B = uf.shape[0]
    cs = jnp.concatenate([jnp.zeros((B, 1, D_POOL), jnp.float32), jnp.cumsum(uf, axis=1)], axis=1)
    t = jnp.arange(T, dtype=jnp.int32)
    outs = []
    for gi, w in enumerate(POOL_WINDOWS):
        sl = slice(gi * POOL_GROUP_DIM, (gi + 1) * POOL_GROUP_DIM)
        s = cs[:, P + 1:P + 1 + T, sl] - cs[:, P + 1 - w:P + 1 - w + T, sl]
        cnt = jnp.minimum(pos0 + t + 1, w).astype(jnp.float32)
        outs.append(s / cnt[None, :, None])
    return jnp.concatenate(outs, axis=-1) - uf[:, P:]


def _layer(x, c, pos0, S_ret, pool_prefix, conv_prefix, norm1_g, w_mod, b_mod, w_in, gn_g,
           w_proj_a, w_pool_grp, pool_scale, w_proj_b, w_out, norm2_g, w_up, conv_w, conv_b, w_down):
    B, T, _ = x.shape
    dt = x.dtype
    mod = (jax.nn.silu(c) @ w_mod + b_mod).reshape(B, N_MOD, D_MODEL)
    shift1, scale1, gate1, shift2, scale2, gate2 = [mod[:, i][:, None, :] for i in range(N_MOD)]

    h = _rmsnorm(x, norm1_g) * (1.0 + scale1) + shift1
    z = h @ w_in
    q, k, v, gr, u, ga, gb = jnp.split(z, [int(s) for s in np.cumsum(IN_SIZES)[:-1]], axis=-1)
    pos = pos0 + jnp.arange(T, dtype=jnp.int32)
    q = _rotary(q.astype(jnp.float32).reshape(B, T, RET_HEADS, RET_QK_DIM).transpose(0, 2, 1, 3), pos)
    k = _rotary(k.astype(jnp.float32).reshape(B, T, RET_HEADS, RET_QK_DIM).transpose(0, 2, 1, 3), pos)
    k = k * (RET_QK_DIM ** -0.5)
    v = v.astype(jnp.float32).reshape(B, T, RET_HEADS, RET_V_DIM).transpose(0, 2, 1, 3)
    o, S_new = _retention(q, k, v, S_ret.astype(jnp.float32))
    y_a = (_group_norm_heads(o, gn_g).astype(dt) * jax.nn.silu(gr)) @ w_proj_a

    u_ext = jnp.concatenate([pool_prefix.astype(dt), u], axis=1)
    p = _pool_mix(u_ext, pos0, T).astype(dt)
    p = jnp.einsum('btgc,gcd->btgd', p.reshape(B, T, POOL_GROUPS, POOL_GROUP_DIM), w_pool_grp)
    y_b = (p.reshape(B, T, D_POOL) * pool_scale) @ w_proj_b

    mixed = jax.nn.sigmoid(ga) * y_a + jax.nn.sigmoid(gb) * y_b
    x = x + gate1 * (mixed @ w_out)

    h2 = _rmsnorm(x, norm2_g) * (1.0 + scale2) + shift2
    up = h2 @ w_up
    ext = jnp.concatenate([conv_prefix.astype(dt), up], axis=1)
    conv = conv_b + sum(ext[:, j:j + T] * conv_w[j] for j in range(CONV_W))
    a, bg = jnp.split(conv, 2, axis=-1)
    x = x + gate2 * ((jax.nn.silu(a) * bg) @ w_down)
    return x, S_new, u_ext[:, -POOL_STATE:], ext[:, -(CONV_W - 1):]


def setup_inputs(seed: int = 0) -> dict:
    key = jax.random.key(seed)
    ks = jax.random.split(key, 24)
    f32 = jnp.float32
    nrm = lambda k, s, sc: jax.random.normal(k, s, f32) * sc
    L = DEPTH
    return {
        'x_prompt': nrm(ks[0], (BATCH, SEQ, D_MODEL), 1.0),
        'x_sample': nrm(ks[1], (DEC_BATCH, DEC_SEQ, D_MODEL), 1.0),
        'c_prompt': nrm(ks[2], (BATCH, D_MODEL), 1.0),
        'c_sample': nrm(ks[3], (DEC_BATCH, D_MODEL), 1.0),
        'state_ret': nrm(ks[4], (L, DEC_BATCH, RET_HEADS, RET_QK_DIM, RET_V_DIM), 1.0),
        'state_pool': nrm(ks[5], (L, DEC_BATCH, POOL_STATE, D_POOL), 1.0),
        'state_conv': nrm(ks[6], (L, DEC_BATCH, CONV_W - 1, 2 * D_FF), 1.0),
        'norm1_g': 1.0 + nrm(ks[7], (L, D_MODEL), 0.02),
        'w_mod': nrm(ks[8], (L, D_MODEL, N_MOD * D_MODEL), 0.1 * D_MODEL ** -0.5),
        'b_mod': nrm(ks[9], (L, N_MOD * D_MODEL), 0.1),
        'w_in': nrm(ks[10], (L, D_MODEL, D_IN), D_MODEL ** -0.5),
        'gn_g': 1.0 + nrm(ks[11], (L, D_RET_V), 0.02),
        'w_proj_a': nrm(ks[12], (L, D_RET_V, D_MODEL), D_RET_V ** -0.5),
        'w_pool_grp': nrm(ks[13], (L, POOL_GROUPS, POOL_GROUP_DIM, POOL_GROUP_DIM), POOL_GROUP_DIM ** -0.5),
        'pool_scale': 1.0 + nrm(ks[14], (L, D_POOL), 0.1),
        'w_proj_b': nrm(ks[15], (L, D_POOL, D_MODEL), D_POOL ** -0.5),
        'w_out': nrm(ks[16], (L, D_MODEL, D_MODEL), D_MODEL ** -0.5),
        'norm2_g': 1.0 + nrm(ks[17], (L, D_MODEL), 0.02),
        'w_up': nrm(ks[18], (L, D_MODEL, 2 * D_FF), D_MODEL ** -0.5),
        'conv_w': nrm(ks[19], (L, CONV_W, 2 * D_FF), CONV_W ** -0.5),
        'conv_b': nrm(ks[20], (L, 2 * D_FF), 0.02),
        'w_down': nrm(ks[21], (L, D_FF, D_MODEL), D_FF ** -0.5),
        'norm_f_g': 1.0 + nrm(ks[22], (D_MODEL,), 0.02),
    }


def reference(x_prompt, x_sample, c_prompt, c_sample, state_ret, state_pool, state_conv,
              norm1_g, w_mod, b_mod, w_in, gn_g, w_proj_a, w_pool_grp, pool_scale, w_proj_b,
              w_out, norm2_g, w_up, conv_w, conv_b, w_down, norm_f_g):
    xp, xs = x_prompt, x_sample
    rp, rs, pp, ps, cp, cs = [], [], [], [], [], []
    for l in range(DEPTH):
        lw = (norm1_g[l], w_mod[l], b_mod[l], w_in[l], gn_g[l], w_proj_a[l], w_pool_grp[l],
              pool_scale[l], w_proj_b[l], w_out[l], norm2_g[l], w_up[l], conv_w[l], conv_b[l], w_down[l])
        S0 = jnp.zeros((BATCH, RET_HEADS, RET_QK_DIM, RET_V_DIM), jnp.float32)
        pool0 = jnp.zeros((BATCH, POOL_STATE, D_POOL), xp.dtype)
        conv0 = jnp.zeros((BATCH, CONV_W - 1, 2 * D_FF), xp.dtype)
        xp, s_r, s_p, s_c = _layer(xp, c_prompt, 0, S0, pool0, conv0, *lw)
        rp.append(s_r.astype(xp.dtype)); pp.append(s_p); cp.append(s_c)
        xs, s_r, s_p, s_c = _layer(xs, c_sample, PAST_LEN, state_ret[l], state_pool[l], state_conv[l], *lw)
        rs.append(s_r.astype(state_ret.dtype)); ps.append(s_p.astype(state_pool.dtype)); cs.append(s_c.astype(state_conv.dtype))
    y_prompt = _rmsnorm(xp, norm_f_g)
    y_sample = _rmsnorm(xs, norm_f_g)
    return (y_prompt, y_sample, jnp.stack(rp), jnp.stack(rs), jnp.stack(pp), jnp.stack(ps), jnp.stack(cp), jnp.stack(cs))
```

```python
import numpy as np
import concourse.bass as bass
import concourse.mybir as mybir
from concourse.bass_utils import run_bass_kernel_spmd

F32 = mybir.dt.float32
BF16 = mybir.dt.bfloat16
AF = mybir.ActivationFunctionType
ALU = mybir.AluOpType
AX = mybir.AxisListType


class _Op:
    __slots__ = ("eng", "fn", "deps", "is_dma", "sem", "semval", "signal", "prev_dma")

    def __init__(self, eng, fn, is_dma=False):
        self.eng = eng
        self.fn = fn
        self.deps = []
        self.is_dma = is_dma
        self.sem = None
        self.semval = None
        self.signal = False
        self.prev_dma = None


class _Rec:
    def __init__(self):
        self.call = None

    def __getattr__(self, name):
        def f(*a, **kw):
            self.call = (name, a, kw)
            return self
        return f


def _bind(fn):
    rec = _Rec()
    fn(rec)
    c = rec.call
    return lambda e: getattr(e, c[0])(*c[1], **c[2])


class Sched:
    ENGS = ("pe", "act", "dve", "pool", "sp")

    def __init__(self, ndma=20):
        self.ops = {e: [] for e in self.ENGS}
        self.last_w = {}
        self.readers = {}
        self.ndma = ndma
        self.dma_rr = {e: 0 for e in self.ENGS}
        self.dma_last = {}
        self.dma_count = {}

    def _add(self, op, reads, writes):
        deps = []
        for k in reads:
            w = self.last_w.get(k)
            if w is not None:
                deps.append(w)
        for k in writes:
            w = self.last_w.get(k)
            if w is not None:
                deps.append(w)
            for r in self.readers.get(k, ()):
                if (not r.is_dma) and (not op.is_dma) and r.eng == op.eng and op.eng in ("dve", "act"):
                    continue
                deps.append(r)
        seen = set()
        for d in deps:
            if d is op or id(d) in seen:
                continue
            seen.add(id(d))
            if (not d.is_dma) and (not op.is_dma) and d.eng == "pe" and op.eng == "pe":
                continue
            op.deps.append(d)
        for k in reads:
            self.readers.setdefault(k, []).append(op)
        for k in writes:
            self.last_w[k] = op
            self.readers[k] = []
        self.ops[op.eng].append(op)
        return op

    def op(self, eng, fn, reads=(), writes=()):
        return self._add(_Op(eng, _bind(fn)), reads, writes)

    def dma(self, eng, fn, reads=(), writes=()):
        op = _Op(eng, _bind(fn), is_dma=True)
        slot = (eng, self.dma_rr[eng] % self.ndma)
        self.dma_rr[eng] += 1
        op.sem = slot
        op.prev_dma = self.dma_last.get(slot)
        self.dma_count[slot] = self.dma_count.get(slot, 0) + 1
        op.semval = 16 * self.dma_count[slot]
        self.dma_last[slot] = op
        return self._add(op, reads, writes)

    def barrier(self, markers):
        ms = [self.op(e, fn, reads, writes) for (e, fn, reads, writes) in markers]
        lastd = list(self.dma_last.values())
        for e in self.ENGS:
            b = _Op(e, None)
            b.deps = list(ms) + lastd
            self.ops[e].append(b)

    def emit(self, nc, final_deps):
        for e in self.ENGS:
            for op in self.ops[e]:
                for d in op.deps:
                    if not d.is_dma:
                        d.signal = True
        fence = _Op("sp", None)
        fence.deps = list(final_deps)
        for d in fence.deps:
            if not d.is_dma:
                d.signal = True
        self.ops["sp"].append(fence)
        for e in self.ENGS:
            n = 0
            for op in self.ops[e]:
                if op.is_dma or op.fn is None:
                    continue
                if op.signal:
                    n += 1
                    op.semval = n
                    op.sem = ("eng", e)
        sem_names = [("eng", e) for e in self.ENGS]
        sem_names += sorted(set(self.dma_last.keys()))
        ctx = []
        sems = {}
        for sn in sem_names:
            cm = nc.semaphore("s_" + "_".join(str(x) for x in sn))
            sems[sn] = cm.__enter__()
            ctx.append(cm)
        sched = self

        def run(engobj, ename):
            waited = {}

            def wait(sn, val):
                if waited.get(sn, 0) >= val:
                    return
                waited[sn] = val
                engobj.wait_ge(sems[sn], val)

            for op in sched.ops[ename]:
                for d in op.deps:
                    wait(d.sem, d.semval)
                if op.is_dma and op.prev_dma is not None:
                    wait(op.prev_dma.sem, op.prev_dma.semval)
                if op.fn is None:
                    continue
                ins = op.fn(engobj)
                if op.is_dma:
                    ins.then_inc(sems[op.sem], 16)
                elif op.signal:
                    ins.then_inc(sems[op.sem], 1)

        with nc.Block() as block:
            @block.tensor
            def _(e):
                run(e, "pe")

            @block.scalar
            def _(e):
                run(e, "act")

            @block.vector
            def _(e):
                run(e, "dve")

            @block.gpsimd
            def _(e):
                run(e, "pool")

            @block.sync
            def _(e):
                run(e, "sp")
        for cm in reversed(ctx):
            cm.__exit__(None, None, None)


class Cfg:
    def __init__(self, D=2048, H=8, DFF=5632, NHT=8, NS=16, PAST=16384, NB=4):
        self.D, self.H, self.DFF, self.NHT, self.NS, self.PAST, self.NB = D, H, DFF, NHT, NS, PAST, NB
        self.KC = D // 128
        self.DK, self.DV = 256, 512
        self.DQK, self.DVT = H * 256, H * 512
        self.DIN = 2 * self.DQK + 2 * self.DVT + 3 * D
        self.NPRE = NHT - 1
        self.NT = NHT + 1
        self.NCOL = self.NT * 128 + NS
        self.SC0 = self.NT * 128
        self.FC = DFF // 128
        self.NG = self.FC // 4
        self.offs = np.cumsum([0, self.DQK, self.DQK, self.DVT, self.DVT, D, D, D])
        self.SEQ = 2 * NHT * 128


def _ngroups(n):
    out, c = [], 0
    while c < n:
        w = min(512, n - c)
        out.append((c, w))
        c += w
    return out


def build_program(cfg):
    from contextlib import ExitStack
    D, H, KC, NT, NS, NCOL, SC0, NPRE = cfg.D, cfg.H, cfg.KC, cfg.NT, cfg.NS, cfg.NCOL, cfg.SC0, cfg.NPRE
    DFF, FC, NG = cfg.DFF, cfg.FC, cfg.NG
    nc = bass.Bass("TRN2", target_bir_lowering=False)
    S = Sched()

    def din(name, shape, dt=F32):
        return nc.dram_tensor(name, list(shape), dt, kind="ExternalInput").ap()

    def dout(name, shape, dt=F32):
        return nc.dram_tensor(name, list(shape), dt, kind="ExternalOutput").ap()

    def dscr(name, shape, dt):
        return nc.dram_tensor(name, list(shape), dt, kind="Internal").ap()

    xw = din("xw", [(NPRE + NT) * 128, D])
    xs = din("xs", [NS, D])
    c17 = din("c17", [NS + 1, D])
    premask = din("premask", [1, 128])
    cosT = din("cos_t", [128, NT + NPRE + 1, 128])
    sinT = din("sin_t", [128, NT + NPRE + 1, 128])
    decayT = din("decay_t", [128, H, 128])
    gq = din("gq", [128, H])
    kd = din("kd", [128, H])
    invc = din("invc", [128, 4, 16])
    ident_d = din("ident", [128, 128])
    m16_d = din("m16", [128, NS, NS])
    norm1T = din("norm1T", [128, KC])
    norm2T = din("norm2T", [128, KC])
    bmodT = din("bmodT", [128, 6 * KC])
    bmod_row = din("bmod_row", [1, 6 * D])
    gn_g = din("gn_g", [1, cfg.DVT])
    pscaleT = din("pscaleT", [128, KC])
    convwT = din("convwT", [128, 3, 2 * FC])
    convbT = din("convbT", [128, 2 * FC])
    normf = din("normf", [1, D])
    w_mod = din("w_mod", [D, 6 * D])
    w_in = din("w_in", [D, cfg.DIN])
    w_pa = din("w_proj_a", [cfg.DVT, D])
    w_pg = din("w_pool_grp", [4, D // 4, D // 4])
    w_pb = din("w_proj_b", [D, D])
    w_out = din("w_out", [D, D])
    w_up = din("w_up", [D, 2 * DFF])
    w_dn = din("w_down", [DFF, D])
    st_ret = din("st_ret", [NS, H, 256, 512])
    st_pool = din("st_pool", [NS, 15, D])
    st_conv = din("st_conv", [NS, 2, 2 * DFF])
    y_main = dout("y_main", [cfg.NHT * 128, D])
    y_s = dout("y_s", [NS, D])
    ret_p = dout("ret_p", [H, 256, 512])
    ret_s = dout("ret_s", [NS, H, 256, 512])
    pool_p = dout("pool_p", [15, D])
    pool_s = dout("pool_s", [NS, 15, D])
    conv_p = dout("conv_p", [2, 2 * DFF])
    conv_s = dout("conv_s", [NS, 2, 2 * DFF])
    hpre_d = dscr("hpre_d", [max(NPRE, 1), 128, KC * 128], BF16)
    actT_d = dscr("actT_d", [H * 4, 128, NCOL], BF16)
    sga_d = dscr("sga_d", [KC, 128, NCOL], BF16)
    sgb_d = dscr("sgb_d", [KC, 128, NCOL], BF16)

    outs = []
    uid = [0]

    def U(p="k"):
        uid[0] += 1
        return "%s%d" % (p, uid[0])

    es = ExitStack()
    with es:
        scopes = [es]

        def sb(name, shape, dt):
            return scopes[-1].enter_context(nc.sbuf_tensor("sb_" + name, list(shape), dt))

        def push_scope():
            scopes.append(ExitStack())

        def pop_scope():
            do_barrier()
            scopes.pop().close()

        banks = [es.enter_context(nc.psum_tensor("bank%d" % i, [128, 512], F32)) for i in range(4)]
        bank45 = es.enter_context(nc.psum_tensor("bank45", [128, 2, 512], F32))
        banks += [bank45[:, 0, :], bank45[:, 1, :]]
        banks += [es.enter_context(nc.psum_tensor("bank%d" % i, [128, 512], F32)) for i in (6, 7)]
        bank_rr = [0]

        def bank():
            i = bank_rr[0] % 7
            bank_rr[0] += 1
            return banks[i], "bank%d" % i

        NSLOT = 3
        slots = [sb("wslot%d" % i, [128, 8192], BF16) for i in range(NSLOT)]
        slot_rr = [0]

        def load_slab(parts, nk, ncols):
            i = slot_rr[0] % len(slots)
            slot_rr[0] += 1
            key = "wslot%d" % i
            view = slots[i][:, 0:nk * ncols].rearrange("p (k n) -> p k n", k=nk)
            for ap, c0 in parts:
                w = ap.shape[1]
                S.dma("pool", lambda e, ap=ap, c0=c0, w=w: e.dma_start(
                    out=view[:, :, c0:c0 + w], in_=ap.rearrange("(k p) n -> p k n", p=128)), writes=[key])
            return view, key

        ident = sb("ident", [128, 128], F32)
        identb = sb("identb", [128, 128], BF16)
        S.dma("sp", lambda e: e.dma_start(out=ident[:], in_=ident_d), writes=["ident"])
        S.op("dve", lambda e: e.tensor_copy(out=identb[:], in_=ident[:]), reads=["ident"], writes=["identb"])
        bscr = sb("bscr", [128, 8], F32)

        def do_barrier():
            S.barrier([
                ("pe", lambda e: e.matmul(out=banks[6][0:1, 0:1], lhsT=identb[0:1, 0:1], rhs=identb[0:1, 0:1], start=True, stop=True), ["identb"], ["bank6", "bank6b"]),
                ("act", lambda e: e.activation(out=bscr[:, 0:1], in_=ident[:, 0:1], func=AF.Copy), ["ident"], ["bscr0"]),
                ("dve", lambda e: e.memset(bscr[:, 1:2], 0.0), [], ["bscr1"]),
                ("pool", lambda e: e.memset(bscr[:, 2:3], 0.0), [], ["bscr2"]),
            ])

        small_f = sb("small_f", [128, 9 * KC + 3 * H + 64 + 8 * FC], F32)
        o = 0
        def carve(n):
            nonlocal o
            v = small_f[:, o:o + n]
            o += n
            return v
        n1T, n2T, bmT0 = carve(KC), carve(KC), carve(6 * KC)
        gq_s, kd_s, invc_s, psc_s = carve(H), carve(H), carve(64), carve(KC)
        cw_s, cb_s = carve(6 * FC), carve(2 * FC)
        kd16 = carve(H)
        cw_s3 = cw_s.rearrange("p (r c) -> p r c", r=3)
        invc_s3 = invc_s.rearrange("p (g c) -> p g c", g=4)
        for dst, src in ((n1T, norm1T), (n2T, norm2T), (bmT0, bmodT), (gq_s, gq), (kd_s, kd), (psc_s, pscaleT), (cb_s, convbT)):
            S.dma("sp", lambda e, dst=dst, src=src: e.dma_start(out=dst, in_=src), writes=["consts"])
        S.dma("sp", lambda e: e.dma_start(out=cw_s3, in_=convwT), writes=["consts"])
        S.op("dve", lambda e: e.tensor_scalar(out=kd16, in0=kd_s, scalar1=0.0625, scalar2=None, op0=ALU.mult), reads=["consts"], writes=["kd16"])
        S.dma("sp", lambda e: e.dma_start(out=invc_s3, in_=invc), writes=["consts"])
        NCS = NT + NPRE + 1

        NM = NS + 1
        gate_d = dscr("gate_d", [NM, 2 * D], F32)
        modk_d = dscr("modk_d", [128, 4, KC * NM], F32)
        scT = sb("scT", [128, KC, NM], BF16)
        SPM = D // 512

        def mod_slab(sl, modbuf, ob0, gt_fn):
            wv, wk = load_slab([(w_mod[:, sl * 512:(sl + 1) * 512], 0)], KC, 512)
            pb, pk = bank()
            for j in range(4):
                for kc in range(KC):
                    S.op("pe", lambda e: e.matmul(out=pb[:, j * NM:(j + 1) * NM], lhsT=wv[:, kc, j * 128:(j + 1) * 128],
                         rhs=scT[:, kc, :], start=(kc == 0), stop=(kc == KC - 1)), reads=[wk, "scT"], writes=[pk])
            for j in range(4):
                ob = sl * 4 + j
                S.op("dve", lambda e: e.tensor_scalar(out=modbuf[:, ob - ob0, :], in0=pb[:, j * NM:(j + 1) * NM],
                     scalar1=bmT0[:, ob:ob + 1], scalar2=None, op0=ALU.add), reads=[pk, "consts"], writes=["modbuf"])
            mi = (sl * 512) // D
            if mi in (2, 5):
                gi = 0 if mi == 2 else 1
                c0 = sl * 512 - mi * D
                pb3, pk3 = bank()
                for kc in range(KC):
                    S.op("pe", lambda e: e.matmul(out=pb3[0:NM, :], lhsT=scT[:, kc, :], rhs=wv[:, kc, :],
                         start=(kc == 0), stop=(kc == KC - 1)), reads=[wk, "scT"], writes=[pk3])
                gt, gtk, br, brk = gt_fn()
                S.dma("sp", lambda e: e.dma_start(out=br[:], in_=bmod_row[:, mi * D + c0:mi * D + c0 + 512].partition_broadcast(NM)), writes=[brk])
                S.op("dve", lambda e: e.tensor_tensor(out=gt[:], in0=pb3[0:NM, :], in1=br[:], op=ALU.add), reads=[pk3, brk], writes=[gtk])
                S.dma("sp", lambda e: e.dma_start(out=gate_d[:, gi * D + c0:gi * D + c0 + 512], in_=gt[:]), reads=[gtk], writes=["gate_d"])

        def mod_finish(modbuf, A, Ak, nT, scale_rows, shift_rows, q0):
            S.op("dve", lambda e: e.tensor_scalar(out=A[:], in0=modbuf[:, scale_rows:scale_rows + KC, :], scalar1=1.0, scalar2=None, op0=ALU.add), reads=["modbuf"], writes=[Ak])
            S.op("dve", lambda e: e.tensor_tensor(out=A[:], in0=A[:], in1=nT.unsqueeze(2).to_broadcast([128, KC, NM]), op=ALU.mult), reads=[Ak, "consts"], writes=[Ak])
            S.dma("sp", lambda e: e.dma_start(out=modk_d[:, q0, :].rearrange("p (k n) -> p k n", k=KC), in_=A[:]), reads=[Ak], writes=["modk_d"])
            S.dma("sp", lambda e: e.dma_start(out=modk_d[:, q0 + 1, :].rearrange("p (k n) -> p k n", k=KC), in_=modbuf[:, shift_rows:shift_rows + KC, :]), reads=["modbuf"], writes=["modk_d"])

        push_scope()
        modT = sb("modT", [128, 2 * KC, NM], F32)
        A1 = sb("A1", [128, KC, NM], F32)
        csb = sb("csb", [NM, D], F32)
        S.dma("sp", lambda e: e.dma_start(out=csb[:], in_=c17), writes=["csb"])
        S.op("act", lambda e: e.activation(out=csb[:], in_=csb[:], func=AF.Silu), reads=["csb"], writes=["csb"])
        pb, pk = bank()
        pv = pb[:, 0:KC * NM].rearrange("p (k n) -> p k n", k=KC)
        for kc in range(KC):
            S.op("pe", lambda e, kc=kc: e.transpose(out=pv[:, kc, :], in_=csb[:, kc * 128:(kc + 1) * 128], identity=ident[0:NM, 0:NM]),
                 reads=["csb", "ident"], writes=[pk])
        S.op("dve", lambda e: e.tensor_copy(out=scT[:], in_=pv), reads=[pk], writes=["scT"])
        for sl in range(2 * SPM):
            mod_slab(sl, modT, 0, None)
        mod_finish(modT, A1, "A1", n1T, KC, 0, 0)
        pop_scope()

        stat = sb("stat", [128, 8 * 16], F32)
        st_rr = [0]
        stmp = sb("stmp", [128, 4, NS], F32)
        push_scope()
        hT = sb("hT", [128, KC, NCOL], BF16)
        push_scope()
        xring = [sb("xring%d" % i, [128, D], F32) for i in range(2)]
        xr_rr = [0]
        sqj = sb("sqj", [128, D], BF16)
        pmask = sb("pmask", [128, 128], F32)
        S.dma("sp", lambda e: e.dma_start(out=pmask[:], in_=premask.partition_broadcast(128)), writes=["consts"])
        mA = sb("mA", [128, 2, KC, NM], F32)
        S.dma("sp", lambda e: e.dma_start(out=mA[:], in_=modk_d[:, 0:2, :].rearrange("p a (k n) -> p a k n", k=KC)), reads=["modk_d"], writes=["mA"])

        def stat4():
            i = st_rr[0] % 16
            st_rr[0] += 1
            return stat[:, i * 8:(i + 1) * 8], "stat%d" % i

        def rstd_of(xt, xk, n, st, sk, eps=1e-6):
            S.op("act", lambda e: e.activation(out=sqj[0:n, :], in_=xt[0:n, :], func=AF.Square, accum_out=st[0:n, 0:1]), reads=[xk], writes=["sqj", sk])
            S.op("dve", lambda e: e.tensor_scalar(out=st[0:n, 1:2], in0=st[0:n, 0:1], scalar1=1.0 / D, scalar2=eps, op0=ALU.mult, op1=ALU.add), reads=[sk], writes=[sk])
            S.op("act", lambda e: e.activation(out=st[0:n, 1:2], in_=st[0:n, 1:2], func=AF.Sqrt), reads=[sk], writes=[sk])
            S.op("dve", lambda e: e.reciprocal(out=st[0:n, 2:3], in_=st[0:n, 1:2]), reads=[sk], writes=[sk])

        def norm_T(xt, xk, n, dest_fn, dkey, mm, mmk):
            st, sk = stat4()
            rstd_of(xt, xk, n, st, sk)
            S.op("act", lambda e: e.activation(out=xt[0:n, :], in_=xt[0:n, :], func=AF.Copy, scale=st[0:n, 2:3]), reads=[xk, sk], writes=[xk])
            for k0 in range(0, KC, 4):
                pb, pk = bank()
                for j in range(min(4, KC - k0)):
                    kc = k0 + j
                    S.op("pe", lambda e, kc=kc, j=j, pb=pb: e.transpose(out=pb[:, j * 128:j * 128 + n], in_=xt[0:n, kc * 128:(kc + 1) * 128], identity=ident[0:n, 0:n]),
                         reads=[xk, "ident"], writes=[pk])
                for j in range(min(4, KC - k0)):
                    kc = k0 + j
                    if n == 128:
                        S.op("act", lambda e, kc=kc, j=j, pb=pb: e.activation(out=dest_fn(kc), in_=pb[:, j * 128:(j + 1) * 128], func=AF.Identity,
                             scale=mm[:, 0, kc, 0:1], bias=mm[:, 1, kc, 0:1]), reads=[pk, mmk], writes=[dkey])
                    else:
                        S.op("dve", lambda e, kc=kc, j=j, pb=pb: e.tensor_tensor(out=stmp[:, j, :], in0=pb[:, j * 128:j * 128 + n], in1=mm[:, 0, kc, 1:NM], op=ALU.mult),
                             reads=[pk, mmk], writes=["stmp%d" % j])
                        S.op("dve", lambda e, kc=kc, j=j, pb=pb: e.tensor_tensor(out=dest_fn(kc), in0=stmp[:, j, :], in1=mm[:, 1, kc, 1:NM], op=ALU.add),
                             reads=["stmp%d" % j, mmk], writes=[dkey])

        def load_x(src_ap, n):
            i = xr_rr[0] % 2
            xr_rr[0] += 1
            xt, xk = xring[i], "xring%d" % i
            S.dma("sp", lambda e: e.dma_start(out=xt[0:n, :], in_=src_ap), writes=[xk])
            return xt, xk

        hstage = [sb("hstage%d" % i, [128, KC, 128], BF16) for i in range(2)]
        pm_b = pmask[:].unsqueeze(1).to_broadcast([128, KC, 128])
        for p in range(NPRE):
            xt, xk = load_x(xw[p * 128:(p + 1) * 128, :], 128)
            hs, hk = hstage[p % 2], "hstage%d" % (p % 2)
            norm_T(xt, xk, 128, lambda kc, hs=hs: hs[:, kc, :], hk, mA, "mA")
            S.op("dve", lambda e, hs=hs: e.tensor_tensor(out=hs[:], in0=hs[:], in1=pm_b, op=ALU.mult), reads=[hk, "consts"], writes=[hk])
            S.dma("sp", lambda e, hs=hs, p=p: e.dma_start(out=hpre_d[p].rearrange("p (k n) -> p k n", k=KC), in_=hs[:]), reads=[hk], writes=["hpre%d" % p])
        for t in range(NT):
            xt, xk = load_x(xw[(NPRE + t) * 128:(NPRE + t + 1) * 128, :], 128)
            norm_T(xt, xk, 128, lambda kc, t=t: hT[:, kc, t * 128:(t + 1) * 128], "hT%d" % t, mA, "mA")
        S.op("dve", lambda e: e.tensor_tensor(out=hT[:, :, 0:128], in0=hT[:, :, 0:128], in1=pm_b, op=ALU.mult), reads=["hT0", "consts"], writes=["hT0"])
        xt, xk = load_x(xs, NS)
        norm_T(xt, xk, NS, lambda kc: hT[:, kc, SC0:SC0 + NS], "hTs", mA, "mA")
        HT_KEYS = ["hT%d" % t for t in range(NT)] + ["hTs"]
        pop_scope()

        push_scope()
        if True:
            sbh = sb
            decr = [sbh("decr%d" % i, [128, 128], F32) for i in range(2)]
            slots.append(sbh("wslot3", [128, 8192], BF16))
            m16 = sbh("m16", [128, NS, NS], F32)
            S.dma("sp", lambda e: e.dma_start(out=m16[:], in_=m16_d), writes=["consts"])
            cos_s = sbh("cos_s", [128, NCS, 128], F32)
            sin_s = sbh("sin_s", [128, NCS, 128], F32)
            S.dma("sp", lambda e: e.dma_start(out=cos_s[:], in_=cosT), writes=["consts"])
            S.dma("sp", lambda e: e.dma_start(out=sin_s[:], in_=sinT), writes=["consts"])
            hring = [sbh("hring%d" % i, [128, KC, 128], BF16) for i in range(2)]
            gnb = [sbh("gnb%d" % i, [128, 512], F32) for i in range(2)]
            rot = [sbh("rot%d" % i, [128, 2, 2, 128], F32) for i in range(2)]
            rtmp = [sbh("rtmp%d" % i, [128, 2, 2, 128], F32) for i in range(2)]
            tokb = [sbh("tokb%d" % i, [128, 8, 128], BF16) for i in range(2)]
            qkT = [sbh("qkT%d" % i, [128, 6, 128], BF16) for i in range(2)]
            vsb = [sbh("vsb%d" % i, [128, 512], BF16) for i in range(2)]
            sgr = [sbh("sgr%d" % i, [128, 512], F32) for i in range(2)]
            scs = [sbh("scs%d" % i, [128, 128], BF16) for i in range(2)]
            yn = [sbh("yn%d" % i, [128, 512], F32) for i in range(2)]
            actb = [sbh("actb%d" % i, [128, 512], BF16) for i in range(2)]
            Sst = sbh("Sst", [128, 2, 512], F32)
            Sbf = sbh("Sbf", [128, 2, 512], BF16)
            astage = sbh("astage", [128, 4, NCOL], BF16)
            sring = [sbh("sring%d" % i, [128, 2, 512], F32) for i in range(3)]
            NSR = 3
            tokb_s = sbh("tokb_s", [NS, 8, 128], BF16)
            vsb_s = sbh("vsb_s", [NS, 512], BF16)
            sgr_s = sbh("sgr_s", [NS, 512], F32)
            snb = [sbh("snb%d" % i, [128, 2, 512], BF16) for i in range(2)]
            KMH = max(NS // 2, 1)
            kmask_all = sbh("kmask_all", [NS, KMH, 256], BF16)
            qmask_all = sbh("qmask_all", [128, NS, 2, NS], BF16)
            qTs = sbh("qTs", [128, 2, NS], BF16)
            it = [0]
            sit = [0]
            log_g = [float(np.log(np.float32(1.0) - np.float32(2.0) ** np.float32(-5.0 - hh))) for hh in range(H)]

            ptr_v = banks[7][:].bitcast(BF16).rearrange("p (a n) -> p a n", a=8)
            psc = banks[7][:, 384:512]
            psck = "bank7s"
            ptr2_v = banks[6][:, 256:512].bitcast(BF16).rearrange("p (a n) -> p a n", a=4)
            pj_rr = [0]

            def pbank():
                i = pj_rr[0] % 2
                pj_rr[0] += 1
                return banks[i], "bank%d" % i

            class Ctx:
                pass

            def stageA_pe(c, which):
                n, lh, lk = c.n, c.lh, c.lk
                if which == "qk":
                    wv_, wk_ = c.wqk, c.wqkk
                    ncols = 512 if (c.is_main or c.is_s) else 256
                elif which == "v":
                    wv_, wk_ = c.wv, c.wvk
                    ncols = 512
                else:
                    if not (c.is_main or c.is_s):
                        return
                    wv_, wk_ = c.wg, c.wgk
                    ncols = 512
                pb, pk = pbank()
                c.pb[which] = (pb, pk)
                for kc in range(KC):
                    S.op("pe", lambda e: e.matmul(out=pb[0:n, 0:ncols], lhsT=lh(kc), rhs=wv_[:, kc, 0:ncols], start=(kc == 0), stop=(kc == KC - 1)), reads=lk + [wk_], writes=[pk])

            def stageA_rot(c):
                n, h, par = c.n, c.h, c.par
                pqk, pqkk = c.pb["qk"]
                na = 2 if (c.is_main or c.is_s) else 1
                ci = (NCS - 1) if c.is_s else c.w
                cb_ = cos_s[0:n, ci, :].unsqueeze(1).unsqueeze(1).to_broadcast([n, na, 2, 128])
                sb_ = sin_s[0:n, ci, :].unsqueeze(1).unsqueeze(1).to_broadcast([n, na, 2, 128])
                v4 = pqk[0:n, :].rearrange("p (a b n) -> p a b n", a=2, b=2)[:, 0:na]
                r, rk = rot[par], "rot%d" % par
                tc_, tck = rtmp[0][0:n, 0:na], "rtmp0"
                ts_, tsk = rtmp[1][0:n, 0:na], "rtmp1"
                S.op("dve", lambda e: e.tensor_tensor(out=tc_, in0=v4, in1=cb_, op=ALU.mult), reads=[pqkk, "consts"], writes=[tck])
                S.op("dve", lambda e: e.tensor_tensor(out=ts_, in0=v4, in1=sb_, op=ALU.mult), reads=[pqkk, "consts"], writes=[tsk])
                S.op("dve", lambda e: e.tensor_tensor(out=r[0:n, 0, 0:na, :], in0=tc_[:, :, 0, :], in1=ts_[:, :, 1, :], op=ALU.subtract), reads=[tck, tsk], writes=[rk])
                S.op("dve", lambda e: e.tensor_tensor(out=r[0:n, 1, 0:na, :], in0=tc_[:, :, 1, :], in1=ts_[:, :, 0, :], op=ALU.add), reads=[tck, tsk], writes=[rk])
                tb, tbk = c.tb, c.tbk
                if na == 2:
                    S.op("act", lambda e: e.activation(out=tb[0:n, 0:2, :], in_=r[0:n, :, 1, :], func=AF.Copy), reads=[rk], writes=[tbk])
                    if not c.is_s:
                        S.op("act", lambda e: e.activation(out=tb[0:n, 2:4, :], in_=r[0:n, :, 1, :], func=AF.Copy, scale=gq_s[0:n, h:h + 1]), reads=[rk, "consts"], writes=[tbk])
                    S.op("act", lambda e: e.mul(out=tb[0:n, 4:6, :], in_=r[0:n, :, 0, :], mul=0.0625), reads=[rk], writes=[tbk])
                if not c.is_s:
                    S.op("act", lambda e: e.activation(out=tb[0:n, 6:8, :], in_=r[0:n, :, 0, :], func=AF.Copy, scale=kd16[0:n, h:h + 1]), reads=[rk, "kd16"], writes=[tbk])

            def stageA_v(c):
                n = c.n
                pb, pk = c.pb["v"]
                S.op("act", lambda e: e.activation(out=c.vb[0:n, :], in_=pb[0:n, :], func=AF.Copy), reads=[pk], writes=[c.vbk])

            def stageA_g(c):
                if not (c.is_main or c.is_s):
                    return
                n = c.n
                pb, pk = c.pb["gr"]
                S.op("act", lambda e: e.activation(out=c.sg[0:n, :], in_=pb[0:n, :], func=AF.Silu), reads=[pk], writes=[c.sgk])
                g, gk = gnb[c.h % 2], "gnb%d" % (c.h % 2)
                S.op("dve", lambda e: e.tensor_tensor(out=c.sg[0:n, :], in0=c.sg[0:n, :], in1=g[0:n, :], op=ALU.mult), reads=[c.sgk, gk], writes=[c.sgk])

            def gn_part1(c, o_ps, ok, n):
                st, sk = stat4()
                st2, sk2 = stat4()
                c.gn = (o_ps, ok, n, st2, sk2)
                S.op("dve", lambda e: e.bn_stats(out=st[0:n, 0:6], in_=o_ps[0:n, :]), reads=[ok], writes=[sk])
                S.op("dve", lambda e: e.bn_aggr(out=st2[0:n, 0:2], in_=st[0:n, 0:6]), reads=[sk], writes=[sk2])
                S.op("dve", lambda e: e.tensor_scalar(out=st2[0:n, 2:3], in0=st2[0:n, 1:2], scalar1=1e-5, scalar2=None, op0=ALU.add), reads=[sk2], writes=[sk2])
                S.op("act", lambda e: e.activation(out=st2[0:n, 2:3], in_=st2[0:n, 2:3], func=AF.Sqrt), reads=[sk2], writes=[sk2])
                S.op("dve", lambda e: e.reciprocal(out=st2[0:n, 3:4], in_=st2[0:n, 2:3]), reads=[sk2], writes=[sk2])

            def gn_part2(c):
                if c is None or not hasattr(c, "gn"):
                    return
                o_ps, ok, n, st2, sk2 = c.gn
                par = c.par
                y, yk = yn[par], "yn%d" % par
                c.ab, c.abk = actb[par], "actb%d" % par
                S.op("dve", lambda e: e.tensor_scalar(out=y[0:n, :], in0=o_ps[0:n, :], scalar1=st2[0:n, 0:1], scalar2=st2[0:n, 3:4], op0=ALU.subtract, op1=ALU.mult), reads=[ok, sk2], writes=[yk])
                S.op("dve", lambda e: e.tensor_tensor(out=c.ab[0:n, :], in0=y[0:n, :], in1=c.sg[0:n, :], op=ALU.mult), reads=[yk, c.sgk], writes=[c.abk])

            def gn_chain(c, o_ps, ok, n):
                gn_part1(c, o_ps, ok, n)
                gn_part2(c)

            def stageB2(c):
                if c is None or not (c.is_main or c.is_s):
                    return
                n = c.n
                col0 = SC0 if c.is_s else (c.w - NPRE) * 128
                for j in range(4):
                    S.op("pe", lambda e: e.transpose(out=ptr2_v[:, j, 0:n], in_=c.ab[0:n, j * 128:(j + 1) * 128], identity=identb[0:n, 0:n]), reads=[c.abk, "identb"], writes=["bank6b"])
                S.op("act", lambda e: e.activation(out=astage[:, :, col0:col0 + n], in_=ptr2_v[:, 0:4, 0:n], func=AF.Copy), reads=["bank6b"], writes=["astage"])

            def make_ctx(h, w, slabs):
                c = Ctx()
                c.h, c.w = h, w
                c.is_s = (w == NPRE + NT)
                c.is_main = (w >= NPRE) and not c.is_s
                c.n = NS if c.is_s else 128
                c.par = it[0] % 2
                it[0] += 1
                c.pb = {}
                (c.wqk, c.wqkk), (c.wv, c.wvk), (c.wg, c.wgk) = slabs
                if c.is_s:
                    c.lh = lambda kc: hT[:, kc, SC0:SC0 + NS]
                    c.lk = ["hTs"]
                    c.tb, c.tbk, c.vb, c.vbk, c.sg, c.sgk = tokb_s, "tokb_s", vsb_s, "vsb_s", sgr_s, "sgr_s"
                else:
                    c.tb, c.tbk = tokb[c.par], "tokb%d" % c.par
                    c.vb, c.vbk = vsb[c.par], "vsb%d" % c.par
                    c.sg, c.sgk = sgr[c.par], "sgr%d" % c.par
                    if c.is_main:
                        t = w - NPRE
                        c.lh = lambda kc, t=t: hT[:, kc, t * 128:(t + 1) * 128]
                        c.lk = ["hT%d" % t]
                    else:
                        hr, hrk = hring[w % 2], "hring%d" % (w % 2)
                        S.dma("sp", lambda e: e.dma_start(out=hr[:], in_=hpre_d[w].rearrange("p (k n) -> p k n", k=KC)), reads=["hpre%d" % w], writes=[hrk])
                        c.lh = lambda kc, hr=hr: hr[:, kc, :]
                        c.lk = [hrk]
                return c

            NWT = NPRE + NT
            SPI = (NS + NWT - 1) // NWT
            for h in range(H):
                g1 = float(np.exp(np.float32(log_g[h])))
                g128 = float(np.exp(np.float32(128.0) * np.float32(log_g[h])))
                q0 = h * 256
                k0 = cfg.DQK + h * 256
                slabs = [load_slab([(w_in[:, k0:k0 + 256], 0), (w_in[:, q0:q0 + 256], 256)], KC, 512),
                         load_slab([(w_in[:, cfg.offs[2] + h * 512:cfg.offs[2] + (h + 1) * 512], 0)], KC, 512),
                         load_slab([(w_in[:, cfg.offs[3] + h * 512:cfg.offs[3] + (h + 1) * 512], 0)], KC, 512)]
                g_, gk_ = gnb[h % 2], "gnb%d" % (h % 2)
                S.dma("sp", lambda e: e.dma_start(out=g_[:], in_=gn_g[:, h * 512:(h + 1) * 512].partition_broadcast(128)), writes=[gk_])
                dcy, dcyk = decr[h % 2], "decr%d" % (h % 2)
                S.dma("sp", lambda e: e.dma_start(out=dcy[:], in_=decayT[:, h, :]), writes=[dcyk])
                sbase = sit[0]
                sit[0] += NS

                def sload(s_):
                    sr, srk = sring[(sbase + s_) % NSR], "sring%d" % ((sbase + s_) % NSR)
                    S.dma("sp", lambda e: e.dma_start(out=sr[:], in_=st_ret[s_, h].rearrange("(b p) v -> p b v", p=128)), writes=[srk])
                for s_ in range(min(NSR - 1, NS)):
                    sload(s_)
                S.op("dve", lambda e: e.memset(Sst[:], 0.0), writes=["Sst"])
                S.op("dve", lambda e: e.memset(Sbf[:], 0.0), writes=["Sbf"])
                cs_ = make_ctx(h, NPRE + NT, slabs)
                stageA_pe(cs_, "qk")
                stageA_rot(cs_)
                stageA_pe(cs_, "v")
                stageA_v(cs_)
                for j2 in range(2):
                    S.op("pe", lambda e: e.transpose(out=ptr_v[:, j2, 0:NS], in_=tokb_s[0:NS, j2, :], identity=identb[0:NS, 0:NS]), reads=["tokb_s", "identb"], writes=["ptr"])
                S.op("act", lambda e: e.activation(out=qTs[:], in_=ptr_v[:, 0:2, 0:NS], func=AF.Copy), reads=["ptr"], writes=["qTs"])

                def build_kmask(s0):
                    S.op("dve", lambda e: e.tensor_tensor(out=kmask_all[:], in0=tokb_s[0:NS, 4:6, :].rearrange("p a n -> p (a n)").unsqueeze(1).to_broadcast([NS, KMH, 256]),
                         in1=ident[0:NS, s0:s0 + KMH].unsqueeze(2).to_broadcast([NS, KMH, 256]), op=ALU.mult), reads=["tokb_s", "ident"], writes=["kmask_all"])
                build_kmask(0)
                S.op("dve", lambda e: e.tensor_tensor(out=qmask_all[:], in0=qTs[:].unsqueeze(1).to_broadcast([128, NS, 2, NS]),
                     in1=m16[:].unsqueeze(2).to_broadcast([128, NS, 2, NS]), op=ALU.mult), reads=["qTs", "consts"], writes=["qmask_all"])
                pos_, posk = banks[2], "bank2"
                sdone = [0]
                pendB = []

                def sample_A():
                    s_ = sdone[0]
                    if s_ >= NS:
                        return
                    sdone[0] += 1
                    if len(pendB) >= len(snb):
                        sample_B()
                    if s_ > 0 and s_ % KMH == 0:
                        build_kmask(s_)
                    si = sbase + s_
                    sr, srk = sring[si % NSR], "sring%d" % (si % NSR)
                    for blk in range(2):
                        S.op("pe", lambda e: e.matmul(out=bank45[:, blk, :], lhsT=kmask_all[:, s_ % KMH, blk * 128:(blk + 1) * 128], rhs=vsb_s[0:NS, :], start=True, stop=True), reads=["kmask_all", "vsb_s"], writes=["bank45"])
                    S.op("dve", lambda e: e.scalar_tensor_tensor(out=sr[:], in0=sr[:], scalar=g1, in1=bank45[:], op0=ALU.mult, op1=ALU.add), reads=["bank45", srk], writes=[srk])
                    outs.append(S.dma("act", lambda e: e.dma_start(out=ret_s[s_, h].rearrange("(b p) v -> p b v", p=128), in_=sr[:]), reads=[srk]))
                    sn, snk = snb[si % 2], "snb%d" % (si % 2)
                    S.op("act", lambda e: e.activation(out=sn[:], in_=sr[:], func=AF.Copy), reads=[srk], writes=[snk])
                    pendB.append((s_, sn, snk))
                    if s_ + NSR - 1 < NS:
                        sload(s_ + NSR - 1)

                def sample_B():
                    while pendB:
                        s_, sn, snk = pendB.pop(0)
                        for blk in range(2):
                            S.op("pe", lambda e: e.matmul(out=pos_[0:NS, :], lhsT=qmask_all[:, s_, blk, :], rhs=sn[:, blk, :], start=(s_ == 0 and blk == 0), stop=(s_ == NS - 1 and blk == 1)),
                                 reads=["qmask_all", snk], writes=[posk])

                ctxs = [None] * NWT
                ctxs[0] = make_ctx(h, 0, slabs)
                stageA_pe(ctxs[0], "qk")
                stageA_rot(ctxs[0])
                stageA_pe(ctxs[0], "v")
                stageA_v(ctxs[0])
                stageA_pe(ctxs[0], "gr")
                stageA_g(ctxs[0])
                prev = None
                for w in range(NWT):
                    c = ctxs[w]
                    nx = None
                    if w + 1 < NWT:
                        nx = ctxs[w + 1] = make_ctx(h, w + 1, slabs)
                    tb, tbk, vb, vbk = c.tb, c.tbk, c.vb, c.vbk
                    if c.is_main:
                        qt, qtk = qkT[c.par], "qkT%d" % c.par
                        for i6 in range(6):
                            S.op("pe", lambda e: e.transpose(out=ptr_v[:, i6, :], in_=tb[:, i6, :], identity=identb[:]), reads=[tbk, "identb"], writes=["ptr"])
                        S.op("act", lambda e: e.activation(out=qt[:], in_=ptr_v[:, 0:6, :], func=AF.Copy), reads=["ptr"], writes=[qtk])
                    if nx is not None:
                        stageA_pe(nx, "qk")
                        stageA_rot(nx)
                    if c.is_main:
                        for blk in range(2):
                            S.op("pe", lambda e: e.matmul(out=psc, lhsT=qt[:, 4 + blk, :], rhs=qt[:, blk, :], start=(blk == 0), stop=(blk == 1)), reads=[qtk], writes=[psck])
                        sc, sck = scs[c.par], "scs%d" % c.par
                        S.op("dve", lambda e: e.tensor_tensor(out=sc[:], in0=psc, in1=dcy[:], op=ALU.mult), reads=[psck, dcyk], writes=[sck])
                    stageB2(prev)
                    prev = c
                    for _ in range(SPI):
                        sample_A()
                    if nx is not None:
                        stageA_pe(nx, "v")
                        stageA_v(nx)
                    if c.is_main:
                        po, pok = banks[3], "bank3"
                        S.op("pe", lambda e: e.matmul(out=po[:], lhsT=sc[:], rhs=vb[:], start=True, stop=False), reads=[sck, vbk], writes=[pok])
                        for blk in range(2):
                            S.op("pe", lambda e: e.matmul(out=po[:], lhsT=qt[:, 2 + blk, :], rhs=Sbf[:, blk, :], start=False, stop=(blk == 1)), reads=[qtk, "Sbf"], writes=[pok])
                        gn_part1(c, po, pok, 128)
                    for blk in range(2):
                        S.op("pe", lambda e: e.matmul(out=bank45[:, blk, :], lhsT=tb[:, 6 + blk, :], rhs=vb[:], start=True, stop=True), reads=[tbk, vbk], writes=["bank45"])
                    S.op("dve", lambda e: e.scalar_tensor_tensor(out=Sst[:], in0=Sst[:], scalar=g128, in1=bank45[:], op0=ALU.mult, op1=ALU.add), reads=["bank45", "Sst"], writes=["Sst"])
                    S.op("act", lambda e: e.activation(out=Sbf[:], in_=Sst[:], func=AF.Copy), reads=["Sst"], writes=["Sbf"])
                    if w == NWT - 1:
                        outs.append(S.dma("sp", lambda e: e.dma_start(out=ret_p[h].rearrange("(b p) v -> p b v", p=128), in_=Sst[:]), reads=["Sst"]))
                    if nx is not None:
                        stageA_pe(nx, "gr")
                        stageA_g(nx)
                    sample_B()
                    if c.is_main:
                        gn_part2(c)
                while sdone[0] < NS:
                    sample_A()
                    sample_B()
                sample_B()
                stageB2(prev)
                stageA_pe(cs_, "gr")
                stageA_g(cs_)
                cs_.par = it[0] % 2
                it[0] += 1
                gn_chain(cs_, pos_, posk, NS)
                stageB2(cs_)
                S.dma("sp", lambda e: e.dma_start(out=actT_d[h * 4:(h + 1) * 4].rearrange("j p n -> p j n"), in_=astage[:]), reads=["astage"], writes=["actT_d%d" % h])
        slots.pop()
        slot_rr[0] = 0
        pop_scope()

        push_scope()
        GD = D // 4
        GC = GD // 128
        pT_d = dscr("pT_d", [KC, 128, NCOL], BF16)
        mix_d = dscr("mix_d", [KC, 128, NCOL], BF16)
        UB = 16 + NCOL
        ubuf = sb("ubuf", [128, UB], F32)
        pa = sb("pa", [128, UB], F32)
        pb2 = sb("pb2", [128, UB], F32)
        for bf_, k_ in ((ubuf, "ubuf"), (pa, "pa"), (pb2, "pb2")):
            S.op("dve", lambda e, bf_=bf_: e.memset(bf_[:, 0:16], 0.0), writes=[k_])
        gstage = sb("gstage", [128, 4, NCOL], BF16)
        ukeep = sb("ukeep", [128, KC, 32], F32)
        prevT = sb("prevT", [128, KC, NS], F32)
        prow = sb("prow", [NS, 15, GD], F32)
        ptok = sb("ptok", [NS, D], F32)
        utok = sb("utok", [32, D], F32)
        t16 = sb("t16", [128, 2, 16], F32)
        GRP = _ngroups(NCOL)
        for gi in range(4):
            w_ = 2 ** (gi + 1)
            nr = w_ - 1
            S.dma("sp", lambda e, gi=gi, nr=nr: e.dma_start(out=prow[:, 0:nr, :], in_=st_pool[:, 15 - nr:15, gi * GD:(gi + 1) * GD]), writes=["prow"])
            S.op("dve", lambda e, gi=gi: e.tensor_copy(out=ptok[:, gi * GD:(gi + 1) * GD], in_=prow[:, 0, :]), reads=["prow"], writes=["ptok"])
            for r_ in range(1, nr):
                S.op("dve", lambda e, gi=gi, r_=r_: e.tensor_tensor(out=ptok[:, gi * GD:(gi + 1) * GD], in0=ptok[:, gi * GD:(gi + 1) * GD], in1=prow[:, r_, :], op=ALU.add), reads=["prow", "ptok"], writes=["ptok"])
        pb, pk = bank()
        for kc in range(KC):
            S.op("pe", lambda e, kc=kc, pb=pb: e.transpose(out=pb[:, kc * NS:(kc + 1) * NS], in_=ptok[0:NS, kc * 128:(kc + 1) * 128], identity=ident[0:NS, 0:NS]), reads=["ptok", "ident"], writes=[pk])
        S.op("dve", lambda e, pb=pb: e.tensor_copy(out=prevT[:], in_=pb[:, 0:KC * NS].rearrange("p (k n) -> p k n", k=KC)), reads=[pk], writes=["prevT"])
        outs.append(S.dma("sp", lambda e: e.dma_start(out=pool_s[:, 0:14, :], in_=st_pool[:, 1:15, :])))

        def feat_mm(wv, wk, j, nk, rhs_fn, rkeys, evac):
            for (c0, wd) in GRP:
                pb, pk = bank()
                for kc in range(nk):
                    S.op("pe", lambda e, kc=kc, pb=pb, c0=c0, wd=wd: e.matmul(out=pb[:, 0:wd], lhsT=wv[:, kc, j * 128:(j + 1) * 128], rhs=rhs_fn(kc, c0, wd),
                         start=(kc == 0), stop=(kc == nk - 1)), reads=[wk] + rkeys, writes=[pk])
                evac(pb, pk, c0, wd)

        hrhs = lambda kc, c0, wd: hT[:, kc, c0:c0 + wd]
        modT2 = sb("modT2", [128, 4 * KC, NM], F32)
        A2 = sb("A2", [128, KC, NM], F32)
        gtmp = [sb("gtmp%d" % i, [NM, 512], F32) for i in range(2)]
        btmp = [sb("btmp%d" % i, [NM, 512], F32) for i in range(2)]
        gcnt = [0]

        def gt_fn():
            i = gcnt[0] % 2
            gcnt[0] += 1
            return gtmp[i], "gtmp%d" % i, btmp[i], "btmp%d" % i

        deferred = list(range(2 * SPM, 6 * SPM))
        g1_done = [0]
        N_G1 = 3 * (KC // 4)

        def run_deferred():
            g1_done[0] += 1
            target = (len(deferred_all) * g1_done[0] + N_G1 - 1) // N_G1
            while len(deferred_all) - len(deferred) < target and deferred:
                mod_slab(deferred.pop(0), modT2, 2 * KC, gt_fn)
            if not deferred and not fin[0]:
                fin[0] = True
                mod_finish(modT2, A2, "A2", n2T, 2 * KC, KC, 2)

        deferred_all = list(deferred)
        fin = [False]
        for i in range(KC // 4):
            wv, wk = load_slab([(w_in[:, cfg.offs[4] + i * 512:cfg.offs[4] + (i + 1) * 512], 0)], KC, 512)
            for j in range(4):
                c = i * 4 + j
                gi = c // GC
                w_ = 2 ** (gi + 1)
                feat_mm(wv, wk, j, KC, hrhs, HT_KEYS, lambda pb, pk, c0, wd: S.op("act", lambda e: e.activation(out=ubuf[:, 16 + c0:16 + c0 + wd], in_=pb[:, 0:wd], func=AF.Copy), reads=[pk], writes=["ubuf"]))
                S.op("act", lambda e, c=c: e.activation(out=ukeep[:, c, :], in_=ubuf[:, SC0:SC0 + 32], func=AF.Copy), reads=["ubuf"], writes=["ukeep"])
                src, srck = ubuf, "ubuf"
                for k_ in range(gi + 1):
                    sh = 2 ** k_
                    dst, dstk = (pa, "pa") if k_ % 2 == 0 else (pb2, "pb2")
                    S.op("dve", lambda e, src=src, dst=dst, sh=sh: e.tensor_tensor(out=dst[:, 16:16 + SC0], in0=src[:, 16:16 + SC0], in1=src[:, 16 - sh:16 + SC0 - sh], op=ALU.add), reads=[srck], writes=[dstk])
                    src, srck = dst, dstk
                S.op("dve", lambda e, src=src, j=j, w_=w_: e.scalar_tensor_tensor(out=gstage[:, j, 0:SC0], in0=src[:, 16:16 + SC0], scalar=1.0 / w_, in1=ubuf[:, 16:16 + SC0], op0=ALU.mult, op1=ALU.subtract), reads=[srck, "ubuf"], writes=["gstage"])
                S.op("dve", lambda e, src=src, gi=gi: e.tensor_tensor(out=t16[:, 0, :], in0=src[:, 144:160], in1=invc_s3[:, gi, :], op=ALU.mult), reads=[srck, "consts"], writes=["t16"])
                S.op("dve", lambda e, j=j: e.tensor_tensor(out=gstage[:, j, 128:144], in0=t16[:, 0, :], in1=ubuf[:, 144:160], op=ALU.subtract), reads=["t16", "ubuf"], writes=["gstage"])
                S.op("dve", lambda e, c=c: e.tensor_tensor(out=t16[:, 1, 0:NS], in0=ubuf[:, 16 + SC0:16 + SC0 + NS], in1=prevT[:, c, :], op=ALU.add), reads=["ubuf", "prevT"], writes=["t16b"])
                S.op("dve", lambda e, j=j, w_=w_: e.scalar_tensor_tensor(out=gstage[:, j, SC0:SC0 + NS], in0=t16[:, 1, 0:NS], scalar=1.0 / w_, in1=ubuf[:, 16 + SC0:16 + SC0 + NS], op0=ALU.mult, op1=ALU.subtract), reads=["t16b", "ubuf"], writes=["gstage"])
            S.dma("sp", lambda e, i=i: e.dma_start(out=pT_d[i * 4:(i + 1) * 4].rearrange("j p n -> p j n"), in_=gstage[:]), reads=["gstage"], writes=["pT_d%d" % i])
            run_deferred()
        for (oi, dst_d, dn) in ((5, sga_d, "sga_d"), (6, sgb_d, "sgb_d")):
            for i in range(KC // 4):
                wv, wk = load_slab([(w_in[:, cfg.offs[oi] + i * 512:cfg.offs[oi] + (i + 1) * 512], 0)], KC, 512)
                for j in range(4):
                    feat_mm(wv, wk, j, KC, hrhs, HT_KEYS, lambda pb, pk, c0, wd, j=j: S.op("act", lambda e: e.activation(out=gstage[:, j, c0:c0 + wd], in_=pb[:, 0:wd], func=AF.Sigmoid), reads=[pk], writes=["gstage"]))
                S.dma("sp", lambda e, i=i, dst_d=dst_d: e.dma_start(out=dst_d[i * 4:(i + 1) * 4].rearrange("j p n -> p j n"), in_=gstage[:]), reads=["gstage"], writes=["%s%d" % (dn, i)])
                run_deferred()
        for k0 in range(0, KC, 4):
            pb, pk = bank()
            for j in range(4):
                S.op("pe", lambda e, j=j, pb=pb, k0=k0: e.transpose(out=pb[0:32, j * 128:(j + 1) * 128], in_=ukeep[:, k0 + j, :], identity=ident[:]), reads=["ukeep", "ident"], writes=[pk])
            S.op("act", lambda e, pb=pb, k0=k0: e.activation(out=utok[:, k0 * 128:(k0 + 4) * 128], in_=pb[0:32, :], func=AF.Copy), reads=[pk], writes=["utok"])
        outs.append(S.dma("sp", lambda e: e.dma_start(out=pool_p, in_=utok[1:16, :]), reads=["utok"]))
        outs.append(S.dma("sp", lambda e: e.dma_start(out=pool_s[:, 14, :], in_=utok[16:32, :]), reads=["utok"]))
        pop_scope()
        pop_scope()

        push_scope()
        p2T = sb("p2T", [128, KC, NCOL], BF16)
        mixT = sb("mixT", [128, KC, NCOL], BF16)
        aTh = sb("aTh", [128, 2 * H, NCOL], BF16)
        pst = sb("pst", [128, 4, NCOL], BF16)
        tmpf = [sb("tmpf%d" % i, [128, 512], F32) for i in range(2)]
        tf_rr = [0]

        def tmp512():
            i = tf_rr[0] % 2
            tf_rr[0] += 1
            return tmpf[i], "tmpf%d" % i

        for g in range(4):
            S.dma("sp", lambda e, g=g: e.dma_start(out=pst[:, 0:GC, :], in_=pT_d[g * GC:(g + 1) * GC].rearrange("j p n -> p j n")), reads=["pT_d%d" % i for i in range(KC // 4)], writes=["pst"])
            wv, wk = load_slab([(w_pg[g], 0)], GC, GD)
            for j in range(GC):
                c = g * GC + j
                feat_mm(wv, wk, j, GC, lambda kc, c0, wd: pst[:, kc, c0:c0 + wd], ["pst"],
                        lambda pb, pk, c0, wd, c=c: S.op("act", lambda e: e.activation(out=p2T[:, c, c0:c0 + wd], in_=pb[:, 0:wd], func=AF.Copy, scale=psc_s[:, c:c + 1]), reads=[pk, "consts"], writes=["p2T"]))
        for i in range(KC // 4):
            wv, wk = load_slab([(w_pb[:, i * 512:(i + 1) * 512], 0)], KC, 512)
            S.dma("sp", lambda e, i=i: e.dma_start(out=pst[:], in_=sgb_d[i * 4:(i + 1) * 4].rearrange("j p n -> p j n")), reads=["sgb_d%d" % i], writes=["pst"])
            for j in range(4):
                ob = i * 4 + j
                feat_mm(wv, wk, j, KC, lambda kc, c0, wd: p2T[:, kc, c0:c0 + wd], ["p2T"],
                        lambda pb, pk, c0, wd, j=j, ob=ob: S.op("dve", lambda e: e.tensor_tensor(out=mixT[:, ob, c0:c0 + wd], in0=pb[:, 0:wd], in1=pst[:, j, c0:c0 + wd], op=ALU.mult), reads=[pk, "pst"], writes=["mixT"]))
        for half in range(2):
            S.dma("sp", lambda e, half=half: e.dma_start(out=aTh[:], in_=actT_d[half * 2 * H:(half + 1) * 2 * H].rearrange("j p n -> p j n")), reads=["actT_d%d" % h for h in range(H)], writes=["aTh"])
            for i in range(KC // 4):
                wv, wk = load_slab([(w_pa[half * 2 * H * 128:(half + 1) * 2 * H * 128, i * 512:(i + 1) * 512], 0)], 2 * H, 512)
                S.dma("sp", lambda e, i=i: e.dma_start(out=pst[:], in_=sga_d[i * 4:(i + 1) * 4].rearrange("j p n -> p j n")), reads=["sga_d%d" % i], writes=["pst"])
                for j in range(4):
                    ob = i * 4 + j

                    def ev(pb, pk, c0, wd, j=j, ob=ob):
                        tf, tfk = tmp512()
                        S.op("dve", lambda e: e.tensor_tensor(out=tf[:, 0:wd], in0=pb[:, 0:wd], in1=pst[:, j, c0:c0 + wd], op=ALU.mult), reads=[pk, "pst"], writes=[tfk])
                        S.op("dve", lambda e: e.tensor_tensor(out=mixT[:, ob, c0:c0 + wd], in0=mixT[:, ob, c0:c0 + wd], in1=tf[:, 0:wd], op=ALU.add), reads=[tfk, "mixT"], writes=["mixT"])
                    feat_mm(wv, wk, j, 2 * H, lambda kc, c0, wd: aTh[:, kc, c0:c0 + wd], ["aTh"], ev)
        S.dma("sp", lambda e: e.dma_start(out=mix_d.rearrange("k p n -> p k n"), in_=mixT[:]), reads=["mixT"], writes=["mix_d"])
        pop_scope()

        push_scope()
        NHT = cfg.NHT
        x1 = sb("x1", [128, NHT, D], F32)
        samp = sb("samp", [NS, 1, D], F32)
        gpr = [sb("gpr%d" % i, [128, 512], F32) for i in range(2)]
        gsr = [sb("gsr%d" % i, [NS, 512], F32) for i in range(2)]
        g_rr = [0]

        def load_gates(goff, cb):
            i = g_rr[0] % 2
            g_rr[0] += 1
            S.dma("sp", lambda e: e.dma_start(out=gpr[i][:], in_=gate_d[0:1, goff + cb * 512:goff + (cb + 1) * 512].partition_broadcast(128)), reads=["gate_d"], writes=["gpr%d" % i])
            S.dma("sp", lambda e: e.dma_start(out=gsr[i][:], in_=gate_d[1:NM, goff + cb * 512:goff + (cb + 1) * 512]), reads=["gate_d"], writes=["gsr%d" % i])
            return gpr[i], gsr[i], ["gpr%d" % i, "gsr%d" % i]
        h2T = sb("h2T", [128, KC, NCOL], BF16)
        tmpf = [sb("tmpg%d" % i, [128, 512], F32) for i in range(2)]
        push_scope()
        xb = sb("xb", [128, D], F32)
        push_scope()
        mring = [sb("mring%d" % i, [128, KC, 128], BF16) for i in range(2)]

        def tile_info(tt):
            if tt == 0:
                return xb, "xb", 128, 0, lambda a, b: xb[:, a:b], None
            if tt == NT:
                return samp, "samp0", NS, SC0, lambda a, b: samp[:, 0, a:b], None
            return x1, "x1_%d" % tt, 128, tt * 128, lambda a, b, tt=tt: x1[:, tt - 1, a:b], None

        for tt in range(NT + 1):
            _, xk, n, c0_, xv, _g = tile_info(tt)
            src = xs if tt == NT else xw[(NPRE + tt) * 128:(NPRE + tt + 1) * 128, :]
            S.dma("sp", lambda e, xv=xv, src=src: e.dma_start(out=xv(0, D), in_=src), writes=[xk])
        mr_rr = [0]
        for cb in range(D // 512):
            wv, wk = load_slab([(w_out[:, cb * 512:(cb + 1) * 512], 0)], KC, 512)
            gp_, gs_, gkeys = load_gates(0, cb)
            for tt in range(NT + 1):
                _, xk, n, c0_, xv, _g = tile_info(tt)
                mi_ = mr_rr[0] % 2
                mr_rr[0] += 1
                mr, mrk = mring[mi_], "mring%d" % mi_
                S.dma("sp", lambda e, mr=mr, c0_=c0_, n=n: e.dma_start(out=mr[:, :, 0:n], in_=mix_d[:, :, c0_:c0_ + n].rearrange("k p n -> p k n")), reads=["mix_d"], writes=[mrk])
                pb, pk = bank()
                for kc in range(KC):
                    S.op("pe", lambda e, kc=kc, pb=pb, mr=mr, n=n, wv=wv: e.matmul(out=pb[0:n, :], lhsT=mr[:, kc, 0:n], rhs=wv[:, kc, :], start=(kc == 0), stop=(kc == KC - 1)), reads=[wk, mrk], writes=[pk])
                tf, tfk = tmp512()
                gsrc = gs_[:] if tt == NT else gp_[:]
                S.op("dve", lambda e, pb=pb, n=n, tf=tf, gsrc=gsrc: e.tensor_tensor(out=tf[0:n, :], in0=pb[0:n, :], in1=gsrc, op=ALU.mult), reads=[pk] + gkeys, writes=[tfk])
                S.op("dve", lambda e, xv=xv, tf=tf, n=n, cb=cb: e.tensor_tensor(out=xv(cb * 512, (cb + 1) * 512), in0=xv(cb * 512, (cb + 1) * 512), in1=tf[0:n, :], op=ALU.add), reads=[tfk, xk], writes=[xk])

        def norm_T2(xv, xk, n, junk, junkk):
            st, sk = stat4()
            S.op("act", lambda e: e.activation(out=junk, in_=xv(0, D).rearrange("p (a b) -> p a b", a=4), func=AF.Square, accum_out=st[0:n, 0:1]), reads=[xk], writes=[junkk, sk])
            S.op("dve", lambda e: e.tensor_scalar(out=st[0:n, 1:2], in0=st[0:n, 0:1], scalar1=1.0 / D, scalar2=1e-6, op0=ALU.mult, op1=ALU.add), reads=[sk], writes=[sk])
            S.op("act", lambda e: e.activation(out=st[0:n, 1:2], in_=st[0:n, 1:2], func=AF.Sqrt), reads=[sk], writes=[sk])
            S.op("dve", lambda e: e.reciprocal(out=st[0:n, 2:3], in_=st[0:n, 1:2]), reads=[sk], writes=[sk])
            return st, sk

        pop_scope()
        push_scope()
        xnb = [sb("xnb0", [128, D], F32)]
        pmask = sb("pmask2", [128, 128], F32)
        S.dma("sp", lambda e: e.dma_start(out=pmask[:], in_=premask.partition_broadcast(128)), writes=["pmask2"])
        mB = sb("mB", [128, 2, KC, NM], F32)
        S.dma("sp", lambda e: e.dma_start(out=mB[:], in_=modk_d[:, 2:4, :].rearrange("p a (k n) -> p a k n", k=KC)), reads=["modk_d"], writes=["mB"])
        xn_rr = [0]
        for tt in range(NT + 1):
            _, xk, n, c0_, xv, _g = tile_info(tt)
            xt, xtk = xnb[0], "xnb0"
            st, sk = norm_T2(xv, xk, n, xt[0:n, :].rearrange("p (a b) -> p a b", a=4), xtk)
            S.op("act", lambda e, xt=xt, xv=xv, n=n, st=st: e.activation(out=xt[0:n, :], in_=xv(0, D), func=AF.Copy, scale=st[0:n, 2:3]), reads=[xk, sk], writes=[xtk])
            for k0 in range(0, KC, 4):
                pb, pk = bank()
                for j in range(4):
                    kc = k0 + j
                    S.op("pe", lambda e, kc=kc, j=j, pb=pb, xt=xt, n=n: e.transpose(out=pb[:, j * 128:j * 128 + n], in_=xt[0:n, kc * 128:(kc + 1) * 128], identity=ident[0:n, 0:n]), reads=[xtk, "ident"], writes=[pk])
                for j in range(4):
                    kc = k0 + j
                    if n == 128:
                        S.op("act", lambda e, kc=kc, j=j, pb=pb, c0_=c0_: e.activation(out=h2T[:, kc, c0_:c0_ + 128], in_=pb[:, j * 128:(j + 1) * 128], func=AF.Identity,
                             scale=mB[:, 0, kc, 0:1], bias=mB[:, 1, kc, 0:1]), reads=[pk, "mB"], writes=["h2T"])
                    else:
                        S.op("dve", lambda e, kc=kc, j=j, pb=pb: e.tensor_tensor(out=stmp[:, j, :], in0=pb[:, j * 128:j * 128 + n], in1=mB[:, 0, kc, 1:NM], op=ALU.mult), reads=[pk, "mB"], writes=["stmp%d" % j])
                        S.op("dve", lambda e, kc=kc, j=j: e.tensor_tensor(out=h2T[:, kc, SC0:SC0 + NS], in0=stmp[:, j, :], in1=mB[:, 1, kc, 1:NM], op=ALU.add), reads=["stmp%d" % j, "mB"], writes=["h2T"])
        S.op("dve", lambda e: e.tensor_tensor(out=h2T[:, :, 0:128], in0=h2T[:, :, 0:128], in1=pmask[:].unsqueeze(1).to_broadcast([128, KC, 128]), op=ALU.mult), reads=["h2T", "pmask2"], writes=["h2T"])
        pop_scope()
        pop_scope()

        push_scope()
        gT = sb("gT", [128, 4, NCOL], BF16)
        upbuf = sb("upbuf", [128, NCOL], F32)
        cab = [sb("cab%d" % i, [128, NCOL], F32) for i in range(2)]
        cst = sb("cst", [NS, 512], F32)
        cstT = sb("cstT", [128, 16, NS], F32)
        upk = sb("upk", [128, 8, 18], F32)
        uptok = sb("uptok", [32, 512], F32)
        outs.append(S.dma("sp", lambda e: e.dma_start(out=conv_s[:, 0, :], in_=st_conv[:, 1, :])))
        for g in range(NG):
            wa, wak = load_slab([(w_up[:, g * 512:(g + 1) * 512], 0)], KC, 512)
            wb, wbk = load_slab([(w_up[:, DFF + g * 512:DFF + (g + 1) * 512], 0)], KC, 512)
            wd_, wdk = load_slab([(w_dn[g * 512:(g + 1) * 512, :], 0)], 4, D)
            pb, pk = bank()
            for r_ in range(2):
                for ab in range(2):
                    S.dma("sp", lambda e, ab=ab, g=g, r_=r_: e.dma_start(out=cst[:], in_=st_conv[:, r_, ab * DFF + g * 512:ab * DFF + (g + 1) * 512]), writes=["cst"])
                    for j in range(4):
                        idx = r_ * 8 + ab * 4 + j
                        S.op("pe", lambda e, j=j, idx=idx, pb=pb: e.transpose(out=pb[:, idx * NS:(idx + 1) * NS], in_=cst[0:NS, j * 128:(j + 1) * 128], identity=ident[0:NS, 0:NS]), reads=["cst", "ident"], writes=[pk])
            S.op("dve", lambda e, pb=pb: e.tensor_copy(out=cstT[:], in_=pb[:, 0:16 * NS].rearrange("p (k n) -> p k n", k=16)), reads=[pk], writes=["cstT"])
            for j in range(4):
                for ab in range(2):
                    W, Wk = (wa, wak) if ab == 0 else (wb, wbk)
                    fidx = ab * FC + g * 4 + j
                    dst, dstk = cab[ab], "cab%d" % ab
                    feat_mm(W, Wk, j, KC, lambda kc, c0, wd: h2T[:, kc, c0:c0 + wd], ["h2T"],
                            lambda pb, pk, c0, wd: S.op("act", lambda e: e.activation(out=upbuf[:, c0:c0 + wd], in_=pb[:, 0:wd], func=AF.Copy), reads=[pk], writes=["upbuf"]))
                    S.op("act", lambda e, dst=dst, fidx=fidx: e.activation(out=dst[:], in_=upbuf[:], func=AF.Identity, scale=cw_s3[:, 2, fidx:fidx + 1], bias=cb_s[:, fidx:fidx + 1]), reads=["upbuf", "consts"], writes=[dstk])
                    S.op("dve", lambda e, dst=dst, fidx=fidx: e.scalar_tensor_tensor(out=dst[:, 1:SC0], in0=upbuf[:, 0:SC0 - 1], scalar=cw_s3[:, 1, fidx:fidx + 1], in1=dst[:, 1:SC0], op0=ALU.mult, op1=ALU.add), reads=["upbuf", dstk, "consts"], writes=[dstk])
                    S.op("dve", lambda e, dst=dst, fidx=fidx: e.scalar_tensor_tensor(out=dst[:, 2:SC0], in0=upbuf[:, 0:SC0 - 2], scalar=cw_s3[:, 0, fidx:fidx + 1], in1=dst[:, 2:SC0], op0=ALU.mult, op1=ALU.add), reads=["upbuf", dstk, "consts"], writes=[dstk])
                    for r_ in range(2):
                        S.op("dve", lambda e, dst=dst, fidx=fidx, r_=r_, ab=ab, j=j: e.scalar_tensor_tensor(out=dst[:, SC0:SC0 + NS], in0=cstT[:, r_ * 8 + ab * 4 + j, :], scalar=cw_s3[:, r_, fidx:fidx + 1], in1=dst[:, SC0:SC0 + NS], op0=ALU.mult, op1=ALU.add), reads=["cstT", dstk, "consts"], writes=[dstk])
                    S.op("act", lambda e, ab=ab, j=j: e.activation(out=upk[:, ab * 4 + j, :], in_=upbuf[:, SC0 - 2:SC0 + NS], func=AF.Copy), reads=["upbuf"], writes=["upk"])
                S.op("act", lambda e: e.activation(out=cab[0][:], in_=cab[0][:], func=AF.Silu), reads=["cab0"], writes=["cab0"])
                S.op("dve", lambda e, j=j: e.tensor_tensor(out=gT[:, j, :], in0=cab[0][:], in1=cab[1][:], op=ALU.mult), reads=["cab0", "cab1"], writes=["gT"])
            for ab in range(2):
                pb, pk = bank()
                for j in range(4):
                    S.op("pe", lambda e, ab=ab, j=j, pb=pb: e.transpose(out=pb[0:18, j * 128:(j + 1) * 128], in_=upk[:, ab * 4 + j, :], identity=ident[:]), reads=["upk", "ident"], writes=[pk])
                S.op("act", lambda e, pb=pb: e.activation(out=uptok[0:18, :], in_=pb[0:18, :], func=AF.Copy), reads=[pk], writes=["uptok"])
                outs.append(S.dma("sp", lambda e, ab=ab, g=g: e.dma_start(out=conv_p[:, ab * DFF + g * 512:ab * DFF + (g + 1) * 512], in_=uptok[0:2, :]), reads=["uptok"]))
                outs.append(S.dma("sp", lambda e, ab=ab, g=g: e.dma_start(out=conv_s[:, 1, ab * DFF + g * 512:ab * DFF + (g + 1) * 512], in_=uptok[2:18, :]), reads=["uptok"]))
            for cb in range(D // 512):
                gp_, gs_, gkeys = load_gates(D, cb)
                _, xk, n, c0_, xv, _g = tile_info(NT)
                pb, pk = bank()
                for j in range(4):
                    S.op("pe", lambda e: e.matmul(out=pb[0:n, :], lhsT=gT[:, j, c0_:c0_ + n], rhs=wd_[:, j, cb * 512:(cb + 1) * 512], start=(j == 0), stop=(j == 3)), reads=["gT", wdk], writes=[pk])
                tf, tfk = tmp512()
                S.op("dve", lambda e: e.tensor_tensor(out=tf[0:n, :], in0=pb[0:n, :], in1=gs_[:], op=ALU.mult), reads=[pk] + gkeys, writes=[tfk])
                S.op("dve", lambda e: e.tensor_tensor(out=xv(cb * 512, (cb + 1) * 512), in0=xv(cb * 512, (cb + 1) * 512), in1=tf[0:n, :], op=ALU.add), reads=[tfk, xk], writes=[xk])
                S.op("dve", lambda e: e.tensor_tensor(out=wd_[:, :, cb * 512:(cb + 1) * 512], in0=wd_[:, :, cb * 512:(cb + 1) * 512],
                     in1=gp_[:].unsqueeze(1).to_broadcast([128, 4, 512]), op=ALU.mult), reads=[wdk] + gkeys, writes=[wdk])
            for cb in range(D // 512):
                for tt in range(1, NT):
                    _, xk, n, c0_, xv, _g = tile_info(tt)
                    pb, pk = bank()
                    for j in range(4):
                        S.op("pe", lambda e: e.matmul(out=pb[0:n, :], lhsT=gT[:, j, c0_:c0_ + n], rhs=wd_[:, j, cb * 512:(cb + 1) * 512], start=(j == 0), stop=(j == 3)), reads=["gT", wdk], writes=[pk])
                    S.op("dve", lambda e: e.tensor_tensor(out=xv(cb * 512, (cb + 1) * 512), in0=xv(cb * 512, (cb + 1) * 512), in1=pb[0:n, :], op=ALU.add), reads=[pk, xk], writes=[xk])
        pop_scope()

        push_scope()
        gbc = sb("gbc", [128, D], F32)
        S.dma("sp", lambda e: e.dma_start(out=gbc[:], in_=normf.partition_broadcast(128)), writes=["gbc"])
        fin = []
        NJ = max(KC // 4, 1)
        for tt in range(1, NT + 1):
            _, xk, n, c0_, xv, _g = tile_info(tt)
            st, sk = stat4()
            jr = (tt % NJ) * 4
            fin.append((tt, xk, n, xv, st, sk, h2T[0:n, jr:jr + 4, 0:D // 4], "junk%d" % (tt % NJ)))
        for (tt, xk, n, xv, st, sk, junk, jk) in fin:
            S.op("act", lambda e: e.activation(out=junk, in_=xv(0, D).rearrange("p (a b) -> p a b", a=4), func=AF.Square, accum_out=st[0:n, 0:1]), reads=[xk], writes=[jk, sk])
        for (tt, xk, n, xv, st, sk, junk, jk) in fin:
            S.op("dve", lambda e: e.tensor_scalar(out=st[0:n, 1:2], in0=st[0:n, 0:1], scalar1=1.0 / D, scalar2=1e-6, op0=ALU.mult, op1=ALU.add), reads=[sk], writes=[sk])
        for (tt, xk, n, xv, st, sk, junk, jk) in fin:
            S.op("act", lambda e: e.activation(out=st[0:n, 1:2], in_=st[0:n, 1:2], func=AF.Sqrt), reads=[sk], writes=[sk])
        for (tt, xk, n, xv, st, sk, junk, jk) in fin:
            S.op("dve", lambda e: e.reciprocal(out=st[0:n, 2:3], in_=st[0:n, 1:2]), reads=[sk], writes=[sk])
        for (tt, xk, n, xv, st, sk, junk, jk) in fin:
            S.op("act", lambda e: e.activation(out=xv(0, D), in_=xv(0, D), func=AF.Copy, scale=st[0:n, 2:3]), reads=[xk, sk], writes=[xk])
        for (tt, xk, n, xv, st, sk, junk, jk) in fin:
            S.op("dve", lambda e: e.tensor_tensor(out=xv(0, D), in0=xv(0, D), in1=gbc[0:n, :], op=ALU.mult), reads=[xk, "gbc"], writes=[xk])
            dst = y_s if tt == NT else y_main[(tt - 1) * 128:tt * 128, :]
            outs.append(S.dma("sp", lambda e: e.dma_start(out=dst, in_=xv(0, D)), reads=[xk]))
        S.emit(nc, outs)
        while len(scopes) > 1:
            scopes.pop().close()
    return nc


def _tables(cfg, half):
    H, NT, NPRE, NS = cfg.H, cfg.NT, cfg.NPRE, cfg.NS
    NCS = NT + NPRE + 1
    halfd = 128
    inv = (np.float32(10000.0) ** (-np.arange(halfd, dtype=np.float32) / np.float32(halfd))).astype(np.float32)
    pos = np.zeros((NCS, 128), np.float32)
    base = (half * cfg.NHT - cfg.NHT) * 128
    for w in range(NCS - 1):
        pos[w] = np.maximum(base + w * 128 + np.arange(128), 0)
    pos[NCS - 1] = cfg.PAST
    ang = pos[:, :, None].astype(np.float32) * inv[None, None, :]
    cos_t = np.ascontiguousarray(np.cos(ang).astype(np.float32).transpose(1, 0, 2))
    sin_t = np.ascontiguousarray(np.sin(ang).astype(np.float32).transpose(1, 0, 2))
    log_g = np.log(np.float32(1.0) - np.float32(2.0) ** (np.float32(-5.0) - np.arange(H, dtype=np.float32))).astype(np.float32)
    idx = np.arange(128, dtype=np.float32)
    diff = idx[None, :] - idx[:, None]
    dec = np.where(diff[:, None, :] >= 0, np.exp(np.maximum(diff, 0.0)[:, None, :] * log_g[None, :, None]), 0.0).astype(np.float32)
    gq = np.exp((idx[:, None] + 1.0) * log_g[None, :]).astype(np.float32)
    kd = np.exp((127.0 - idx[:, None]) * log_g[None, :]).astype(np.float32)
    invc = np.zeros((128, 4, 16), np.float32)
    p0 = half * cfg.NHT * 128
    for gi in range(4):
        w_ = 2 ** (gi + 1)
        invc[:, gi, :] = 1.0 / np.minimum(p0 + np.arange(16) + 1, w_).astype(np.float32)
    m16 = np.zeros((128, NS, NS), np.float32)
    for s in range(NS):
        m16[:, s, s] = 1.0
    return dict(cos_t=cos_t, sin_t=sin_t, decay_t=np.ascontiguousarray(dec), gq=gq, kd=kd, invc=invc,
                ident=np.eye(128, dtype=np.float32), m16=m16)


def _colT(v):
    return np.ascontiguousarray(v.reshape(-1, 128).T)


_PROG = {}


def run_cfg(cfg, inp):
    key = (cfg.D, cfg.H, cfg.DFF, cfg.NHT, cfg.NB)
    if key not in _PROG:
        _PROG[key] = build_program(cfg)
    nc = _PROG[key]
    D, H, NS, NHT, DFF = cfg.D, cfg.H, cfg.NS, cfg.NHT, cfg.DFF
    ncores = 2 * cfg.NB
    f = lambda a: np.ascontiguousarray(np.asarray(a, dtype=np.float32))
    shared = dict(
        norm1T=_colT(f(inp["norm1_g"][0])), norm2T=_colT(f(inp["norm2_g"][0])), bmodT=_colT(f(inp["b_mod"][0])),
        bmod_row=f(inp["b_mod"][0])[None, :], gn_g=f(inp["gn_g"][0])[None, :], pscaleT=_colT(f(inp["pool_scale"][0])),
        convwT=np.ascontiguousarray(f(inp["conv_w"][0]).reshape(3, -1, 128).transpose(2, 0, 1)), convbT=_colT(f(inp["conv_b"][0])),
        normf=f(inp["norm_f_g"])[None, :], w_mod=f(inp["w_mod"][0]), w_in=f(inp["w_in"][0]), w_proj_a=f(inp["w_proj_a"][0]),
        w_pool_grp=f(inp["w_pool_grp"][0]), w_proj_b=f(inp["w_proj_b"][0]), w_out=f(inp["w_out"][0]), w_up=f(inp["w_up"][0]),
        w_down=f(inp["w_down"][0]))
    tabs = [_tables(cfg, 0), _tables(cfg, 1)]
    xp, xsmp = f(inp["x_prompt"]), f(inp["x_sample"])
    in_maps = []
    for c in range(ncores):
        b, half = c // 2, c % 2
        xwin = np.zeros((2 * NHT * 128, D), np.float32)
        if half == 0:
            xwin[NHT * 128:] = xp[b, 0:NHT * 128]
        else:
            xwin[:] = xp[b]
        ss = slice(c * NS, (c + 1) * NS)
        m = dict(shared)
        m.update(tabs[half])
        m.update(xw=xwin, xs=f(xsmp[ss, 0]), c17=np.concatenate([f(inp["c_prompt"])[b:b + 1], f(inp["c_sample"])[ss]], 0),
                 premask=np.full((1, 128), float(half), np.float32),
                 st_ret=f(inp["state_ret"][0][ss]), st_pool=f(inp["state_pool"][0][ss]), st_conv=f(inp["state_conv"][0][ss]))
        in_maps.append(m)
    res = run_bass_kernel_spmd(nc, in_maps, core_ids=list(range(ncores)))
    R = res.results
    NB = cfg.NB
    y_p = np.stack([np.concatenate([R[2 * b]["y_main"], R[2 * b + 1]["y_main"]], 0) for b in range(NB)], 0)
    y_s = np.concatenate([R[c]["y_s"] for c in range(ncores)], 0)[:, None, :]
    ret_p = np.stack([R[2 * b + 1]["ret_p"] for b in range(NB)], 0)[None]
    ret_s = np.concatenate([R[c]["ret_s"] for c in range(ncores)], 0)[None]
    pool_p = np.stack([R[2 * b + 1]["pool_p"] for b in range(NB)], 0)[None]
    pool_s = np.concatenate([R[c]["pool_s"] for c in range(ncores)], 0)[None]
    conv_p = np.stack([R[2 * b + 1]["conv_p"] for b in range(NB)], 0)[None]
    conv_s = np.concatenate([R[c]["conv_s"] for c in range(ncores)], 0)[None]
    return tuple(np.ascontiguousarray(a.astype(np.float32)) for a in (y_p, y_s, ret_p, ret_s, pool_p, pool_s, conv_p, conv_s))


def kernel(**inputs):
    return run_cfg(Cfg(), inputs)
```

```python
import numpy as np
import concourse.bass as bass
import concourse.mybir as mybir
from concourse.bass_utils import run_bass_kernel_spmd

F32 = mybir.dt.float32
BF16 = mybir.dt.bfloat16
AF = mybir.ActivationFunctionType
ALU = mybir.AluOpType
AX = mybir.AxisListType
_EVPAR = 1


class _Op:
    __slots__ = ("eng", "fn", "deps", "is_dma", "sem", "semval", "signal", "prev_dma")

    def __init__(self, eng, fn, is_dma=False):
        self.eng = eng
        self.fn = fn
        self.deps = []
        self.is_dma = is_dma
        self.sem = None
        self.semval = None
        self.signal = False
        self.prev_dma = None


class _Rec:
    def __init__(self):
        self.call = None

    def __getattr__(self, name):
        def f(*a, **kw):
            self.call = (name, a, kw)
            return self
        return f


def _bind(fn):
    rec = _Rec()
    fn(rec)
    c = rec.call
    return lambda e: getattr(e, c[0])(*c[1], **c[2])


class Sched:
    ENGS = ("pe", "act", "dve", "pool", "sp")

    def __init__(self, ndma=20):
        self.ops = {e: [] for e in self.ENGS}
        self.last_w = {}
        self.readers = {}
        self.ndma = ndma
        self.dma_rr = {e: 0 for e in self.ENGS}
        self.dma_last = {}
        self.dma_count = {}

    def _add(self, op, reads, writes):
        deps = []
        for k in reads:
            w = self.last_w.get(k)
            if w is not None:
                deps.append(w)
        for k in writes:
            w = self.last_w.get(k)
            if w is not None:
                deps.append(w)
            for r in self.readers.get(k, ()):
                if (not r.is_dma) and (not op.is_dma) and r.eng == op.eng and op.eng in ("dve", "act"):
                    continue
                deps.append(r)
        seen = set()
        for d in deps:
            if d is op or id(d) in seen:
                continue
            seen.add(id(d))
            if (not d.is_dma) and (not op.is_dma) and d.eng == "pe" and op.eng == "pe":
                continue
            op.deps.append(d)
        for k in reads:
            self.readers.setdefault(k, []).append(op)
        for k in writes:
            self.last_w[k] = op
            self.readers[k] = []
        self.ops[op.eng].append(op)
        return op

    def op(self, eng, fn, reads=(), writes=()):
        return self._add(_Op(eng, _bind(fn)), reads, writes)

    def dma(self, eng, fn, reads=(), writes=()):
        op = _Op(eng, _bind(fn), is_dma=True)
        slot = (eng, self.dma_rr[eng] % self.ndma)
        self.dma_rr[eng] += 1
        op.sem = slot
        op.prev_dma = self.dma_last.get(slot)
        self.dma_count[slot] = self.dma_count.get(slot, 0) + 1
        op.semval = 16 * self.dma_count[slot]
        self.dma_last[slot] = op
        return self._add(op, reads, writes)

    def barrier(self, markers):
        ms = [self.op(e, fn, reads, writes) for (e, fn, reads, writes) in markers]
        lastd = list(self.dma_last.values())
        for e in self.ENGS:
            b = _Op(e, None)
            b.deps = list(ms) + lastd
            self.ops[e].append(b)

    def emit(self, nc, final_deps):
        for e in self.ENGS:
            for op in self.ops[e]:
                for d in op.deps:
                    if not d.is_dma:
                        d.signal = True
        fence = _Op("sp", None)
        fence.deps = list(final_deps)
        for d in fence.deps:
            if not d.is_dma:
                d.signal = True
        self.ops["sp"].append(fence)
        for e in self.ENGS:
            n = 0
            for op in self.ops[e]:
                if op.is_dma or op.fn is None:
                    continue
                if op.signal:
                    n += 1
                    op.semval = n
                    op.sem = ("eng", e)
        sem_names = [("eng", e) for e in self.ENGS]
        sem_names += sorted(set(self.dma_last.keys()))
        ctx = []
        sems = {}
        for sn in sem_names:
            cm = nc.semaphore("s_" + "_".join(str(x) for x in sn))
            sems[sn] = cm.__enter__()
            ctx.append(cm)
        sched = self

        def run(engobj, ename):
            waited = {}

            def wait(sn, val):
                if waited.get(sn, 0) >= val:
                    return
                waited[sn] = val
                engobj.wait_ge(sems[sn], val)

            for op in sched.ops[ename]:
                for d in op.deps:
                    wait(d.sem, d.semval)
                if op.is_dma and op.prev_dma is not None:
                    wait(op.prev_dma.sem, op.prev_dma.semval)
                if op.fn is None:
                    continue
                ins = op.fn(engobj)
                if op.is_dma:
                    ins.then_inc(sems[op.sem], 16)
                elif op.signal:
                    ins.then_inc(sems[op.sem], 1)

        with nc.Block() as block:
            @block.tensor
            def _(e):
                run(e, "pe")

            @block.scalar
            def _(e):
                run(e, "act")

            @block.vector
            def _(e):
                run(e, "dve")

            @block.gpsimd
            def _(e):
                run(e, "pool")

            @block.sync
            def _(e):
                run(e, "sp")
        for cm in reversed(ctx):
            cm.__exit__(None, None, None)


class Cfg:
    def __init__(self, D=2048, H=8, DFF=5632, NHT=8, NS=16, PAST=16384, NB=4):
        self.D, self.H, self.DFF, self.NHT, self.NS, self.PAST, self.NB = D, H, DFF, NHT, NS, PAST, NB
        self.KC = D // 128
        self.DK, self.DV = 256, 512
        self.DQK, self.DVT = H * 256, H * 512
        self.DIN = 2 * self.DQK + 2 * self.DVT + 3 * D
        self.NPRE = NHT - 1
        self.NT = NHT + 1
        self.NCOL = self.NT * 128 + NS
        self.SC0 = self.NT * 128
        self.FC = DFF // 128
        self.NG = self.FC // 4
        self.offs = np.cumsum([0, self.DQK, self.DQK, self.DVT, self.DVT, D, D, D])
        self.SEQ = 2 * NHT * 128


def _ngroups(n):
    out, c = [], 0
    while c < n:
        w = min(512, n - c)
        out.append((c, w))
        c += w
    return out


def build_program(cfg):
    from contextlib import ExitStack
    D, H, KC, NT, NS, NCOL, SC0, NPRE = cfg.D, cfg.H, cfg.KC, cfg.NT, cfg.NS, cfg.NCOL, cfg.SC0, cfg.NPRE
    DFF, FC, NG = cfg.DFF, cfg.FC, cfg.NG
    nc = bass.Bass("TRN2", target_bir_lowering=False)
    S = Sched()

    def din(name, shape, dt=F32):
        return nc.dram_tensor(name, list(shape), dt, kind="ExternalInput").ap()

    def dout(name, shape, dt=F32):
        return nc.dram_tensor(name, list(shape), dt, kind="ExternalOutput").ap()

    def dscr(name, shape, dt):
        return nc.dram_tensor(name, list(shape), dt, kind="Internal").ap()

    xw = din("xw", [(NPRE + NT) * 128, D])
    xs = din("xs", [NS, D])
    c17 = din("c17", [NS + 1, D])
    premask = din("premask", [1, 128])
    cosT = din("cos_t", [128, NT + NPRE + 1, 128])
    sinT = din("sin_t", [128, NT + NPRE + 1, 128])
    decayT = din("decay_t", [128, H, 128])
    gq = din("gq", [128, H])
    kd = din("kd", [128, H])
    invc = din("invc", [128, 4, 16])
    ident_d = din("ident", [128, 128])
    m16_d = din("m16", [128, NS, NS])
    norm1T = din("norm1T", [128, KC])
    norm2T = din("norm2T", [128, KC])
    bmodT = din("bmodT", [128, 6 * KC])
    bmod_row = din("bmod_row", [1, 6 * D])
    gn_g = din("gn_g", [1, cfg.DVT])
    pscaleT = din("pscaleT", [128, KC])
    convwT = din("convwT", [128, 3, 2 * FC])
    convbT = din("convbT", [128, 2 * FC])
    normf = din("normf", [1, D])
    w_mod = din("w_mod", [D, 6 * D])
    w_in = din("w_in", [D, cfg.DIN])
    w_pa = din("w_proj_a", [cfg.DVT, D])
    w_pg = din("w_pool_grp", [4, D // 4, D // 4])
    w_pb = din("w_proj_b", [D, D])
    w_out = din("w_out", [D, D])
    w_up = din("w_up", [D, 2 * DFF])
    w_dn = din("w_down", [DFF, D])
    st_ret = din("st_ret", [NS, H, 256, 512])
    st_pool = din("st_pool", [NS, 15, D])
    st_conv = din("st_conv", [NS, 2, 2 * DFF])
    y_main = dout("y_main", [cfg.NHT * 128, D])
    y_s = dout("y_s", [NS, D])
    ret_p = dout("ret_p", [H, 256, 512])
    ret_s = dout("ret_s", [NS, H, 256, 512])
    pool_p = dout("pool_p", [15, D])
    pool_s = dout("pool_s", [NS, 15, D])
    conv_p = dout("conv_p", [2, 2 * DFF])
    conv_s = dout("conv_s", [NS, 2, 2 * DFF])
    hpre_d = dscr("hpre_d", [max(NPRE, 1), 128, KC * 128], BF16)
    actT_d = dscr("actT_d", [H * 4, 128, NCOL], BF16)
    sga_d = dscr("sga_d", [KC, 128, NCOL], BF16)
    sgb_d = dscr("sgb_d", [KC, 128, NCOL], BF16)

    outs = []
    uid = [0]

    def U(p="k"):
        uid[0] += 1
        return "%s%d" % (p, uid[0])

    es = ExitStack()
    with es:
        scopes = [es]

        def sb(name, shape, dt):
            return scopes[-1].enter_context(nc.sbuf_tensor("sb_" + name, list(shape), dt))

        def push_scope():
            scopes.append(ExitStack())

        def pop_scope():
            do_barrier()
            scopes.pop().close()

        banks = [es.enter_context(nc.psum_tensor("bank%d" % i, [128, 512], F32)) for i in range(4)]
        bank45 = es.enter_context(nc.psum_tensor("bank45", [128, 2, 512], F32))
        banks += [bank45[:, 0, :], bank45[:, 1, :]]
        banks += [es.enter_context(nc.psum_tensor("bank%d" % i, [128, 512], F32)) for i in (6, 7)]
        bank_rr = [0]

        def bank():
            i = bank_rr[0] % 7
            bank_rr[0] += 1
            return banks[i], "bank%d" % i

        NSLOT = 3
        slots = [sb("wslot%d" % i, [128, 8192], BF16) for i in range(NSLOT)]
        slot_rr = [0]

        def load_slab(parts, nk, ncols):
            i = slot_rr[0] % len(slots)
            slot_rr[0] += 1
            key = "wslot%d" % i
            view = slots[i][:, 0:nk * ncols].rearrange("p (k n) -> p k n", k=nk)
            for ap, c0 in parts:
                w = ap.shape[1]
                S.dma("pool", lambda e, ap=ap, c0=c0, w=w: e.dma_start(
                    out=view[:, :, c0:c0 + w], in_=ap.rearrange("(k p) n -> p k n", p=128)), writes=[key])
            return view, key

        ident = sb("ident", [128, 128], F32)
        identb = sb("identb", [128, 128], BF16)
        S.dma("sp", lambda e: e.dma_start(out=ident[:], in_=ident_d), writes=["ident"])
        S.op("dve", lambda e: e.tensor_copy(out=identb[:], in_=ident[:]), reads=["ident"], writes=["identb"])
        bscr = sb("bscr", [128, 8], F32)

        def do_barrier():
            S.barrier([
                ("pe", lambda e: e.matmul(out=banks[6][0:1, 0:1], lhsT=identb[0:1, 0:1], rhs=identb[0:1, 0:1], start=True, stop=True), ["identb"], ["bank6", "bank6b"]),
                ("act", lambda e: e.activation(out=bscr[:, 0:1], in_=ident[:, 0:1], func=AF.Copy), ["ident"], ["bscr0"]),
                ("dve", lambda e: e.memset(bscr[:, 1:2], 0.0), [], ["bscr1"]),
                ("pool", lambda e: e.memset(bscr[:, 2:3], 0.0), [], ["bscr2"]),
            ])

        small_f = sb("small_f", [128, 9 * KC + 3 * H + 64 + 8 * FC], F32)
        o = 0
        def carve(n):
            nonlocal o
            v = small_f[:, o:o + n]
            o += n
            return v
        n1T, n2T, bmT0 = carve(KC), carve(KC), carve(6 * KC)
        gq_s, kd_s, invc_s, psc_s = carve(H), carve(H), carve(64), carve(KC)
        cw_s, cb_s = carve(6 * FC), carve(2 * FC)
        kd16 = carve(H)
        cw_s3 = cw_s.rearrange("p (r c) -> p r c", r=3)
        invc_s3 = invc_s.rearrange("p (g c) -> p g c", g=4)
        for dst, src in ((n1T, norm1T), (n2T, norm2T), (bmT0, bmodT), (gq_s, gq), (kd_s, kd), (psc_s, pscaleT), (cb_s, convbT)):
            S.dma("sp", lambda e, dst=dst, src=src: e.dma_start(out=dst, in_=src), writes=["consts"])
        S.dma("sp", lambda e: e.dma_start(out=cw_s3, in_=convwT), writes=["consts"])
        S.op("dve", lambda e: e.tensor_scalar(out=kd16, in0=kd_s, scalar1=0.0625, scalar2=None, op0=ALU.mult), reads=["consts"], writes=["kd16"])
        S.dma("sp", lambda e: e.dma_start(out=invc_s3, in_=invc), writes=["consts"])
        NCS = NT + NPRE + 1

        NM = NS + 1
        gate_d = dscr("gate_d", [NM, 2 * D], F32)
        modk_d = dscr("modk_d", [128, 4, KC * NM], F32)
        scT = sb("scT", [128, KC, NM], BF16)
        SPM = D // 512

        def mod_slab(sl, modbuf, ob0, gt_fn):
            wv, wk = load_slab([(w_mod[:, sl * 512:(sl + 1) * 512], 0)], KC, 512)
            pb, pk = bank()
            for j in range(4):
                for kc in range(KC):
                    S.op("pe", lambda e: e.matmul(out=pb[:, j * NM:(j + 1) * NM], lhsT=wv[:, kc, j * 128:(j + 1) * 128],
                         rhs=scT[:, kc, :], start=(kc == 0), stop=(kc == KC - 1)), reads=[wk, "scT"], writes=[pk])
            for j in range(4):
                ob = sl * 4 + j
                S.op("dve", lambda e: e.tensor_scalar(out=modbuf[:, ob - ob0, :], in0=pb[:, j * NM:(j + 1) * NM],
                     scalar1=bmT0[:, ob:ob + 1], scalar2=None, op0=ALU.add), reads=[pk, "consts"], writes=["modbuf"])
            mi = (sl * 512) // D
            if mi in (2, 5):
                gi = 0 if mi == 2 else 1
                c0 = sl * 512 - mi * D
                pb3, pk3 = bank()
                for kc in range(KC):
                    S.op("pe", lambda e: e.matmul(out=pb3[0:NM, :], lhsT=scT[:, kc, :], rhs=wv[:, kc, :],
                         start=(kc == 0), stop=(kc == KC - 1)), reads=[wk, "scT"], writes=[pk3])
                gt, gtk, br, brk = gt_fn()
                S.dma("sp", lambda e: e.dma_start(out=br[:], in_=bmod_row[:, mi * D + c0:mi * D + c0 + 512].partition_broadcast(NM)), writes=[brk])
                S.op("dve", lambda e: e.tensor_tensor(out=gt[:], in0=pb3[0:NM, :], in1=br[:], op=ALU.add), reads=[pk3, brk], writes=[gtk])
                S.dma("sp", lambda e: e.dma_start(out=gate_d[:, gi * D + c0:gi * D + c0 + 512], in_=gt[:]), reads=[gtk], writes=["gate_d"])

        def mod_finish(modbuf, A, Ak, nT, scale_rows, shift_rows, q0):
            S.op("dve", lambda e: e.tensor_scalar(out=A[:], in0=modbuf[:, scale_rows:scale_rows + KC, :], scalar1=1.0, scalar2=None, op0=ALU.add), reads=["modbuf"], writes=[Ak])
            S.op("dve", lambda e: e.tensor_tensor(out=A[:], in0=A[:], in1=nT.unsqueeze(2).to_broadcast([128, KC, NM]), op=ALU.mult), reads=[Ak, "consts"], writes=[Ak])
            S.dma("sp", lambda e: e.dma_start(out=modk_d[:, q0, :].rearrange("p (k n) -> p k n", k=KC), in_=A[:]), reads=[Ak], writes=["modk_d"])
            S.dma("sp", lambda e: e.dma_start(out=modk_d[:, q0 + 1, :].rearrange("p (k n) -> p k n", k=KC), in_=modbuf[:, shift_rows:shift_rows + KC, :]), reads=["modbuf"], writes=["modk_d"])

        push_scope()
        modT = sb("modT", [128, 2 * KC, NM], F32)
        A1 = sb("A1", [128, KC, NM], F32)
        csb = sb("csb", [NM, D], F32)
        S.dma("sp", lambda e: e.dma_start(out=csb[:], in_=c17), writes=["csb"])
        S.op("act", lambda e: e.activation(out=csb[:], in_=csb[:], func=AF.Silu), reads=["csb"], writes=["csb"])
        pb, pk = bank()
        pv = pb[:, 0:KC * NM].rearrange("p (k n) -> p k n", k=KC)
        for kc in range(KC):
            S.op("pe", lambda e, kc=kc: e.transpose(out=pv[:, kc, :], in_=csb[:, kc * 128:(kc + 1) * 128], identity=ident[0:NM, 0:NM]),
                 reads=["csb", "ident"], writes=[pk])
        S.op("dve", lambda e: e.tensor_copy(out=scT[:], in_=pv), reads=[pk], writes=["scT"])
        for sl in range(2 * SPM):
            mod_slab(sl, modT, 0, None)
        mod_finish(modT, A1, "A1", n1T, KC, 0, 0)
        pop_scope()

        stat = sb("stat", [128, 8 * 16], F32)
        st_rr = [0]
        stmp = sb("stmp", [128, 4, NS], F32)
        push_scope()
        hT = sb("hT", [128, KC, NCOL], BF16)
        push_scope()
        xring = [sb("xring%d" % i, [128, D], F32) for i in range(2)]
        xr_rr = [0]
        sqj = sb("sqj", [128, D], BF16)
        pmask = sb("pmask", [128, 128], F32)
        S.dma("sp", lambda e: e.dma_start(out=pmask[:], in_=premask.partition_broadcast(128)), writes=["consts"])
        evt = [sb("evt%d" % i, [128, 4, 128], F32) for i in range(2)]
        mA = sb("mA", [128, 2, KC, NM], F32)
        S.dma("sp", lambda e: e.dma_start(out=mA[:], in_=modk_d[:, 0:2, :].rearrange("p a (k n) -> p a k n", k=KC)), reads=["modk_d"], writes=["mA"])

        def stat4():
            i = st_rr[0] % 16
            st_rr[0] += 1
            return stat[:, i * 8:(i + 1) * 8], "stat%d" % i

        def rstd_of(xt, xk, n, st, sk, eps=1e-6):
            S.op("act", lambda e: e.activation(out=sqj[0:n, :], in_=xt[0:n, :], func=AF.Square, accum_out=st[0:n, 0:1]), reads=[xk], writes=["sqj", sk])
            S.op("dve", lambda e: e.tensor_scalar(out=st[0:n, 1:2], in0=st[0:n, 0:1], scalar1=1.0 / D, scalar2=eps, op0=ALU.mult, op1=ALU.add), reads=[sk], writes=[sk])
            S.op("act", lambda e: e.activation(out=st[0:n, 1:2], in_=st[0:n, 1:2], func=AF.Sqrt), reads=[sk], writes=[sk])
            S.op("dve", lambda e: e.reciprocal(out=st[0:n, 2:3], in_=st[0:n, 1:2]), reads=[sk], writes=[sk])

        def norm_T(xt, xk, n, dest_fn, dkey, mm, mmk, dest4_fn=None):
            st, sk = stat4()
            rstd_of(xt, xk, n, st, sk)
            S.op("dve", lambda e: e.tensor_scalar(out=xt[0:n, :], in0=xt[0:n, :], scalar1=st[0:n, 2:3], scalar2=None, op0=ALU.mult), reads=[xk, sk], writes=[xk])
            for k0 in range(0, KC, 4):
                pb, pk = bank()
                for j in range(min(4, KC - k0)):
                    kc = k0 + j
                    S.op("pe", lambda e, kc=kc, j=j, pb=pb: e.transpose(out=pb[:, j * 128:j * 128 + n], in_=xt[0:n, kc * 128:(kc + 1) * 128], identity=ident[0:n, 0:n]),
                         reads=[xk, "ident"], writes=[pk])
                if n == 128 and dest4_fn is not None and (k0 // 4) % 2 == _EVPAR and KC - k0 >= 4:
                    et, etk = evt[(k0 // 8) % 2], "evt%d" % ((k0 // 8) % 2)
                    S.op("dve", lambda e: e.tensor_tensor(out=et[:], in0=pb[:].rearrange("p (a n) -> p a n", a=4), in1=mm[:, 0, k0:k0 + 4, 0:1].to_broadcast([128, 4, 128]), op=ALU.mult),
                         reads=[pk, mmk], writes=[etk])
                    S.op("dve", lambda e: e.tensor_tensor(out=dest4_fn(k0), in0=et[:], in1=mm[:, 1, k0:k0 + 4, 0:1].to_broadcast([128, 4, 128]), op=ALU.add),
                         reads=[etk, mmk], writes=[dkey])
                    continue
                for j in range(min(4, KC - k0)):
                    kc = k0 + j
                    if n == 128:
                        S.op("act", lambda e, kc=kc, j=j, pb=pb: e.activation(out=dest_fn(kc), in_=pb[:, j * 128:(j + 1) * 128], func=AF.Identity,
                             scale=mm[:, 0, kc, 0:1], bias=mm[:, 1, kc, 0:1]), reads=[pk, mmk], writes=[dkey])
                    else:
                        S.op("dve", lambda e, kc=kc, j=j, pb=pb: e.tensor_tensor(out=stmp[:, j, :], in0=pb[:, j * 128:j * 128 + n], in1=mm[:, 0, kc, 1:NM], op=ALU.mult),
                             reads=[pk, mmk], writes=["stmp%d" % j])
                        S.op("dve", lambda e, kc=kc, j=j, pb=pb: e.tensor_tensor(out=dest_fn(kc), in0=stmp[:, j, :], in1=mm[:, 1, kc, 1:NM], op=ALU.add),
                             reads=["stmp%d" % j, mmk], writes=[dkey])

        def load_x(src_ap, n):
            i = xr_rr[0] % 2
            xr_rr[0] += 1
            xt, xk = xring[i], "xring%d" % i
            S.dma("sp", lambda e: e.dma_start(out=xt[0:n, :], in_=src_ap), writes=[xk])
            return xt, xk

        hstage = [sb("hstage%d" % i, [128, KC, 128], BF16) for i in range(2)]
        pm_b = pmask[:].unsqueeze(1).to_broadcast([128, KC, 128])
        for p in range(NPRE):
            xt, xk = load_x(xw[p * 128:(p + 1) * 128, :], 128)
            hs, hk = hstage[p % 2], "hstage%d" % (p % 2)
            norm_T(xt, xk, 128, lambda kc, hs=hs: hs[:, kc, :], hk, mA, "mA", dest4_fn=lambda k0, hs=hs: hs[:, k0:k0 + 4, :])
            S.op("dve", lambda e, hs=hs: e.tensor_tensor(out=hs[:], in0=hs[:], in1=pm_b, op=ALU.mult), reads=[hk, "consts"], writes=[hk])
            S.dma("sp", lambda e, hs=hs, p=p: e.dma_start(out=hpre_d[p].rearrange("p (k n) -> p k n", k=KC), in_=hs[:]), reads=[hk], writes=["hpre%d" % p])
        for t in range(NT):
            xt, xk = load_x(xw[(NPRE + t) * 128:(NPRE + t + 1) * 128, :], 128)
            norm_T(xt, xk, 128, lambda kc, t=t: hT[:, kc, t * 128:(t + 1) * 128], "hT%d" % t, mA, "mA", dest4_fn=lambda k0, t=t: hT[:, k0:k0 + 4, t * 128:(t + 1) * 128])
        S.op("dve", lambda e: e.tensor_tensor(out=hT[:, :, 0:128], in0=hT[:, :, 0:128], in1=pm_b, op=ALU.mult), reads=["hT0", "consts"], writes=["hT0"])
        xt, xk = load_x(xs, NS)
        norm_T(xt, xk, NS, lambda kc: hT[:, kc, SC0:SC0 + NS], "hTs", mA, "mA")
        HT_KEYS = ["hT%d" % t for t in range(NT)] + ["hTs"]
        pop_scope()

        push_scope()
        if True:
            sbh = sb
            decr = [sbh("decr%d" % i, [128, 128], F32) for i in range(2)]
            slots.append(sbh("wslot3", [128, 8192], BF16))
            m16 = sbh("m16", [128, NS, NS], F32)
            S.dma("sp", lambda e: e.dma_start(out=m16[:], in_=m16_d), writes=["consts"])
            cos_s = sbh("cos_s", [128, NCS, 128], F32)
            sin_s = sbh("sin_s", [128, NCS, 128], F32)
            S.dma("sp", lambda e: e.dma_start(out=cos_s[:], in_=cosT), writes=["consts"])
            S.dma("sp", lambda e: e.dma_start(out=sin_s[:], in_=sinT), writes=["consts"])
            hring = [sbh("hring%d" % i, [128, KC, 128], BF16) for i in range(2)]
            gnb = [sbh("gnb%d" % i, [128, 512], F32) for i in range(2)]
            rot = [sbh("rot%d" % i, [128, 2, 2, 128], F32) for i in range(2)]
            rtmp = [sbh("rtmp%d" % i, [128, 2, 2, 128], F32) for i in range(2)]
            tokb = [sbh("tokb%d" % i, [128, 8, 128], BF16) for i in range(2)]
            qkT = [sbh("qkT%d" % i, [128, 6, 128], BF16) for i in range(2)]
            vsb = [sbh("vsb%d" % i, [128, 512], BF16) for i in range(2)]
            sgr = [sbh("sgr%d" % i, [128, 512], F32) for i in range(2)]
            scs = [sbh("scs%d" % i, [128, 128], BF16) for i in range(2)]
            yn = [sbh("yn%d" % i, [128, 512], F32) for i in range(2)]
            actb = [sbh("actb%d" % i, [128, 512], BF16) for i in range(2)]
            Sst = sbh("Sst", [128, 2, 512], F32)
            Sbf = sbh("Sbf", [128, 2, 512], BF16)
            astage = sbh("astage", [128, 4, NCOL], BF16)
            sring = [sbh("sring%d" % i, [128, 2, 512], F32) for i in range(3)]
            NSR = 3
            tokb_s = sbh("tokb_s", [NS, 8, 128], BF16)
            vsb_s = sbh("vsb_s", [NS, 512], BF16)
            sgr_s = sbh("sgr_s", [NS, 512], F32)
            snb = [sbh("snb%d" % i, [128, 2, 512], BF16) for i in range(2)]
            KMH = max(NS // 2, 1)
            kmask_all = sbh("kmask_all", [NS, KMH, 256], BF16)
            qmask_all = sbh("qmask_all", [128, NS, 2, NS], BF16)
            qTs = sbh("qTs", [128, 2, NS], BF16)
            it = [0]
            sit = [0]
            log_g = [float(np.log(np.float32(1.0) - np.float32(2.0) ** np.float32(-5.0 - hh))) for hh in range(H)]

            ptr_v = banks[7][:].bitcast(BF16).rearrange("p (a n) -> p a n", a=8)
            psc = banks[7][:, 384:512]
            psck = "bank7s"
            ptr2_v = banks[6][:, 256:512].bitcast(BF16).rearrange("p (a n) -> p a n", a=4)
            pj_rr = [0]

            def pbank():
                i = pj_rr[0] % 2
                pj_rr[0] += 1
                return banks[i], "bank%d" % i

            class Ctx:
                pass

            def stageA_pe(c, which):
                n, lh, lk = c.n, c.lh, c.lk
                if which == "qk":
                    wv_, wk_ = c.wqk, c.wqkk
                    ncols = 512 if (c.is_main or c.is_s) else 256
                elif which == "v":
                    wv_, wk_ = c.wv, c.wvk
                    ncols = 512
                else:
                    if not (c.is_main or c.is_s):
                        return
                    wv_, wk_ = c.wg, c.wgk
                    ncols = 512
                pb, pk = pbank()
                c.pb[which] = (pb, pk)
                for kc in range(KC):
                    S.op("pe", lambda e: e.matmul(out=pb[0:n, 0:ncols], lhsT=lh(kc), rhs=wv_[:, kc, 0:ncols], start=(kc == 0), stop=(kc == KC - 1)), reads=lk + [wk_], writes=[pk])

            def stageA_rot(c):
                n, h, par = c.n, c.h, c.par
                pqk, pqkk = c.pb["qk"]
                na = 2 if (c.is_main or c.is_s) else 1
                ci = (NCS - 1) if c.is_s else c.w
                cb_ = cos_s[0:n, ci, :].unsqueeze(1).unsqueeze(1).to_broadcast([n, na, 2, 128])
                sb_ = sin_s[0:n, ci, :].unsqueeze(1).unsqueeze(1).to_broadcast([n, na, 2, 128])
                v4 = pqk[0:n, :].rearrange("p (a b n) -> p a b n", a=2, b=2)[:, 0:na]
                r, rk = rot[par], "rot%d" % par
                tc_, tck = rtmp[0][0:n, 0:na], "rtmp0"
                ts_, tsk = rtmp[1][0:n, 0:na], "rtmp1"
                S.op("dve", lambda e: e.tensor_tensor(out=tc_, in0=v4, in1=cb_, op=ALU.mult), reads=[pqkk, "consts"], writes=[tck])
                S.op("dve", lambda e: e.tensor_tensor(out=ts_, in0=v4, in1=sb_, op=ALU.mult), reads=[pqkk, "consts"], writes=[tsk])
                S.op("dve", lambda e: e.tensor_tensor(out=r[0:n, 0, 0:na, :], in0=tc_[:, :, 0, :], in1=ts_[:, :, 1, :], op=ALU.subtract), reads=[tck, tsk], writes=[rk])
                S.op("dve", lambda e: e.tensor_tensor(out=r[0:n, 1, 0:na, :], in0=tc_[:, :, 1, :], in1=ts_[:, :, 0, :], op=ALU.add), reads=[tck, tsk], writes=[rk])
                tb, tbk = c.tb, c.tbk
                if na == 2:
                    S.op("act", lambda e: e.activation(out=tb[0:n, 0:2, :], in_=r[0:n, :, 1, :], func=AF.Copy), reads=[rk], writes=[tbk])
                    if not c.is_s:
                        S.op("act", lambda e: e.activation(out=tb[0:n, 2:4, :], in_=r[0:n, :, 1, :], func=AF.Copy, scale=gq_s[0:n, h:h + 1]), reads=[rk, "consts"], writes=[tbk])
                    S.op("act", lambda e: e.mul(out=tb[0:n, 4:6, :], in_=r[0:n, :, 0, :], mul=0.0625), reads=[rk], writes=[tbk])
                if not c.is_s:
                    S.op("act", lambda e: e.activation(out=tb[0:n, 6:8, :], in_=r[0:n, :, 0, :], func=AF.Copy, scale=kd16[0:n, h:h + 1]), reads=[rk, "kd16"], writes=[tbk])

            def stageA_v(c):
                n = c.n
                pb, pk = c.pb["v"]
                S.op("act", lambda e: e.activation(out=c.vb[0:n, :], in_=pb[0:n, :], func=AF.Copy), reads=[pk], writes=[c.vbk])

            def stageA_g(c):
                if not (c.is_main or c.is_s):
                    return
                n = c.n
                pb, pk = c.pb["gr"]
                S.op("act", lambda e: e.activation(out=c.sg[0:n, :], in_=pb[0:n, :], func=AF.Silu), reads=[pk], writes=[c.sgk])
                g, gk = gnb[c.h % 2], "gnb%d" % (c.h % 2)
                S.op("dve", lambda e: e.tensor_tensor(out=c.sg[0:n, :], in0=c.sg[0:n, :], in1=g[0:n, :], op=ALU.mult), reads=[c.sgk, gk], writes=[c.sgk])

            def gn_part1(c, o_ps, ok, n):
                st, sk = stat4()
                st2, sk2 = stat4()
                c.gn = (o_ps, ok, n, st2, sk2)
                S.op("dve", lambda e: e.bn_stats(out=st[0:n, 0:6], in_=o_ps[0:n, :]), reads=[ok], writes=[sk])
                S.op("dve", lambda e: e.bn_aggr(out=st2[0:n, 0:2], in_=st[0:n, 0:6]), reads=[sk], writes=[sk2])
                S.op("dve", lambda e: e.tensor_scalar(out=st2[0:n, 2:3], in0=st2[0:n, 1:2], scalar1=1e-5, scalar2=None, op0=ALU.add), reads=[sk2], writes=[sk2])
                S.op("act", lambda e: e.activation(out=st2[0:n, 2:3], in_=st2[0:n, 2:3], func=AF.Sqrt), reads=[sk2], writes=[sk2])
                S.op("dve", lambda e: e.reciprocal(out=st2[0:n, 3:4], in_=st2[0:n, 2:3]), reads=[sk2], writes=[sk2])

            def gn_part2(c):
                if c is None or not hasattr(c, "gn"):
                    return
                o_ps, ok, n, st2, sk2 = c.gn
                par = c.par
                y, yk = yn[par], "yn%d" % par
                c.ab, c.abk = actb[par], "actb%d" % par
                S.op("dve", lambda e: e.tensor_scalar(out=y[0:n, :], in0=o_ps[0:n, :], scalar1=st2[0:n, 0:1], scalar2=st2[0:n, 3:4], op0=ALU.subtract, op1=ALU.mult), reads=[ok, sk2], writes=[yk])
                S.op("dve", lambda e: e.tensor_tensor(out=c.ab[0:n, :], in0=y[0:n, :], in1=c.sg[0:n, :], op=ALU.mult), reads=[yk, c.sgk], writes=[c.abk])

            def gn_chain(c, o_ps, ok, n):
                gn_part1(c, o_ps, ok, n)
                gn_part2(c)

            def stageB2(c):
                if c is None or not (c.is_main or c.is_s):
                    return
                n = c.n
                col0 = SC0 if c.is_s else (c.w - NPRE) * 128
                for j in range(4):
                    S.op("pe", lambda e: e.transpose(out=ptr2_v[:, j, 0:n], in_=c.ab[0:n, j * 128:(j + 1) * 128], identity=identb[0:n, 0:n]), reads=[c.abk, "identb"], writes=["bank6b"])
                S.op("act", lambda e: e.activation(out=astage[:, :, col0:col0 + n], in_=ptr2_v[:, 0:4, 0:n], func=AF.Copy), reads=["bank6b"], writes=["astage"])

            def make_ctx(h, w, slabs):
                c = Ctx()
                c.h, c.w = h, w
                c.is_s = (w == NPRE + NT)
                c.is_main = (w >= NPRE) and not c.is_s
                c.n = NS if c.is_s else 128
                c.par = it[0] % 2
                it[0] += 1
                c.pb = {}
                (c.wqk, c.wqkk), (c.wv, c.wvk), (c.wg, c.wgk) = slabs
                if c.is_s:
                    c.lh = lambda kc: hT[:, kc, SC0:SC0 + NS]
                    c.lk = ["hTs"]
                    c.tb, c.tbk, c.vb, c.vbk, c.sg, c.sgk = tokb_s, "tokb_s", vsb_s, "vsb_s", sgr_s, "sgr_s"
                else:
                    c.tb, c.tbk = tokb[c.par], "tokb%d" % c.par
                    c.vb, c.vbk = vsb[c.par], "vsb%d" % c.par
                    c.sg, c.sgk = sgr[c.par], "sgr%d" % c.par
                    if c.is_main:
                        t = w - NPRE
                        c.lh = lambda kc, t=t: hT[:, kc, t * 128:(t + 1) * 128]
                        c.lk = ["hT%d" % t]
                    else:
                        hr, hrk = hring[w % 2], "hring%d" % (w % 2)
                        S.dma("sp", lambda e: e.dma_start(out=hr[:], in_=hpre_d[w].rearrange("p (k n) -> p k n", k=KC)), reads=["hpre%d" % w], writes=[hrk])
                        c.lh = lambda kc, hr=hr: hr[:, kc, :]
                        c.lk = [hrk]
                return c

            NWT = NPRE + NT
            SPI = (NS + NWT - 1) // NWT
            for h in range(H):
                g1 = float(np.exp(np.float32(log_g[h])))
                g128 = float(np.exp(np.float32(128.0) * np.float32(log_g[h])))
                q0 = h * 256
                k0 = cfg.DQK + h * 256
                slabs = [load_slab([(w_in[:, k0:k0 + 256], 0), (w_in[:, q0:q0 + 256], 256)], KC, 512),
                         load_slab([(w_in[:, cfg.offs[2] + h * 512:cfg.offs[2] + (h + 1) * 512], 0)], KC, 512),
                         load_slab([(w_in[:, cfg.offs[3] + h * 512:cfg.offs[3] + (h + 1) * 512], 0)], KC, 512)]
                g_, gk_ = gnb[h % 2], "gnb%d" % (h % 2)
                S.dma("sp", lambda e: e.dma_start(out=g_[:], in_=gn_g[:, h * 512:(h + 1) * 512].partition_broadcast(128)), writes=[gk_])
                dcy, dcyk = decr[h % 2], "decr%d" % (h % 2)
                S.dma("sp", lambda e: e.dma_start(out=dcy[:], in_=decayT[:, h, :]), writes=[dcyk])
                sbase = sit[0]
                sit[0] += NS

                def sload(s_):
                    sr, srk = sring[(sbase + s_) % NSR], "sring%d" % ((sbase + s_) % NSR)
                    S.dma("sp", lambda e: e.dma_start(out=sr[:], in_=st_ret[s_, h].rearrange("(b p) v -> p b v", p=128)), writes=[srk])
                for s_ in range(min(NSR - 1, NS)):
                    sload(s_)
                S.op("dve", lambda e: e.memset(Sst[:], 0.0), writes=["Sst"])
                S.op("dve", lambda e: e.memset(Sbf[:], 0.0), writes=["Sbf"])
                cs_ = make_ctx(h, NPRE + NT, slabs)
                stageA_pe(cs_, "qk")
                stageA_rot(cs_)
                stageA_pe(cs_, "v")
                stageA_v(cs_)
                for j2 in range(2):
                    S.op("pe", lambda e: e.transpose(out=ptr_v[:, j2, 0:NS], in_=tokb_s[0:NS, j2, :], identity=identb[0:NS, 0:NS]), reads=["tokb_s", "identb"], writes=["ptr"])
                S.op("act", lambda e: e.activation(out=qTs[:], in_=ptr_v[:, 0:2, 0:NS], func=AF.Copy), reads=["ptr"], writes=["qTs"])

                def build_kmask(s0):
                    S.op("dve", lambda e: e.tensor_tensor(out=kmask_all[:], in0=tokb_s[0:NS, 4:6, :].rearrange("p a n -> p (a n)").unsqueeze(1).to_broadcast([NS, KMH, 256]),
                         in1=ident[0:NS, s0:s0 + KMH].unsqueeze(2).to_broadcast([NS, KMH, 256]), op=ALU.mult), reads=["tokb_s", "ident"], writes=["kmask_all"])
                build_kmask(0)
                S.op("dve", lambda e: e.tensor_tensor(out=qmask_all[:], in0=qTs[:].unsqueeze(1).to_broadcast([128, NS, 2, NS]),
                     in1=m16[:].unsqueeze(2).to_broadcast([128, NS, 2, NS]), op=ALU.mult), reads=["qTs", "consts"], writes=["qmask_all"])
                pos_, posk = banks[2], "bank2"
                sdone = [0]
                pendB = []

                def sample_A():
                    s_ = sdone[0]
                    if s_ >= NS:
                        return
                    sdone[0] += 1
                    if len(pendB) >= len(snb):
                        sample_B()
                    if s_ > 0 and s_ % KMH == 0:
                        build_kmask(s_)
                    si = sbase + s_
                    sr, srk = sring[si % NSR], "sring%d" % (si % NSR)
                    for blk in range(2):
                        S.op("pe", lambda e: e.matmul(out=bank45[:, blk, :], lhsT=kmask_all[:, s_ % KMH, blk * 128:(blk + 1) * 128], rhs=vsb_s[0:NS, :], start=True, stop=True), reads=["kmask_all", "vsb_s"], writes=["bank45"])
                    S.op("dve", lambda e: e.scalar_tensor_tensor(out=sr[:], in0=sr[:], scalar=g1, in1=bank45[:], op0=ALU.mult, op1=ALU.add), reads=["bank45", srk], writes=[srk])
                    outs.append(S.dma("act", lambda e: e.dma_start(out=ret_s[s_, h].rearrange("(b p) v -> p b v", p=128), in_=sr[:]), reads=[srk]))
                    sn, snk = snb[si % 2], "snb%d" % (si % 2)
                    S.op("act", lambda e: e.activation(out=sn[:], in_=sr[:], func=AF.Copy), reads=[srk], writes=[snk])
                    pendB.append((s_, sn, snk))
                    if s_ + NSR - 1 < NS:
                        sload(s_ + NSR - 1)

                def sample_B():
                    while pendB:
                        s_, sn, snk = pendB.pop(0)
                        for blk in range(2):
                            S.op("pe", lambda e: e.matmul(out=pos_[0:NS, :], lhsT=qmask_all[:, s_, blk, :], rhs=sn[:, blk, :], start=(s_ == 0 and blk == 0), stop=(s_ == NS - 1 and blk == 1)),
                                 reads=["qmask_all", snk], writes=[posk])

                ctxs = [None] * NWT
                ctxs[0] = make_ctx(h, 0, slabs)
                stageA_pe(ctxs[0], "qk")
                stageA_rot(ctxs[0])
                stageA_pe(ctxs[0], "v")
                stageA_v(ctxs[0])
                stageA_pe(ctxs[0], "gr")
                stageA_g(ctxs[0])
                prev = None
                for w in range(NWT):
                    c = ctxs[w]
                    nx = None
                    if w + 1 < NWT:
                        nx = ctxs[w + 1] = make_ctx(h, w + 1, slabs)
                    tb, tbk, vb, vbk = c.tb, c.tbk, c.vb, c.vbk
                    if c.is_main:
                        qt, qtk = qkT[c.par], "qkT%d" % c.par
                        for i6 in range(6):
                            S.op("pe", lambda e: e.transpose(out=ptr_v[:, i6, :], in_=tb[:, i6, :], identity=identb[:]), reads=[tbk, "identb"], writes=["ptr"])
                        S.op("act", lambda e: e.activation(out=qt[:], in_=ptr_v[:, 0:6, :], func=AF.Copy), reads=["ptr"], writes=[qtk])
                    if nx is not None:
                        stageA_pe(nx, "qk")
                        stageA_rot(nx)
                    if c.is_main:
                        for blk in range(2):
                            S.op("pe", lambda e: e.matmul(out=psc, lhsT=qt[:, 4 + blk, :], rhs=qt[:, blk, :], start=(blk == 0), stop=(blk == 1)), reads=[qtk], writes=[psck])
                        sc, sck = scs[c.par], "scs%d" % c.par
                        S.op("dve", lambda e: e.tensor_tensor(out=sc[:], in0=psc, in1=dcy[:], op=ALU.mult), reads=[psck, dcyk], writes=[sck])
                    stageB2(prev)
                    prev = c
                    for _ in range(SPI):
                        sample_A()
                    if nx is not None:
                        stageA_pe(nx, "v")
                        stageA_v(nx)
                    if c.is_main:
                        po, pok = banks[3], "bank3"
                        S.op("pe", lambda e: e.matmul(out=po[:], lhsT=sc[:], rhs=vb[:], start=True, stop=False), reads=[sck, vbk], writes=[pok])
                        for blk in range(2):
                            S.op("pe", lambda e: e.matmul(out=po[:], lhsT=qt[:, 2 + blk, :], rhs=Sbf[:, blk, :], start=False, stop=(blk == 1)), reads=[qtk, "Sbf"], writes=[pok])
                        gn_part1(c, po, pok, 128)
                    for blk in range(2):
                        S.op("pe", lambda e: e.matmul(out=bank45[:, blk, :], lhsT=tb[:, 6 + blk, :], rhs=vb[:], start=True, stop=True), reads=[tbk, vbk], writes=["bank45"])
                    S.op("dve", lambda e: e.scalar_tensor_tensor(out=Sst[:], in0=Sst[:], scalar=g128, in1=bank45[:], op0=ALU.mult, op1=ALU.add), reads=["bank45", "Sst"], writes=["Sst"])
                    S.op("act", lambda e: e.activation(out=Sbf[:], in_=Sst[:], func=AF.Copy), reads=["Sst"], writes=["Sbf"])
                    if w == NWT - 1:
                        outs.append(S.dma("sp", lambda e: e.dma_start(out=ret_p[h].rearrange("(b p) v -> p b v", p=128), in_=Sst[:]), reads=["Sst"]))
                    if nx is not None:
                        stageA_pe(nx, "gr")
                        stageA_g(nx)
                    sample_B()
                    if c.is_main:
                        gn_part2(c)
                while sdone[0] < NS:
                    sample_A()
                    sample_B()
                sample_B()
                stageB2(prev)
                stageA_pe(cs_, "gr")
                stageA_g(cs_)
                cs_.par = it[0] % 2
                it[0] += 1
                gn_chain(cs_, pos_, posk, NS)
                stageB2(cs_)
                S.dma("sp", lambda e: e.dma_start(out=actT_d[h * 4:(h + 1) * 4].rearrange("j p n -> p j n"), in_=astage[:]), reads=["astage"], writes=["actT_d%d" % h])
        slots.pop()
        slot_rr[0] = 0
        pop_scope()

        push_scope()
        GD = D // 4
        GC = GD // 128
        pT_d = dscr("pT_d", [KC, 128, NCOL], BF16)
        mix_d = dscr("mix_d", [KC, 128, NCOL], BF16)
        UB = 16 + NCOL
        ubuf = sb("ubuf", [128, UB], F32)
        pa = sb("pa", [128, UB], F32)
        pb2 = sb("pb2", [128, UB], F32)
        for bf_, k_ in ((ubuf, "ubuf"), (pa, "pa"), (pb2, "pb2")):
            S.op("dve", lambda e, bf_=bf_: e.memset(bf_[:, 0:16], 0.0), writes=[k_])
        gstage = sb("gstage", [128, 4, NCOL], BF16)
        ukeep = sb("ukeep", [128, KC, 32], F32)
        prevT = sb("prevT", [128, KC, NS], F32)
        prow = sb("prow", [NS, 15, GD], F32)
        ptok = sb("ptok", [NS, D], F32)
        utok = sb("utok", [32, D], F32)
        t16 = sb("t16", [128, 2, 16], F32)
        GRP = _ngroups(NCOL)
        for gi in range(4):
            w_ = 2 ** (gi + 1)
            nr = w_ - 1
            S.dma("sp", lambda e, gi=gi, nr=nr: e.dma_start(out=prow[:, 0:nr, :], in_=st_pool[:, 15 - nr:15, gi * GD:(gi + 1) * GD]), writes=["prow"])
            S.op("dve", lambda e, gi=gi: e.tensor_copy(out=ptok[:, gi * GD:(gi + 1) * GD], in_=prow[:, 0, :]), reads=["prow"], writes=["ptok"])
            for r_ in range(1, nr):
                S.op("dve", lambda e, gi=gi, r_=r_: e.tensor_tensor(out=ptok[:, gi * GD:(gi + 1) * GD], in0=ptok[:, gi * GD:(gi + 1) * GD], in1=prow[:, r_, :], op=ALU.add), reads=["prow", "ptok"], writes=["ptok"])
        pb, pk = bank()
        for kc in range(KC):
            S.op("pe", lambda e, kc=kc, pb=pb: e.transpose(out=pb[:, kc * NS:(kc + 1) * NS], in_=ptok[0:NS, kc * 128:(kc + 1) * 128], identity=ident[0:NS, 0:NS]), reads=["ptok", "ident"], writes=[pk])
        S.op("dve", lambda e, pb=pb: e.tensor_copy(out=prevT[:], in_=pb[:, 0:KC * NS].rearrange("p (k n) -> p k n", k=KC)), reads=[pk], writes=["prevT"])
        outs.append(S.dma("sp", lambda e: e.dma_start(out=pool_s[:, 0:14, :], in_=st_pool[:, 1:15, :])))

        def feat_mm(wv, wk, j, nk, rhs_fn, rkeys, evac):
            for (c0, wd) in GRP:
                pb, pk = bank()
                for kc in range(nk):
                    S.op("pe", lambda e, kc=kc, pb=pb, c0=c0, wd=wd: e.matmul(out=pb[:, 0:wd], lhsT=wv[:, kc, j * 128:(j + 1) * 128], rhs=rhs_fn(kc, c0, wd),
                         start=(kc == 0), stop=(kc == nk - 1)), reads=[wk] + rkeys, writes=[pk])
                evac(pb, pk, c0, wd)

        hrhs = lambda kc, c0, wd: hT[:, kc, c0:c0 + wd]
        modT2 = sb("modT2", [128, 4 * KC, NM], F32)
        A2 = sb("A2", [128, KC, NM], F32)
        gtmp = [sb("gtmp%d" % i, [NM, 512], F32) for i in range(2)]
        btmp = [sb("btmp%d" % i, [NM, 512], F32) for i in range(2)]
        gcnt = [0]

        def gt_fn():
            i = gcnt[0] % 2
            gcnt[0] += 1
            return gtmp[i], "gtmp%d" % i, btmp[i], "btmp%d" % i

        deferred = list(range(2 * SPM, 6 * SPM))
        g1_done = [0]
        N_G1 = 3 * (KC // 4)

        def run_deferred():
            g1_done[0] += 1
            target = (len(deferred_all) * g1_done[0] + N_G1 - 1) // N_G1
            while len(deferred_all) - len(deferred) < target and deferred:
                mod_slab(deferred.pop(0), modT2, 2 * KC, gt_fn)
            if not deferred and not fin[0]:
                fin[0] = True
                mod_finish(modT2, A2, "A2", n2T, 2 * KC, KC, 2)

        deferred_all = list(deferred)
        fin = [False]
        for i in range(KC // 4):
            wv, wk = load_slab([(w_in[:, cfg.offs[4] + i * 512:cfg.offs[4] + (i + 1) * 512], 0)], KC, 512)
            for j in range(4):
                c = i * 4 + j
                gi = c // GC
                w_ = 2 ** (gi + 1)
                feat_mm(wv, wk, j, KC, hrhs, HT_KEYS, lambda pb, pk, c0, wd: S.op("act", lambda e: e.activation(out=ubuf[:, 16 + c0:16 + c0 + wd], in_=pb[:, 0:wd], func=AF.Copy), reads=[pk], writes=["ubuf"]))
                S.op("act", lambda e, c=c: e.activation(out=ukeep[:, c, :], in_=ubuf[:, SC0:SC0 + 32], func=AF.Copy), reads=["ubuf"], writes=["ukeep"])
                src, srck = ubuf, "ubuf"
                for k_ in range(gi + 1):
                    sh = 2 ** k_
                    dst, dstk = (pa, "pa") if k_ % 2 == 0 else (pb2, "pb2")
                    S.op("dve", lambda e, src=src, dst=dst, sh=sh: e.tensor_tensor(out=dst[:, 16:16 + SC0], in0=src[:, 16:16 + SC0], in1=src[:, 16 - sh:16 + SC0 - sh], op=ALU.add), reads=[srck], writes=[dstk])
                    src, srck = dst, dstk
                S.op("dve", lambda e, src=src, j=j, w_=w_: e.scalar_tensor_tensor(out=gstage[:, j, 0:SC0], in0=src[:, 16:16 + SC0], scalar=1.0 / w_, in1=ubuf[:, 16:16 + SC0], op0=ALU.mult, op1=ALU.subtract), reads=[srck, "ubuf"], writes=["gstage"])
                S.op("dve", lambda e, src=src, gi=gi: e.tensor_tensor(out=t16[:, 0, :], in0=src[:, 144:160], in1=invc_s3[:, gi, :], op=ALU.mult), reads=[srck, "consts"], writes=["t16"])
                S.op("dve", lambda e, j=j: e.tensor_tensor(out=gstage[:, j, 128:144], in0=t16[:, 0, :], in1=ubuf[:, 144:160], op=ALU.subtract), reads=["t16", "ubuf"], writes=["gstage"])
                S.op("dve", lambda e, c=c: e.tensor_tensor(out=t16[:, 1, 0:NS], in0=ubuf[:, 16 + SC0:16 + SC0 + NS], in1=prevT[:, c, :], op=ALU.add), reads=["ubuf", "prevT"], writes=["t16b"])
                S.op("dve", lambda e, j=j, w_=w_: e.scalar_tensor_tensor(out=gstage[:, j, SC0:SC0 + NS], in0=t16[:, 1, 0:NS], scalar=1.0 / w_, in1=ubuf[:, 16 + SC0:16 + SC0 + NS], op0=ALU.mult, op1=ALU.subtract), reads=["t16b", "ubuf"], writes=["gstage"])
            S.dma("sp", lambda e, i=i: e.dma_start(out=pT_d[i * 4:(i + 1) * 4].rearrange("j p n -> p j n"), in_=gstage[:]), reads=["gstage"], writes=["pT_d%d" % i])
            run_deferred()
        for (oi, dst_d, dn) in ((5, sga_d, "sga_d"), (6, sgb_d, "sgb_d")):
            for i in range(KC // 4):
                wv, wk = load_slab([(w_in[:, cfg.offs[oi] + i * 512:cfg.offs[oi] + (i + 1) * 512], 0)], KC, 512)
                for j in range(4):
                    feat_mm(wv, wk, j, KC, hrhs, HT_KEYS, lambda pb, pk, c0, wd, j=j: S.op("act", lambda e: e.activation(out=gstage[:, j, c0:c0 + wd], in_=pb[:, 0:wd], func=AF.Sigmoid), reads=[pk], writes=["gstage"]))
                S.dma("sp", lambda e, i=i, dst_d=dst_d: e.dma_start(out=dst_d[i * 4:(i + 1) * 4].rearrange("j p n -> p j n"), in_=gstage[:]), reads=["gstage"], writes=["%s%d" % (dn, i)])
                run_deferred()
        for k0 in range(0, KC, 4):
            pb, pk = bank()
            for j in range(4):
                S.op("pe", lambda e, j=j, pb=pb, k0=k0: e.transpose(out=pb[0:32, j * 128:(j + 1) * 128], in_=ukeep[:, k0 + j, :], identity=ident[:]), reads=["ukeep", "ident"], writes=[pk])
            S.op("act", lambda e, pb=pb, k0=k0: e.activation(out=utok[:, k0 * 128:(k0 + 4) * 128], in_=pb[0:32, :], func=AF.Copy), reads=[pk], writes=["utok"])
        outs.append(S.dma("sp", lambda e: e.dma_start(out=pool_p, in_=utok[1:16, :]), reads=["utok"]))
        outs.append(S.dma("sp", lambda e: e.dma_start(out=pool_s[:, 14, :], in_=utok[16:32, :]), reads=["utok"]))
        pop_scope()
        pop_scope()

        push_scope()
        p2T = sb("p2T", [128, KC, NCOL], BF16)
        mixT = sb("mixT", [128, KC, NCOL], BF16)
        aTh = sb("aTh", [128, 2 * H, NCOL], BF16)
        pst = sb("pst", [128, 4, NCOL], BF16)
        tmpf = [sb("tmpf%d" % i, [128, 512], F32) for i in range(2)]
        tf_rr = [0]

        def tmp512():
            i = tf_rr[0] % 2
            tf_rr[0] += 1
            return tmpf[i], "tmpf%d" % i

        for g in range(4):
            S.dma("sp", lambda e, g=g: e.dma_start(out=pst[:, 0:GC, :], in_=pT_d[g * GC:(g + 1) * GC].rearrange("j p n -> p j n")), reads=["pT_d%d" % i for i in range(KC // 4)], writes=["pst"])
            wv, wk = load_slab([(w_pg[g], 0)], GC, GD)
            for j in range(GC):
                c = g * GC + j
                feat_mm(wv, wk, j, GC, lambda kc, c0, wd: pst[:, kc, c0:c0 + wd], ["pst"],
                        lambda pb, pk, c0, wd, c=c: S.op("act", lambda e: e.activation(out=p2T[:, c, c0:c0 + wd], in_=pb[:, 0:wd], func=AF.Copy, scale=psc_s[:, c:c + 1]), reads=[pk, "consts"], writes=["p2T"]))
        for i in range(KC // 4):
            wv, wk = load_slab([(w_pb[:, i * 512:(i + 1) * 512], 0)], KC, 512)
            S.dma("sp", lambda e, i=i: e.dma_start(out=pst[:], in_=sgb_d[i * 4:(i + 1) * 4].rearrange("j p n -> p j n")), reads=["sgb_d%d" % i], writes=["pst"])
            for j in range(4):
                ob = i * 4 + j
                feat_mm(wv, wk, j, KC, lambda kc, c0, wd: p2T[:, kc, c0:c0 + wd], ["p2T"],
                        lambda pb, pk, c0, wd, j=j, ob=ob: S.op("dve", lambda e: e.tensor_tensor(out=mixT[:, ob, c0:c0 + wd], in0=pb[:, 0:wd], in1=pst[:, j, c0:c0 + wd], op=ALU.mult), reads=[pk, "pst"], writes=["mixT"]))
        for half in range(2):
            S.dma("sp", lambda e, half=half: e.dma_start(out=aTh[:], in_=actT_d[half * 2 * H:(half + 1) * 2 * H].rearrange("j p n -> p j n")), reads=["actT_d%d" % h for h in range(H)], writes=["aTh"])
            for i in range(KC // 4):
                wv, wk = load_slab([(w_pa[half * 2 * H * 128:(half + 1) * 2 * H * 128, i * 512:(i + 1) * 512], 0)], 2 * H, 512)
                S.dma("sp", lambda e, i=i: e.dma_start(out=pst[:], in_=sga_d[i * 4:(i + 1) * 4].rearrange("j p n -> p j n")), reads=["sga_d%d" % i], writes=["pst"])
                for j in range(4):
                    ob = i * 4 + j

                    def ev(pb, pk, c0, wd, j=j, ob=ob):
                        tf, tfk = tmp512()
                        S.op("dve", lambda e: e.tensor_tensor(out=tf[:, 0:wd], in0=pb[:, 0:wd], in1=pst[:, j, c0:c0 + wd], op=ALU.mult), reads=[pk, "pst"], writes=[tfk])
                        S.op("dve", lambda e: e.tensor_tensor(out=mixT[:, ob, c0:c0 + wd], in0=mixT[:, ob, c0:c0 + wd], in1=tf[:, 0:wd], op=ALU.add), reads=[tfk, "mixT"], writes=["mixT"])
                    feat_mm(wv, wk, j, 2 * H, lambda kc, c0, wd: aTh[:, kc, c0:c0 + wd], ["aTh"], ev)
        S.dma("sp", lambda e: e.dma_start(out=mix_d.rearrange("k p n -> p k n"), in_=mixT[:]), reads=["mixT"], writes=["mix_d"])
        pop_scope()

        push_scope()
        NHT = cfg.NHT
        x1 = sb("x1", [128, NHT, D], F32)
        samp = sb("samp", [NS, 1, D], F32)
        gpr = [sb("gpr%d" % i, [128, 512], F32) for i in range(2)]
        gsr = [sb("gsr%d" % i, [NS, 512], F32) for i in range(2)]
        g_rr = [0]

        def load_gates(goff, cb):
            i = g_rr[0] % 2
            g_rr[0] += 1
            S.dma("sp", lambda e: e.dma_start(out=gpr[i][:], in_=gate_d[0:1, goff + cb * 512:goff + (cb + 1) * 512].partition_broadcast(128)), reads=["gate_d"], writes=["gpr%d" % i])
            S.dma("sp", lambda e: e.dma_start(out=gsr[i][:], in_=gate_d[1:NM, goff + cb * 512:goff + (cb + 1) * 512]), reads=["gate_d"], writes=["gsr%d" % i])
            return gpr[i], gsr[i], ["gpr%d" % i, "gsr%d" % i]
        h2T = sb("h2T", [128, KC, NCOL], BF16)
        tmpf = [sb("tmpg%d" % i, [128, 512], F32) for i in range(2)]
        push_scope()
        xb = sb("xb", [128, D], F32)
        push_scope()
        mring = [sb("mring%d" % i, [128, KC, 128], BF16) for i in range(2)]

        def tile_info(tt):
            if tt == 0:
                return xb, "xb", 128, 0, lambda a, b: xb[:, a:b], None
            if tt == NT:
                return samp, "samp0", NS, SC0, lambda a, b: samp[:, 0, a:b], None
            return x1, "x1_%d" % tt, 128, tt * 128, lambda a, b, tt=tt: x1[:, tt - 1, a:b], None

        for tt in range(NT + 1):
            _, xk, n, c0_, xv, _g = tile_info(tt)
            src = xs if tt == NT else xw[(NPRE + tt) * 128:(NPRE + tt + 1) * 128, :]
            S.dma("sp", lambda e, xv=xv, src=src: e.dma_start(out=xv(0, D), in_=src), writes=[xk])
        mr_rr = [0]
        for cb in range(D // 512):
            wv, wk = load_slab([(w_out[:, cb * 512:(cb + 1) * 512], 0)], KC, 512)
            gp_, gs_, gkeys = load_gates(0, cb)
            for tt in range(NT + 1):
                _, xk, n, c0_, xv, _g = tile_info(tt)
                mi_ = mr_rr[0] % 2
                mr_rr[0] += 1
                mr, mrk = mring[mi_], "mring%d" % mi_
                S.dma("sp", lambda e, mr=mr, c0_=c0_, n=n: e.dma_start(out=mr[:, :, 0:n], in_=mix_d[:, :, c0_:c0_ + n].rearrange("k p n -> p k n")), reads=["mix_d"], writes=[mrk])
                pb, pk = bank()
                for kc in range(KC):
                    S.op("pe", lambda e, kc=kc, pb=pb, mr=mr, n=n, wv=wv: e.matmul(out=pb[0:n, :], lhsT=mr[:, kc, 0:n], rhs=wv[:, kc, :], start=(kc == 0), stop=(kc == KC - 1)), reads=[wk, mrk], writes=[pk])
                tf, tfk = tmp512()
                gsrc = gs_[:] if tt == NT else gp_[:]
                S.op("dve", lambda e, pb=pb, n=n, tf=tf, gsrc=gsrc: e.tensor_tensor(out=tf[0:n, :], in0=pb[0:n, :], in1=gsrc, op=ALU.mult), reads=[pk] + gkeys, writes=[tfk])
                S.op("dve", lambda e, xv=xv, tf=tf, n=n, cb=cb: e.tensor_tensor(out=xv(cb * 512, (cb + 1) * 512), in0=xv(cb * 512, (cb + 1) * 512), in1=tf[0:n, :], op=ALU.add), reads=[tfk, xk], writes=[xk])

        def norm_T2(xv, xk, n, junk, junkk):
            st, sk = stat4()
            S.op("act", lambda e: e.activation(out=junk, in_=xv(0, D).rearrange("p (a b) -> p a b", a=4), func=AF.Square, accum_out=st[0:n, 0:1]), reads=[xk], writes=[junkk, sk])
            S.op("dve", lambda e: e.tensor_scalar(out=st[0:n, 1:2], in0=st[0:n, 0:1], scalar1=1.0 / D, scalar2=1e-6, op0=ALU.mult, op1=ALU.add), reads=[sk], writes=[sk])
            S.op("act", lambda e: e.activation(out=st[0:n, 1:2], in_=st[0:n, 1:2], func=AF.Sqrt), reads=[sk], writes=[sk])
            S.op("dve", lambda e: e.reciprocal(out=st[0:n, 2:3], in_=st[0:n, 1:2]), reads=[sk], writes=[sk])
            return st, sk

        pop_scope()
        push_scope()
        xnb = [sb("xnb0", [128, D], F32)]
        pmask = sb("pmask2", [128, 128], F32)
        S.dma("sp", lambda e: e.dma_start(out=pmask[:], in_=premask.partition_broadcast(128)), writes=["pmask2"])
        mB = sb("mB", [128, 2, KC, NM], F32)
        S.dma("sp", lambda e: e.dma_start(out=mB[:], in_=modk_d[:, 2:4, :].rearrange("p a (k n) -> p a k n", k=KC)), reads=["modk_d"], writes=["mB"])
        xn_rr = [0]
        for tt in range(NT + 1):
            _, xk, n, c0_, xv, _g = tile_info(tt)
            xt, xtk = xnb[0], "xnb0"
            st, sk = norm_T2(xv, xk, n, xt[0:n, :].rearrange("p (a b) -> p a b", a=4), xtk)
            S.op("act", lambda e, xt=xt, xv=xv, n=n, st=st: e.activation(out=xt[0:n, :], in_=xv(0, D), func=AF.Copy, scale=st[0:n, 2:3]), reads=[xk, sk], writes=[xtk])
            for k0 in range(0, KC, 4):
                pb, pk = bank()
                for j in range(4):
                    kc = k0 + j
                    S.op("pe", lambda e, kc=kc, j=j, pb=pb, xt=xt, n=n: e.transpose(out=pb[:, j * 128:j * 128 + n], in_=xt[0:n, kc * 128:(kc + 1) * 128], identity=ident[0:n, 0:n]), reads=[xtk, "ident"], writes=[pk])
                for j in range(4):
                    kc = k0 + j
                    if n == 128:
                        S.op("act", lambda e, kc=kc, j=j, pb=pb, c0_=c0_: e.activation(out=h2T[:, kc, c0_:c0_ + 128], in_=pb[:, j * 128:(j + 1) * 128], func=AF.Identity,
                             scale=mB[:, 0, kc, 0:1], bias=mB[:, 1, kc, 0:1]), reads=[pk, "mB"], writes=["h2T"])
                    else:
                        S.op("dve", lambda e, kc=kc, j=j, pb=pb: e.tensor_tensor(out=stmp[:, j, :], in0=pb[:, j * 128:j * 128 + n], in1=mB[:, 0, kc, 1:NM], op=ALU.mult), reads=[pk, "mB"], writes=["stmp%d" % j])
                        S.op("dve", lambda e, kc=kc, j=j: e.tensor_tensor(out=h2T[:, kc, SC0:SC0 + NS], in0=stmp[:, j, :], in1=mB[:, 1, kc, 1:NM], op=ALU.add), reads=["stmp%d" % j, "mB"], writes=["h2T"])
        S.op("dve", lambda e: e.tensor_tensor(out=h2T[:, :, 0:128], in0=h2T[:, :, 0:128], in1=pmask[:].unsqueeze(1).to_broadcast([128, KC, 128]), op=ALU.mult), reads=["h2T", "pmask2"], writes=["h2T"])
        pop_scope()
        pop_scope()

        push_scope()
        gT = sb("gT", [128, 4, NCOL], BF16)
        upbuf = sb("upbuf", [128, NCOL], F32)
        cab = [sb("cab%d" % i, [128, NCOL], F32) for i in range(2)]
        cst = sb("cst", [NS, 512], F32)
        cstT = sb("cstT", [128, 16, NS], F32)
        upk = sb("upk", [128, 8, 18], F32)
        uptok = sb("uptok", [32, 512], F32)
        outs.append(S.dma("sp", lambda e: e.dma_start(out=conv_s[:, 0, :], in_=st_conv[:, 1, :])))
        for g in range(NG):
            wa, wak = load_slab([(w_up[:, g * 512:(g + 1) * 512], 0)], KC, 512)
            wb, wbk = load_slab([(w_up[:, DFF + g * 512:DFF + (g + 1) * 512], 0)], KC, 512)
            wd_, wdk = load_slab([(w_dn[g * 512:(g + 1) * 512, :], 0)], 4, D)
            pb, pk = bank()
            for r_ in range(2):
                for ab in range(2):
                    S.dma("sp", lambda e, ab=ab, g=g, r_=r_: e.dma_start(out=cst[:], in_=st_conv[:, r_, ab * DFF + g * 512:ab * DFF + (g + 1) * 512]), writes=["cst"])
                    for j in range(4):
                        idx = r_ * 8 + ab * 4 + j
                        S.op("pe", lambda e, j=j, idx=idx, pb=pb: e.transpose(out=pb[:, idx * NS:(idx + 1) * NS], in_=cst[0:NS, j * 128:(j + 1) * 128], identity=ident[0:NS, 0:NS]), reads=["cst", "ident"], writes=[pk])
            S.op("dve", lambda e, pb=pb: e.tensor_copy(out=cstT[:], in_=pb[:, 0:16 * NS].rearrange("p (k n) -> p k n", k=16)), reads=[pk], writes=["cstT"])
            for j in range(4):
                for ab in range(2):
                    W, Wk = (wa, wak) if ab == 0 else (wb, wbk)
                    fidx = ab * FC + g * 4 + j
                    dst, dstk = cab[ab], "cab%d" % ab
                    feat_mm(W, Wk, j, KC, lambda kc, c0, wd: h2T[:, kc, c0:c0 + wd], ["h2T"],
                            lambda pb, pk, c0, wd: S.op("act", lambda e: e.activation(out=upbuf[:, c0:c0 + wd], in_=pb[:, 0:wd], func=AF.Copy), reads=[pk], writes=["upbuf"]))
                    S.op("act", lambda e, dst=dst, fidx=fidx: e.activation(out=dst[:], in_=upbuf[:], func=AF.Identity, scale=cw_s3[:, 2, fidx:fidx + 1], bias=cb_s[:, fidx:fidx + 1]), reads=["upbuf", "consts"], writes=[dstk])
                    S.op("dve", lambda e, dst=dst, fidx=fidx: e.scalar_tensor_tensor(out=dst[:, 1:SC0], in0=upbuf[:, 0:SC0 - 1], scalar=cw_s3[:, 1, fidx:fidx + 1], in1=dst[:, 1:SC0], op0=ALU.mult, op1=ALU.add), reads=["upbuf", dstk, "consts"], writes=[dstk])
                    S.op("dve", lambda e, dst=dst, fidx=fidx: e.scalar_tensor_tensor(out=dst[:, 2:SC0], in0=upbuf[:, 0:SC0 - 2], scalar=cw_s3[:, 0, fidx:fidx + 1], in1=dst[:, 2:SC0], op0=ALU.mult, op1=ALU.add), reads=["upbuf", dstk, "consts"], writes=[dstk])
                    for r_ in range(2):
                        S.op("dve", lambda e, dst=dst, fidx=fidx, r_=r_, ab=ab, j=j: e.scalar_tensor_tensor(out=dst[:, SC0:SC0 + NS], in0=cstT[:, r_ * 8 + ab * 4 + j, :], scalar=cw_s3[:, r_, fidx:fidx + 1], in1=dst[:, SC0:SC0 + NS], op0=ALU.mult, op1=ALU.add), reads=["cstT", dstk, "consts"], writes=[dstk])
                    S.op("act", lambda e, ab=ab, j=j: e.activation(out=upk[:, ab * 4 + j, :], in_=upbuf[:, SC0 - 2:SC0 + NS], func=AF.Copy), reads=["upbuf"], writes=["upk"])
                S.op("act", lambda e: e.activation(out=cab[0][:], in_=cab[0][:], func=AF.Silu), reads=["cab0"], writes=["cab0"])
                S.op("dve", lambda e, j=j: e.tensor_tensor(out=gT[:, j, :], in0=cab[0][:], in1=cab[1][:], op=ALU.mult), reads=["cab0", "cab1"], writes=["gT"])
            for ab in range(2):
                pb, pk = bank()
                for j in range(4):
                    S.op("pe", lambda e, ab=ab, j=j, pb=pb: e.transpose(out=pb[0:18, j * 128:(j + 1) * 128], in_=upk[:, ab * 4 + j, :], identity=ident[:]), reads=["upk", "ident"], writes=[pk])
                S.op("act", lambda e, pb=pb: e.activation(out=uptok[0:18, :], in_=pb[0:18, :], func=AF.Copy), reads=[pk], writes=["uptok"])
                outs.append(S.dma("sp", lambda e, ab=ab, g=g: e.dma_start(out=conv_p[:, ab * DFF + g * 512:ab * DFF + (g + 1) * 512], in_=uptok[0:2, :]), reads=["uptok"]))
                outs.append(S.dma("sp", lambda e, ab=ab, g=g: e.dma_start(out=conv_s[:, 1, ab * DFF + g * 512:ab * DFF + (g + 1) * 512], in_=uptok[2:18, :]), reads=["uptok"]))
            for cb in range(D // 512):
                gp_, gs_, gkeys = load_gates(D, cb)
                for tt in range(1, NT + 1):
                    _, xk, n, c0_, xv, _g = tile_info(tt)
                    pb, pk = bank()
                    for j in range(4):
                        S.op("pe", lambda e, j=j, pb=pb, n=n, c0_=c0_, cb=cb, wd_=wd_: e.matmul(out=pb[0:n, :], lhsT=gT[:, j, c0_:c0_ + n], rhs=wd_[:, j, cb * 512:(cb + 1) * 512], start=(j == 0), stop=(j == 3)), reads=["gT", wdk], writes=[pk])
                    tf, tfk = tmp512()
                    gsrc = gs_[:] if tt == NT else gp_[:]
                    S.op("dve", lambda e, pb=pb, n=n, tf=tf, gsrc=gsrc: e.tensor_tensor(out=tf[0:n, :], in0=pb[0:n, :], in1=gsrc, op=ALU.mult), reads=[pk] + gkeys, writes=[tfk])
                    S.op("dve", lambda e, xv=xv, tf=tf, n=n, cb=cb: e.tensor_tensor(out=xv(cb * 512, (cb + 1) * 512), in0=xv(cb * 512, (cb + 1) * 512), in1=tf[0:n, :], op=ALU.add), reads=[tfk, xk], writes=[xk])
        pop_scope()

        push_scope()
        gbc = sb("gbc", [128, D], F32)
        S.dma("sp", lambda e: e.dma_start(out=gbc[:], in_=normf.partition_broadcast(128)), writes=["gbc"])
        fin = []
        NJ = max(KC // 4, 1)
        for tt in range(1, NT + 1):
            _, xk, n, c0_, xv, _g = tile_info(tt)
            st, sk = stat4()
            jr = (tt % NJ) * 4
            fin.append((tt, xk, n, xv, st, sk, h2T[0:n, jr:jr + 4, 0:D // 4], "junk%d" % (tt % NJ)))
        for (tt, xk, n, xv, st, sk, junk, jk) in fin:
            S.op("act", lambda e: e.activation(out=junk, in_=xv(0, D).rearrange("p (a b) -> p a b", a=4), func=AF.Square, accum_out=st[0:n, 0:1]), reads=[xk], writes=[jk, sk])
        for (tt, xk, n, xv, st, sk, junk, jk) in fin:
            S.op("dve", lambda e: e.tensor_scalar(out=st[0:n, 1:2], in0=st[0:n, 0:1], scalar1=1.0 / D, scalar2=1e-6, op0=ALU.mult, op1=ALU.add), reads=[sk], writes=[sk])
        for (tt, xk, n, xv, st, sk, junk, jk) in fin:
            S.op("act", lambda e: e.activation(out=st[0:n, 1:2], in_=st[0:n, 1:2], func=AF.Sqrt), reads=[sk], writes=[sk])
        for (tt, xk, n, xv, st, sk, junk, jk) in fin:
            S.op("dve", lambda e: e.reciprocal(out=st[0:n, 2:3], in_=st[0:n, 1:2]), reads=[sk], writes=[sk])
        for (tt, xk, n, xv, st, sk, junk, jk) in fin:
            S.op("act", lambda e: e.activation(out=xv(0, D), in_=xv(0, D), func=AF.Copy, scale=st[0:n, 2:3]), reads=[xk, sk], writes=[xk])
        for (tt, xk, n, xv, st, sk, junk, jk) in fin:
            S.op("dve", lambda e: e.tensor_tensor(out=xv(0, D), in0=xv(0, D), in1=gbc[0:n, :], op=ALU.mult), reads=[xk, "gbc"], writes=[xk])
            dst = y_s if tt == NT else y_main[(tt - 1) * 128:tt * 128, :]
            outs.append(S.dma("sp", lambda e: e.dma_start(out=dst, in_=xv(0, D)), reads=[xk]))
        S.emit(nc, outs)
        while len(scopes) > 1:
            scopes.pop().close()
    return nc


def _tables(cfg, half):
    H, NT, NPRE, NS = cfg.H, cfg.NT, cfg.NPRE, cfg.NS
    NCS = NT + NPRE + 1
    halfd = 128
    inv = (np.float32(10000.0) ** (-np.arange(halfd, dtype=np.float32) / np.float32(halfd))).astype(np.float32)
    pos = np.zeros((NCS, 128), np.float32)
    base = (half * cfg.NHT - cfg.NHT) * 128
    for w in range(NCS - 1):
        pos[w] = np.maximum(base + w * 128 + np.arange(128), 0)
    pos[NCS - 1] = cfg.PAST
    ang = pos[:, :, None].astype(np.float32) * inv[None, None, :]
    cos_t = np.ascontiguousarray(np.cos(ang).astype(np.float32).transpose(1, 0, 2))
    sin_t = np.ascontiguousarray(np.sin(ang).astype(np.float32).transpose(1, 0, 2))
    log_g = np.log(np.float32(1.0) - np.float32(2.0) ** (np.float32(-5.0) - np.arange(H, dtype=np.float32))).astype(np.float32)
    idx = np.arange(128, dtype=np.float32)
    diff = idx[None, :] - idx[:, None]
    dec = np.where(diff[:, None, :] >= 0, np.exp(np.maximum(diff, 0.0)[:, None, :] * log_g[None, :, None]), 0.0).astype(np.float32)
    gq = np.exp((idx[:, None] + 1.0) * log_g[None, :]).astype(np.float32)
    kd = np.exp((127.0 - idx[:, None]) * log_g[None, :]).astype(np.float32)
    invc = np.zeros((128, 4, 16), np.float32)
    p0 = half * cfg.NHT * 128
    for gi in range(4):
        w_ = 2 ** (gi + 1)
        invc[:, gi, :] = 1.0 / np.minimum(p0 + np.arange(16) + 1, w_).astype(np.float32)
    m16 = np.zeros((128, NS, NS), np.float32)
    for s in range(NS):
        m16[:, s, s] = 1.0
    return dict(cos_t=cos_t, sin_t=sin_t, decay_t=np.ascontiguousarray(dec), gq=gq, kd=kd, invc=invc,
                ident=np.eye(128, dtype=np.float32), m16=m16)


def _colT(v):
    return np.ascontiguousarray(v.reshape(-1, 128).T)


_PROG = {}


def run_cfg(cfg, inp):
    key = (cfg.D, cfg.H, cfg.DFF, cfg.NHT, cfg.NB)
    if key not in _PROG:
        _PROG[key] = build_program(cfg)
    nc = _PROG[key]
    D, H, NS, NHT, DFF = cfg.D, cfg.H, cfg.NS, cfg.NHT, cfg.DFF
    ncores = 2 * cfg.NB
    f = lambda a: np.ascontiguousarray(np.asarray(a, dtype=np.float32))
    shared = dict(
        norm1T=_colT(f(inp["norm1_g"][0])), norm2T=_colT(f(inp["norm2_g"][0])), bmodT=_colT(f(inp["b_mod"][0])),
        bmod_row=f(inp["b_mod"][0])[None, :], gn_g=f(inp["gn_g"][0])[None, :], pscaleT=_colT(f(inp["pool_scale"][0])),
        convwT=np.ascontiguousarray(f(inp["conv_w"][0]).reshape(3, -1, 128).transpose(2, 0, 1)), convbT=_colT(f(inp["conv_b"][0])),
        normf=f(inp["norm_f_g"])[None, :], w_mod=f(inp["w_mod"][0]), w_in=f(inp["w_in"][0]), w_proj_a=f(inp["w_proj_a"][0]),
        w_pool_grp=f(inp["w_pool_grp"][0]), w_proj_b=f(inp["w_proj_b"][0]), w_out=f(inp["w_out"][0]), w_up=f(inp["w_up"][0]),
        w_down=f(inp["w_down"][0]))
    tabs = [_tables(cfg, 0), _tables(cfg, 1)]
    xp, xsmp = f(inp["x_prompt"]), f(inp["x_sample"])
    in_maps = []
    for c in range(ncores):
        b, half = c // 2, c % 2
        xwin = np.zeros((2 * NHT * 128, D), np.float32)
        if half == 0:
            xwin[NHT * 128:] = xp[b, 0:NHT * 128]
        else:
            xwin[:] = xp[b]
        ss = slice(c * NS, (c + 1) * NS)
        m = dict(shared)
        m.update(tabs[half])
        m.update(xw=xwin, xs=f(xsmp[ss, 0]), c17=np.concatenate([f(inp["c_prompt"])[b:b + 1], f(inp["c_sample"])[ss]], 0),
                 premask=np.full((1, 128), float(half), np.float32),
                 st_ret=f(inp["state_ret"][0][ss]), st_pool=f(inp["state_pool"][0][ss]), st_conv=f(inp["state_conv"][0][ss]))
        in_maps.append(m)
    res = run_bass_kernel_spmd(nc, in_maps, core_ids=list(range(ncores)))
    R = res.results
    NB = cfg.NB
    y_p = np.stack([np.concatenate([R[2 * b]["y_main"], R[2 * b + 1]["y_main"]], 0) for b in range(NB)], 0)
    y_s = np.concatenate([R[c]["y_s"] for c in range(ncores)], 0)[:, None, :]
    ret_p = np.stack([R[2 * b + 1]["ret_p"] for b in range(NB)], 0)[None]
    ret_s = np.concatenate([R[c]["ret_s"] for c in range(ncores)], 0)[None]
    pool_p = np.stack([R[2 * b + 1]["pool_p"] for b in range(NB)], 0)[None]
    pool_s = np.concatenate([R[c]["pool_s"] for c in range(ncores)], 0)[None]
    conv_p = np.stack([R[2 * b + 1]["conv_p"] for b in range(NB)], 0)[None]
    conv_s = np.concatenate([R[c]["conv_s"] for c in range(ncores)], 0)[None]
    return tuple(np.ascontiguousarray(a.astype(np.float32)) for a in (y_p, y_s, ret_p, ret_s, pool_p, pool_s, conv_p, conv_s))


def kernel(**inputs):
    return run_cfg(Cfg(), inputs)
```

```python
import numpy as np
import concourse.bass as bass
import concourse.mybir as mybir
from concourse.bass_utils import run_bass_kernel_spmd

F32 = mybir.dt.float32
BF16 = mybir.dt.bfloat16
AF = mybir.ActivationFunctionType
ALU = mybir.AluOpType
AX = mybir.AxisListType
_EVPAR = 1


class _Op:
    __slots__ = ("eng", "fn", "deps", "is_dma", "sem", "semval", "signal", "prev_dma")

    def __init__(self, eng, fn, is_dma=False):
        self.eng = eng
        self.fn = fn
        self.deps = []
        self.is_dma = is_dma
        self.sem = None
        self.semval = None
        self.signal = False
        self.prev_dma = None


class _Rec:
    def __init__(self):
        self.call = None

    def __getattr__(self, name):
        def f(*a, **kw):
            self.call = (name, a, kw)
            return self
        return f


def _bind(fn):
    rec = _Rec()
    fn(rec)
    c = rec.call
    return lambda e: getattr(e, c[0])(*c[1], **c[2])


class Sched:
    ENGS = ("pe", "act", "dve", "pool", "sp")

    def __init__(self, ndma=20):
        self.ops = {e: [] for e in self.ENGS}
        self.last_w = {}
        self.readers = {}
        self.ndma = ndma
        self.dma_rr = {e: 0 for e in self.ENGS}
        self.dma_last = {}
        self.dma_count = {}
        self.last_ms = []

    def _add(self, op, reads, writes, extra=()):
        deps = list(extra)
        for k in reads:
            w = self.last_w.get(k)
            if w is not None:
                deps.append(w)
        for k in writes:
            w = self.last_w.get(k)
            if w is not None:
                if not ((not w.is_dma) and (not op.is_dma) and w.eng == op.eng and op.eng in ("dve", "act")):
                    deps.append(w)
            for r in self.readers.get(k, ()):
                if (not r.is_dma) and (not op.is_dma) and r.eng == op.eng and op.eng in ("dve", "act"):
                    continue
                deps.append(r)
        seen = set()
        for d in deps:
            if d is op or id(d) in seen:
                continue
            seen.add(id(d))
            if (not d.is_dma) and (not op.is_dma) and d.eng == "pe" and op.eng == "pe":
                continue
            op.deps.append(d)
        for k in reads:
            self.readers.setdefault(k, []).append(op)
        for k in writes:
            self.last_w[k] = op
            self.readers[k] = []
        self.ops[op.eng].append(op)
        return op

    def op(self, eng, fn, reads=(), writes=()):
        return self._add(_Op(eng, _bind(fn)), reads, writes)

    def dma(self, eng, fn, reads=(), writes=(), extra=()):
        op = _Op(eng, _bind(fn), is_dma=True)
        slot = (eng, self.dma_rr[eng] % self.ndma)
        self.dma_rr[eng] += 1
        op.sem = slot
        op.prev_dma = self.dma_last.get(slot)
        self.dma_count[slot] = self.dma_count.get(slot, 0) + 1
        op.semval = 16 * self.dma_count[slot]
        self.dma_last[slot] = op
        return self._add(op, reads, writes, extra)

    def barrier(self, markers):
        ms = [self.op(e, fn, reads, writes) for (e, fn, reads, writes) in markers]
        self.last_ms = ms
        lastd = [d for d in self.dma_last.values()]
        for e in self.ENGS:
            if e == "pool":
                continue
            b = _Op(e, None)
            b.deps = list(ms) + lastd
            self.ops[e].append(b)

    def emit(self, nc, final_deps):
        for e in self.ENGS:
            for op in self.ops[e]:
                for d in op.deps:
                    if not d.is_dma:
                        d.signal = True
        fence = _Op("sp", None)
        fence.deps = list(final_deps)
        for d in fence.deps:
            if not d.is_dma:
                d.signal = True
        self.ops["sp"].append(fence)
        for e in self.ENGS:
            n = 0
            for op in self.ops[e]:
                if op.is_dma or op.fn is None:
                    continue
                if op.signal:
                    n += 1
                    op.semval = n
                    op.sem = ("eng", e)
        sem_names = [("eng", e) for e in self.ENGS]
        sem_names += sorted(set(self.dma_last.keys()))
        ctx = []
        sems = {}
        for sn in sem_names:
            cm = nc.semaphore("s_" + "_".join(str(x) for x in sn))
            sems[sn] = cm.__enter__()
            ctx.append(cm)
        sched = self

        def run(engobj, ename):
            waited = {}

            def wait(sn, val):
                if waited.get(sn, 0) >= val:
                    return
                waited[sn] = val
                engobj.wait_ge(sems[sn], val)

            for op in sched.ops[ename]:
                for d in op.deps:
                    wait(d.sem, d.semval)
                if op.is_dma and op.prev_dma is not None:
                    wait(op.prev_dma.sem, op.prev_dma.semval)
                if op.fn is None:
                    continue
                ins = op.fn(engobj)
                if op.is_dma:
                    ins.then_inc(sems[op.sem], 16)
                elif op.signal:
                    ins.then_inc(sems[op.sem], 1)

        with nc.Block() as block:
            @block.tensor
            def _(e):
                run(e, "pe")

            @block.scalar
            def _(e):
                run(e, "act")

            @block.vector
            def _(e):
                run(e, "dve")

            @block.gpsimd
            def _(e):
                run(e, "pool")

            @block.sync
            def _(e):
                run(e, "sp")
        for cm in reversed(ctx):
            cm.__exit__(None, None, None)


class Cfg:
    def __init__(self, D=2048, H=8, DFF=5632, NHT=8, NS=16, PAST=16384, NB=4):
        self.D, self.H, self.DFF, self.NHT, self.NS, self.PAST, self.NB = D, H, DFF, NHT, NS, PAST, NB
        self.KC = D // 128
        self.DK, self.DV = 256, 512
        self.DQK, self.DVT = H * 256, H * 512
        self.DIN = 2 * self.DQK + 2 * self.DVT + 3 * D
        self.NPRE = NHT - 1
        self.NT = NHT + 1
        self.NCOL = self.NT * 128 + NS
        self.SC0 = self.NT * 128
        self.FC = DFF // 128
        self.NG = self.FC // 4
        self.offs = np.cumsum([0, self.DQK, self.DQK, self.DVT, self.DVT, D, D, D])
        self.SEQ = 2 * NHT * 128


def _ngroups(n):
    out, c = [], 0
    while c < n:
        w = min(512, n - c)
        out.append((c, w))
        c += w
    return out


def build_program(cfg):
    from contextlib import ExitStack
    D, H, KC, NT, NS, NCOL, SC0, NPRE = cfg.D, cfg.H, cfg.KC, cfg.NT, cfg.NS, cfg.NCOL, cfg.SC0, cfg.NPRE
    DFF, FC, NG = cfg.DFF, cfg.FC, cfg.NG
    nc = bass.Bass("TRN2", target_bir_lowering=False)
    S = Sched()

    def din(name, shape, dt=F32):
        return nc.dram_tensor(name, list(shape), dt, kind="ExternalInput").ap()

    def dout(name, shape, dt=F32):
        return nc.dram_tensor(name, list(shape), dt, kind="ExternalOutput").ap()

    def dscr(name, shape, dt):
        return nc.dram_tensor(name, list(shape), dt, kind="Internal").ap()

    xw = din("xw", [(NPRE + NT) * 128, D])
    xs = din("xs", [NS, D])
    c17 = din("c17", [NS + 1, D])
    premask = din("premask", [1, 128])
    cosT = din("cos_t", [128, NT + NPRE + 1, 128])
    sinT = din("sin_t", [128, NT + NPRE + 1, 128])
    decayT = din("decay_t", [128, H, 128])
    gq = din("gq", [128, H])
    kd = din("kd", [128, H])
    invc = din("invc", [128, 4, 16])
    ident_d = din("ident", [128, 128])
    m16_d = din("m16", [128, NS, NS])
    norm1T = din("norm1T", [128, KC])
    norm2T = din("norm2T", [128, KC])
    bmodT = din("bmodT", [128, 6 * KC])
    bmod_row = din("bmod_row", [1, 6 * D])
    gn_g = din("gn_g", [1, cfg.DVT])
    pscaleT = din("pscaleT", [128, KC])
    convwT = din("convwT", [128, 3, 2 * FC])
    convbT = din("convbT", [128, 2 * FC])
    normf = din("normf", [1, D])
    w_mod = din("w_mod", [D, 6 * D])
    w_in = din("w_in", [D, cfg.DIN])
    w_pa = din("w_proj_a", [cfg.DVT, D])
    w_pg = din("w_pool_grp", [4, D // 4, D // 4])
    w_pb = din("w_proj_b", [D, D])
    w_out = din("w_out", [D, D])
    w_up = din("w_up", [D, 2 * DFF])
    w_dn = din("w_down", [DFF, D])
    st_ret = din("st_ret", [NS, H, 256, 512])
    st_pool = din("st_pool", [NS, 15, D])
    st_conv = din("st_conv", [NS, 2, 2 * DFF])
    y_main = dout("y_main", [cfg.NHT * 128, D])
    y_s = dout("y_s", [NS, D])
    ret_p = dout("ret_p", [H, 256, 512])
    ret_s = dout("ret_s", [NS, H, 256, 512])
    pool_p = dout("pool_p", [15, D])
    pool_s = dout("pool_s", [NS, 15, D])
    conv_p = dout("conv_p", [2, 2 * DFF])
    conv_s = dout("conv_s", [NS, 2, 2 * DFF])
    hpre_d = dscr("hpre_d", [max(NPRE, 1), 128, KC * 128], BF16)
    actT_d = dscr("actT_d", [H * 4, 128, NCOL], BF16)
    sga_d = dscr("sga_d", [KC, 128, NCOL], BF16)
    sgb_d = dscr("sgb_d", [KC, 128, NCOL], BF16)

    outs = []
    uid = [0]

    def U(p="k"):
        uid[0] += 1
        return "%s%d" % (p, uid[0])

    es = ExitStack()
    with es:
        scopes = [es]

        def sb(name, shape, dt):
            return scopes[-1].enter_context(nc.sbuf_tensor("sb_" + name, list(shape), dt))

        def push_scope():
            scopes.append(ExitStack())

        def pop_scope():
            do_barrier()
            scopes.pop().close()

        banks = [es.enter_context(nc.psum_tensor("bank%d" % i, [128, 512], F32)) for i in range(4)]
        bank45 = es.enter_context(nc.psum_tensor("bank45", [128, 2, 512], F32))
        banks += [bank45[:, 0, :], bank45[:, 1, :]]
        banks += [es.enter_context(nc.psum_tensor("bank%d" % i, [128, 512], F32)) for i in (6, 7)]
        bank_rr = [0]

        def bank():
            i = bank_rr[0] % 7
            bank_rr[0] += 1
            return banks[i], "bank%d" % i

        NSLOT = 3
        slots = [sb("wslot%d" % i, [128, 8192], BF16) for i in range(NSLOT)]
        slot_rr = [0]

        def load_slab(parts, nk, ncols):
            i = slot_rr[0] % len(slots)
            slot_rr[0] += 1
            key = "wslot%d" % i
            view = slots[i][:, 0:nk * ncols].rearrange("p (k n) -> p k n", k=nk)
            extra = list(S.last_ms) if i >= 3 else []
            for ap, c0 in parts:
                w = ap.shape[1]
                S.dma("pool", lambda e, ap=ap, c0=c0, w=w: e.dma_start(
                    out=view[:, :, c0:c0 + w], in_=ap.rearrange("(k p) n -> p k n", p=128)), writes=[key], extra=extra)
            return view, key

        ident = sb("ident", [128, 128], F32)
        identb = sb("identb", [128, 128], BF16)
        S.dma("sp", lambda e: e.dma_start(out=ident[:], in_=ident_d), writes=["ident"])
        S.op("dve", lambda e: e.tensor_copy(out=identb[:], in_=ident[:]), reads=["ident"], writes=["identb"])
        bscr = sb("bscr", [128, 8], F32)

        def do_barrier():
            S.barrier([
                ("pe", lambda e: e.matmul(out=banks[6][0:1, 0:1], lhsT=identb[0:1, 0:1], rhs=identb[0:1, 0:1], start=True, stop=True), ["identb"], ["bank6", "bank6b"]),
                ("act", lambda e: e.activation(out=bscr[:, 0:1], in_=ident[:, 0:1], func=AF.Copy), ["ident"], ["bscr0"]),
                ("dve", lambda e: e.memset(bscr[:, 1:2], 0.0), [], ["bscr1"]),
            ])

        small_f = sb("small_f", [128, 9 * KC + 3 * H + 64 + 8 * FC], F32)
        o = 0
        def carve(n):
            nonlocal o
            v = small_f[:, o:o + n]
            o += n
            return v
        n1T, n2T, bmT0 = carve(KC), carve(KC), carve(6 * KC)
        gq_s, kd_s, invc_s, psc_s = carve(H), carve(H), carve(64), carve(KC)
        cw_s, cb_s = carve(6 * FC), carve(2 * FC)
        kd16 = carve(H)
        cw_s3 = cw_s.rearrange("p (r c) -> p r c", r=3)
        invc_s3 = invc_s.rearrange("p (g c) -> p g c", g=4)
        for dst, src in ((n1T, norm1T), (n2T, norm2T), (bmT0, bmodT), (gq_s, gq), (kd_s, kd), (psc_s, pscaleT), (cb_s, convbT)):
            S.dma("sp", lambda e, dst=dst, src=src: e.dma_start(out=dst, in_=src), writes=["consts"])
        S.dma("sp", lambda e: e.dma_start(out=cw_s3, in_=convwT), writes=["consts"])
        S.op("dve", lambda e: e.tensor_scalar(out=kd16, in0=kd_s, scalar1=0.0625, scalar2=None, op0=ALU.mult), reads=["consts"], writes=["kd16"])
        S.dma("sp", lambda e: e.dma_start(out=invc_s3, in_=invc), writes=["consts"])
        NCS = NT + NPRE + 1

        NM = NS + 1
        gate_d = dscr("gate_d", [NM, 2 * D], F32)
        modk_d = dscr("modk_d", [128, 4, KC * NM], F32)
        scT = sb("scT", [128, KC, NM], BF16)
        SPM = D // 512

        def mod_slab(sl, modbuf, ob0, gt_fn):
            wv, wk = load_slab([(w_mod[:, sl * 512:(sl + 1) * 512], 0)], KC, 512)
            pb, pk = bank()
            for j in range(4):
                for kc in range(KC):
                    S.op("pe", lambda e: e.matmul(out=pb[:, j * NM:(j + 1) * NM], lhsT=wv[:, kc, j * 128:(j + 1) * 128],
                         rhs=scT[:, kc, :], start=(kc == 0), stop=(kc == KC - 1)), reads=[wk, "scT"], writes=[pk])
            for j in range(4):
                ob = sl * 4 + j
                S.op("dve", lambda e: e.tensor_scalar(out=modbuf[:, ob - ob0, :], in0=pb[:, j * NM:(j + 1) * NM],
                     scalar1=bmT0[:, ob:ob + 1], scalar2=None, op0=ALU.add), reads=[pk, "consts"], writes=["modbuf"])
            mi = (sl * 512) // D
            if mi in (2, 5):
                gi = 0 if mi == 2 else 1
                c0 = sl * 512 - mi * D
                pb3, pk3 = bank()
                for kc in range(KC):
                    S.op("pe", lambda e: e.matmul(out=pb3[0:NM, :], lhsT=scT[:, kc, :], rhs=wv[:, kc, :],
                         start=(kc == 0), stop=(kc == KC - 1)), reads=[wk, "scT"], writes=[pk3])
                gt, gtk, br, brk = gt_fn()
                S.dma("sp", lambda e: e.dma_start(out=br[:], in_=bmod_row[:, mi * D + c0:mi * D + c0 + 512].partition_broadcast(NM)), writes=[brk])
                S.op("dve", lambda e: e.tensor_tensor(out=gt[:], in0=pb3[0:NM, :], in1=br[:], op=ALU.add), reads=[pk3, brk], writes=[gtk])
                S.dma("sp", lambda e: e.dma_start(out=gate_d[:, gi * D + c0:gi * D + c0 + 512], in_=gt[:]), reads=[gtk], writes=["gate_d"])

        def mod_finish(modbuf, A, Ak, nT, scale_rows, shift_rows, q0):
            S.op("dve", lambda e: e.tensor_scalar(out=A[:], in0=modbuf[:, scale_rows:scale_rows + KC, :], scalar1=1.0, scalar2=None, op0=ALU.add), reads=["modbuf"], writes=[Ak])
            S.op("dve", lambda e: e.tensor_tensor(out=A[:], in0=A[:], in1=nT.unsqueeze(2).to_broadcast([128, KC, NM]), op=ALU.mult), reads=[Ak, "consts"], writes=[Ak])
            S.dma("sp", lambda e: e.dma_start(out=modk_d[:, q0, :].rearrange("p (k n) -> p k n", k=KC), in_=A[:]), reads=[Ak], writes=["modk_d"])
            S.dma("sp", lambda e: e.dma_start(out=modk_d[:, q0 + 1, :].rearrange("p (k n) -> p k n", k=KC), in_=modbuf[:, shift_rows:shift_rows + KC, :]), reads=["modbuf"], writes=["modk_d"])

        push_scope()
        modT = sb("modT", [128, 2 * KC, NM], F32)
        A1 = sb("A1", [128, KC, NM], F32)
        csb = sb("csb", [NM, D], F32)
        S.dma("sp", lambda e: e.dma_start(out=csb[:], in_=c17), writes=["csb"])
        S.op("act", lambda e: e.activation(out=csb[:], in_=csb[:], func=AF.Silu), reads=["csb"], writes=["csb"])
        pb, pk = bank()
        pv = pb[:, 0:KC * NM].rearrange("p (k n) -> p k n", k=KC)
        for kc in range(KC):
            S.op("pe", lambda e, kc=kc: e.transpose(out=pv[:, kc, :], in_=csb[:, kc * 128:(kc + 1) * 128], identity=ident[0:NM, 0:NM]),
                 reads=["csb", "ident"], writes=[pk])
        S.op("dve", lambda e: e.tensor_copy(out=scT[:], in_=pv), reads=[pk], writes=["scT"])
        for sl in range(2 * SPM):
            mod_slab(sl, modT, 0, None)
        mod_finish(modT, A1, "A1", n1T, KC, 0, 0)
        pop_scope()

        stat = sb("stat", [128, 8 * 16], F32)
        st_rr = [0]
        stmp = sb("stmp", [128, 4, NS], F32)
        push_scope()
        hT = sb("hT", [128, KC, NCOL], BF16)
        push_scope()
        xring = [sb("xring%d" % i, [128, D], F32) for i in range(3)]
        xr_rr = [0]
        sqj = sb("sqj", [128, D], BF16)
        pmask = sb("pmask", [128, 128], F32)
        S.dma("sp", lambda e: e.dma_start(out=pmask[:], in_=premask.partition_broadcast(128)), writes=["consts"])
        evt = [sb("evt%d" % i, [128, 4, 128], F32) for i in range(2)]
        mA = sb("mA", [128, 2, KC, NM], F32)
        S.dma("sp", lambda e: e.dma_start(out=mA[:], in_=modk_d[:, 0:2, :].rearrange("p a (k n) -> p a k n", k=KC)), reads=["modk_d"], writes=["mA"])

        def stat4():
            i = st_rr[0] % 16
            st_rr[0] += 1
            return stat[:, i * 8:(i + 1) * 8], "stat%d" % i

        def rstd_of(xt, xk, n, st, sk, eps=1e-6):
            S.op("act", lambda e: e.activation(out=sqj[0:n, :], in_=xt[0:n, :], func=AF.Square, accum_out=st[0:n, 0:1]), reads=[xk], writes=["sqj", sk])
            S.op("dve", lambda e: e.tensor_scalar(out=st[0:n, 1:2], in0=st[0:n, 0:1], scalar1=1.0 / D, scalar2=eps, op0=ALU.mult, op1=ALU.add), reads=[sk], writes=[sk])
            S.op("act", lambda e: e.activation(out=st[0:n, 1:2], in_=st[0:n, 1:2], func=AF.Sqrt), reads=[sk], writes=[sk])
            S.op("dve", lambda e: e.reciprocal(out=st[0:n, 2:3], in_=st[0:n, 1:2]), reads=[sk], writes=[sk])

        def norm_T(xt, xk, n, dest_fn, dkey, mm, mmk, dest4_fn=None):
            st, sk = stat4()
            rstd_of(xt, xk, n, st, sk)
            S.op("dve", lambda e: e.tensor_scalar(out=xt[0:n, :], in0=xt[0:n, :], scalar1=st[0:n, 2:3], scalar2=None, op0=ALU.mult), reads=[xk, sk], writes=[xk])
            for k0 in range(0, KC, 4):
                pb, pk = bank()
                for j in range(min(4, KC - k0)):
                    kc = k0 + j
                    S.op("pe", lambda e, kc=kc, j=j, pb=pb: e.transpose(out=pb[:, j * 128:j * 128 + n], in_=xt[0:n, kc * 128:(kc + 1) * 128], identity=ident[0:n, 0:n]),
                         reads=[xk, "ident"], writes=[pk])
                if n == 128 and dest4_fn is not None and (k0 // 4) % 2 == _EVPAR and KC - k0 >= 4:
                    et, etk = evt[(k0 // 8) % 2], "evt%d" % ((k0 // 8) % 2)
                    S.op("dve", lambda e: e.tensor_tensor(out=et[:], in0=pb[:].rearrange("p (a n) -> p a n", a=4), in1=mm[:, 0, k0:k0 + 4, 0:1].to_broadcast([128, 4, 128]), op=ALU.mult),
                         reads=[pk, mmk], writes=[etk])
                    S.op("dve", lambda e: e.tensor_tensor(out=dest4_fn(k0), in0=et[:], in1=mm[:, 1, k0:k0 + 4, 0:1].to_broadcast([128, 4, 128]), op=ALU.add),
                         reads=[etk, mmk], writes=[dkey])
                    continue
                for j in range(min(4, KC - k0)):
                    kc = k0 + j
                    if n == 128:
                        S.op("act", lambda e, kc=kc, j=j, pb=pb: e.activation(out=dest_fn(kc), in_=pb[:, j * 128:(j + 1) * 128], func=AF.Identity,
                             scale=mm[:, 0, kc, 0:1], bias=mm[:, 1, kc, 0:1]), reads=[pk, mmk], writes=[dkey])
                    else:
                        S.op("dve", lambda e, kc=kc, j=j, pb=pb: e.tensor_tensor(out=stmp[:, j, :], in0=pb[:, j * 128:j * 128 + n], in1=mm[:, 0, kc, 1:NM], op=ALU.mult),
                             reads=[pk, mmk], writes=["stmp%d" % j])
                        S.op("dve", lambda e, kc=kc, j=j, pb=pb: e.tensor_tensor(out=dest_fn(kc), in0=stmp[:, j, :], in1=mm[:, 1, kc, 1:NM], op=ALU.add),
                             reads=["stmp%d" % j, mmk], writes=[dkey])

        def load_x(src_ap, n):
            i = xr_rr[0] % 3
            xr_rr[0] += 1
            xt, xk = xring[i], "xring%d" % i
            S.dma("sp", lambda e: e.dma_start(out=xt[0:n, :], in_=src_ap), writes=[xk])
            return xt, xk

        hstage = [sb("hstage%d" % i, [128, KC, 128], BF16) for i in range(2)]
        pm_b = pmask[:].unsqueeze(1).to_broadcast([128, KC, 128])
        for p in range(NPRE):
            xt, xk = load_x(xw[p * 128:(p + 1) * 128, :], 128)
            hs, hk = hstage[p % 2], "hstage%d" % (p % 2)
            norm_T(xt, xk, 128, lambda kc, hs=hs: hs[:, kc, :], hk, mA, "mA", dest4_fn=lambda k0, hs=hs: hs[:, k0:k0 + 4, :])
            S.op("dve", lambda e, hs=hs: e.tensor_tensor(out=hs[:], in0=hs[:], in1=pm_b, op=ALU.mult), reads=[hk, "consts"], writes=[hk])
            S.dma("sp", lambda e, hs=hs, p=p: e.dma_start(out=hpre_d[p].rearrange("p (k n) -> p k n", k=KC), in_=hs[:]), reads=[hk], writes=["hpre%d" % p])
        for t in range(NT):
            xt, xk = load_x(xw[(NPRE + t) * 128:(NPRE + t + 1) * 128, :], 128)
            norm_T(xt, xk, 128, lambda kc, t=t: hT[:, kc, t * 128:(t + 1) * 128], "hT%d" % t, mA, "mA", dest4_fn=lambda k0, t=t: hT[:, k0:k0 + 4, t * 128:(t + 1) * 128])
        S.op("dve", lambda e: e.tensor_tensor(out=hT[:, :, 0:128], in0=hT[:, :, 0:128], in1=pm_b, op=ALU.mult), reads=["hT0", "consts"], writes=["hT0"])
        xt, xk = load_x(xs, NS)
        norm_T(xt, xk, NS, lambda kc: hT[:, kc, SC0:SC0 + NS], "hTs", mA, "mA")
        HT_KEYS = ["hT%d" % t for t in range(NT)] + ["hTs"]
        pop_scope()

        push_scope()
        if True:
            sbh = sb
            decr = [sbh("decr%d" % i, [128, 128], F32) for i in range(2)]
            slots.append(sbh("wslot3", [128, 8192], BF16))
            m16 = sbh("m16", [128, NS, NS], F32)
            S.dma("sp", lambda e: e.dma_start(out=m16[:], in_=m16_d), writes=["consts"])
            cos_s = sbh("cos_s", [128, NCS, 128], F32)
            sin_s = sbh("sin_s", [128, NCS, 128], F32)
            S.dma("sp", lambda e: e.dma_start(out=cos_s[:], in_=cosT), writes=["consts"])
            S.dma("sp", lambda e: e.dma_start(out=sin_s[:], in_=sinT), writes=["consts"])
            hring = [sbh("hring%d" % i, [128, KC, 128], BF16) for i in range(2)]
            gnb = [sbh("gnb%d" % i, [128, 512], F32) for i in range(2)]
            rot = [sbh("rot%d" % i, [128, 2, 2, 128], F32) for i in range(2)]
            rtmp = [sbh("rtmp%d" % i, [128, 2, 2, 128], F32) for i in range(2)]
            tokb = [sbh("tokb%d" % i, [128, 8, 128], BF16) for i in range(2)]
            qkT = [sbh("qkT%d" % i, [128, 6, 128], BF16) for i in range(2)]
            vsb = [sbh("vsb%d" % i, [128, 512], BF16) for i in range(2)]
            sgr = [sbh("sgr%d" % i, [128, 512], F32) for i in range(2)]
            scs = [sbh("scs%d" % i, [128, 128], BF16) for i in range(2)]
            yn = [sbh("yn%d" % i, [128, 512], F32) for i in range(2)]
            actb = [sbh("actb%d" % i, [128, 512], BF16) for i in range(2)]
            Sst = sbh("Sst", [128, 2, 512], F32)
            Sbf = sbh("Sbf", [128, 2, 512], BF16)
            astage = sbh("astage", [128, 4, NCOL], BF16)
            sring = [sbh("sring%d" % i, [128, 2, 512], F32) for i in range(3)]
            NSR = 3
            tokb_s = sbh("tokb_s", [NS, 8, 128], BF16)
            vsb_s = sbh("vsb_s", [NS, 512], BF16)
            sgr_s = sbh("sgr_s", [NS, 512], F32)
            snb = [sbh("snb%d" % i, [128, 2, 512], BF16) for i in range(2)]
            KMH = max(NS // 2, 1)
            kmask_all = sbh("kmask_all", [NS, KMH, 256], BF16)
            qmask_all = sbh("qmask_all", [128, NS, 2, NS], BF16)
            qTs = sbh("qTs", [128, 2, NS], BF16)
            it = [0]
            sit = [0]
            log_g = [float(np.log(np.float32(1.0) - np.float32(2.0) ** np.float32(-5.0 - hh))) for hh in range(H)]

            ptr_v = banks[7][:].bitcast(BF16).rearrange("p (a n) -> p a n", a=8)
            psc = banks[7][:, 384:512]
            psck = "bank7s"
            ptr2_v = banks[6][:, 256:512].bitcast(BF16).rearrange("p (a n) -> p a n", a=4)
            pj_rr = [0]

            def pbank():
                i = pj_rr[0] % 2
                pj_rr[0] += 1
                return banks[i], "bank%d" % i

            class Ctx:
                pass

            def stageA_pe(c, which):
                n, lh, lk = c.n, c.lh, c.lk
                if which == "qk":
                    wv_, wk_ = c.wqk, c.wqkk
                    ncols = 512 if (c.is_main or c.is_s) else 256
                elif which == "v":
                    wv_, wk_ = c.wv, c.wvk
                    ncols = 512
                else:
                    if not (c.is_main or c.is_s):
                        return
                    wv_, wk_ = c.wg, c.wgk
                    ncols = 512
                pb, pk = pbank()
                c.pb[which] = (pb, pk)
                for kc in range(KC):
                    S.op("pe", lambda e: e.matmul(out=pb[0:n, 0:ncols], lhsT=lh(kc), rhs=wv_[:, kc, 0:ncols], start=(kc == 0), stop=(kc == KC - 1)), reads=lk + [wk_], writes=[pk])

            def stageA_rot(c):
                n, h, par = c.n, c.h, c.par
                pqk, pqkk = c.pb["qk"]
                na = 2 if (c.is_main or c.is_s) else 1
                ci = (NCS - 1) if c.is_s else c.w
                cb_ = cos_s[0:n, ci, :].unsqueeze(1).unsqueeze(1).to_broadcast([n, na, 2, 128])
                sb_ = sin_s[0:n, ci, :].unsqueeze(1).unsqueeze(1).to_broadcast([n, na, 2, 128])
                v4 = pqk[0:n, :].rearrange("p (a b n) -> p a b n", a=2, b=2)[:, 0:na]
                r, rk = rot[par], "rot%d" % par
                tc_, tck = rtmp[0][0:n, 0:na], "rtmp0"
                ts_, tsk = rtmp[1][0:n, 0:na], "rtmp1"
                S.op("dve", lambda e: e.tensor_tensor(out=tc_, in0=v4, in1=cb_, op=ALU.mult), reads=[pqkk, "consts"], writes=[tck])
                S.op("dve", lambda e: e.tensor_tensor(out=ts_, in0=v4, in1=sb_, op=ALU.mult), reads=[pqkk, "consts"], writes=[tsk])
                S.op("dve", lambda e: e.tensor_tensor(out=r[0:n, 0, 0:na, :], in0=tc_[:, :, 0, :], in1=ts_[:, :, 1, :], op=ALU.subtract), reads=[tck, tsk], writes=[rk])
                S.op("dve", lambda e: e.tensor_tensor(out=r[0:n, 1, 0:na, :], in0=tc_[:, :, 1, :], in1=ts_[:, :, 0, :], op=ALU.add), reads=[tck, tsk], writes=[rk])
                tb, tbk = c.tb, c.tbk
                if na == 2:
                    S.op("act", lambda e: e.activation(out=tb[0:n, 0:2, :], in_=r[0:n, :, 1, :], func=AF.Copy), reads=[rk], writes=[tbk])
                    if not c.is_s:
                        S.op("act", lambda e: e.activation(out=tb[0:n, 2:4, :], in_=r[0:n, :, 1, :], func=AF.Copy, scale=gq_s[0:n, h:h + 1]), reads=[rk, "consts"], writes=[tbk])
                    S.op("act", lambda e: e.mul(out=tb[0:n, 4:6, :], in_=r[0:n, :, 0, :], mul=0.0625), reads=[rk], writes=[tbk])
                if not c.is_s:
                    S.op("act", lambda e: e.activation(out=tb[0:n, 6:8, :], in_=r[0:n, :, 0, :], func=AF.Copy, scale=kd16[0:n, h:h + 1]), reads=[rk, "kd16"], writes=[tbk])

            def stageA_v(c):
                n = c.n
                pb, pk = c.pb["v"]
                S.op("act", lambda e: e.activation(out=c.vb[0:n, :], in_=pb[0:n, :], func=AF.Copy), reads=[pk], writes=[c.vbk])

            def stageA_g(c):
                if not (c.is_main or c.is_s):
                    return
                n = c.n
                pb, pk = c.pb["gr"]
                S.op("act", lambda e: e.activation(out=c.sg[0:n, :], in_=pb[0:n, :], func=AF.Silu), reads=[pk], writes=[c.sgk])
                g, gk = gnb[c.h % 2], "gnb%d" % (c.h % 2)
                S.op("dve", lambda e: e.tensor_tensor(out=c.sg[0:n, :], in0=c.sg[0:n, :], in1=g[0:n, :], op=ALU.mult), reads=[c.sgk, gk], writes=[c.sgk])

            def gn_part1(c, o_ps, ok, n):
                st, sk = stat4()
                st2, sk2 = stat4()
                c.gn = (o_ps, ok, n, st2, sk2)
                S.op("dve", lambda e: e.bn_stats(out=st[0:n, 0:6], in_=o_ps[0:n, :]), reads=[ok], writes=[sk])
                S.op("dve", lambda e: e.bn_aggr(out=st2[0:n, 0:2], in_=st[0:n, 0:6]), reads=[sk], writes=[sk2])
                S.op("dve", lambda e: e.tensor_scalar(out=st2[0:n, 2:3], in0=st2[0:n, 1:2], scalar1=1e-5, scalar2=None, op0=ALU.add), reads=[sk2], writes=[sk2])
                S.op("act", lambda e: e.activation(out=st2[0:n, 2:3], in_=st2[0:n, 2:3], func=AF.Sqrt), reads=[sk2], writes=[sk2])
                S.op("dve", lambda e: e.reciprocal(out=st2[0:n, 3:4], in_=st2[0:n, 2:3]), reads=[sk2], writes=[sk2])

            def gn_part2(c):
                if c is None or not hasattr(c, "gn"):
                    return
                o_ps, ok, n, st2, sk2 = c.gn
                par = c.par
                y, yk = yn[par], "yn%d" % par
                c.ab, c.abk = actb[par], "actb%d" % par
                S.op("dve", lambda e: e.tensor_scalar(out=y[0:n, :], in0=o_ps[0:n, :], scalar1=st2[0:n, 0:1], scalar2=st2[0:n, 3:4], op0=ALU.subtract, op1=ALU.mult), reads=[ok, sk2], writes=[yk])
                S.op("dve", lambda e: e.tensor_tensor(out=c.ab[0:n, :], in0=y[0:n, :], in1=c.sg[0:n, :], op=ALU.mult), reads=[yk, c.sgk], writes=[c.abk])

            def gn_chain(c, o_ps, ok, n):
                gn_part1(c, o_ps, ok, n)
                gn_part2(c)

            def stageB2(c):
                if c is None or not (c.is_main or c.is_s):
                    return
                n = c.n
                col0 = SC0 if c.is_s else (c.w - NPRE) * 128
                for j in range(4):
                    S.op("pe", lambda e: e.transpose(out=ptr2_v[:, j, 0:n], in_=c.ab[0:n, j * 128:(j + 1) * 128], identity=identb[0:n, 0:n]), reads=[c.abk, "identb"], writes=["bank6b"])
                S.op("act", lambda e: e.activation(out=astage[:, :, col0:col0 + n], in_=ptr2_v[:, 0:4, 0:n], func=AF.Copy), reads=["bank6b"], writes=["astage"])

            def make_ctx(h, w, slabs):
                c = Ctx()
                c.h, c.w = h, w
                c.is_s = (w == NPRE + NT)
                c.is_main = (w >= NPRE) and not c.is_s
                c.n = NS if c.is_s else 128
                c.par = it[0] % 2
                it[0] += 1
                c.pb = {}
                (c.wqk, c.wqkk), (c.wv, c.wvk), (c.wg, c.wgk) = slabs
                if c.is_s:
                    c.lh = lambda kc: hT[:, kc, SC0:SC0 + NS]
                    c.lk = ["hTs"]
                    c.tb, c.tbk, c.vb, c.vbk, c.sg, c.sgk = tokb_s, "tokb_s", vsb_s, "vsb_s", sgr_s, "sgr_s"
                else:
                    c.tb, c.tbk = tokb[c.par], "tokb%d" % c.par
                    c.vb, c.vbk = vsb[c.par], "vsb%d" % c.par
                    c.sg, c.sgk = sgr[c.par], "sgr%d" % c.par
                    if c.is_main:
                        t = w - NPRE
                        c.lh = lambda kc, t=t: hT[:, kc, t * 128:(t + 1) * 128]
                        c.lk = ["hT%d" % t]
                    else:
                        hr, hrk = hring[w % 2], "hring%d" % (w % 2)
                        S.dma("sp", lambda e: e.dma_start(out=hr[:], in_=hpre_d[w].rearrange("p (k n) -> p k n", k=KC)), reads=["hpre%d" % w], writes=[hrk])
                        c.lh = lambda kc, hr=hr: hr[:, kc, :]
                        c.lk = [hrk]
                return c

            NWT = NPRE + NT
            SPI = (NS + NWT - 1) // NWT
            for h in range(H):
                g1 = float(np.exp(np.float32(log_g[h])))
                g128 = float(np.exp(np.float32(128.0) * np.float32(log_g[h])))
                q0 = h * 256
                k0 = cfg.DQK + h * 256
                slabs = [load_slab([(w_in[:, k0:k0 + 256], 0), (w_in[:, q0:q0 + 256], 256)], KC, 512),
                         load_slab([(w_in[:, cfg.offs[2] + h * 512:cfg.offs[2] + (h + 1) * 512], 0)], KC, 512),
                         load_slab([(w_in[:, cfg.offs[3] + h * 512:cfg.offs[3] + (h + 1) * 512], 0)], KC, 512)]
                g_, gk_ = gnb[h % 2], "gnb%d" % (h % 2)
                S.dma("sp", lambda e: e.dma_start(out=g_[:], in_=gn_g[:, h * 512:(h + 1) * 512].partition_broadcast(128)), writes=[gk_])
                dcy, dcyk = decr[h % 2], "decr%d" % (h % 2)
                S.dma("sp", lambda e: e.dma_start(out=dcy[:], in_=decayT[:, h, :]), writes=[dcyk])
                sbase = sit[0]
                sit[0] += NS

                def sload(s_):
                    sr, srk = sring[(sbase + s_) % NSR], "sring%d" % ((sbase + s_) % NSR)
                    S.dma("sp", lambda e: e.dma_start(out=sr[:], in_=st_ret[s_, h].rearrange("(b p) v -> p b v", p=128)), writes=[srk])
                for s_ in range(min(NSR - 1, NS)):
                    sload(s_)
                S.op("dve", lambda e: e.memset(Sst[:], 0.0), writes=["Sst"])
                S.op("dve", lambda e: e.memset(Sbf[:], 0.0), writes=["Sbf"])
                cs_ = make_ctx(h, NPRE + NT, slabs)
                stageA_pe(cs_, "qk")
                stageA_rot(cs_)
                stageA_pe(cs_, "v")
                stageA_v(cs_)
                for j2 in range(2):
                    S.op("pe", lambda e: e.transpose(out=ptr_v[:, j2, 0:NS], in_=tokb_s[0:NS, j2, :], identity=identb[0:NS, 0:NS]), reads=["tokb_s", "identb"], writes=["ptr"])
                S.op("act", lambda e: e.activation(out=qTs[:], in_=ptr_v[:, 0:2, 0:NS], func=AF.Copy), reads=["ptr"], writes=["qTs"])

                def build_kmask(s0):
                    S.op("dve", lambda e: e.tensor_tensor(out=kmask_all[:], in0=tokb_s[0:NS, 4:6, :].rearrange("p a n -> p (a n)").unsqueeze(1).to_broadcast([NS, KMH, 256]),
                         in1=ident[0:NS, s0:s0 + KMH].unsqueeze(2).to_broadcast([NS, KMH, 256]), op=ALU.mult), reads=["tokb_s", "ident"], writes=["kmask_all"])
                build_kmask(0)
                S.op("dve", lambda e: e.tensor_tensor(out=qmask_all[:], in0=qTs[:].unsqueeze(1).to_broadcast([128, NS, 2, NS]),
                     in1=m16[:].unsqueeze(2).to_broadcast([128, NS, 2, NS]), op=ALU.mult), reads=["qTs", "consts"], writes=["qmask_all"])
                pos_, posk = banks[2], "bank2"
                sdone = [0]
                pendB = []

                def sample_A():
                    s_ = sdone[0]
                    if s_ >= NS:
                        return
                    sdone[0] += 1
                    if len(pendB) >= len(snb):
                        sample_B()
                    if s_ > 0 and s_ % KMH == 0:
                        build_kmask(s_)
                    si = sbase + s_
                    sr, srk = sring[si % NSR], "sring%d" % (si % NSR)
                    for blk in range(2):
                        S.op("pe", lambda e: e.matmul(out=bank45[:, blk, :], lhsT=kmask_all[:, s_ % KMH, blk * 128:(blk + 1) * 128], rhs=vsb_s[0:NS, :], start=True, stop=True), reads=["kmask_all", "vsb_s"], writes=["bank45"])
                    S.op("dve", lambda e: e.scalar_tensor_tensor(out=sr[:], in0=sr[:], scalar=g1, in1=bank45[:], op0=ALU.mult, op1=ALU.add), reads=["bank45", srk], writes=[srk])
                    outs.append(S.dma("act", lambda e: e.dma_start(out=ret_s[s_, h].rearrange("(b p) v -> p b v", p=128), in_=sr[:]), reads=[srk]))
                    sn, snk = snb[si % 2], "snb%d" % (si % 2)
                    S.op("act", lambda e: e.activation(out=sn[:], in_=sr[:], func=AF.Copy), reads=[srk], writes=[snk])
                    pendB.append((s_, sn, snk))
                    if s_ + NSR - 1 < NS:
                        sload(s_ + NSR - 1)

                def sample_B():
                    while pendB:
                        s_, sn, snk = pendB.pop(0)
                        for blk in range(2):
                            S.op("pe", lambda e: e.matmul(out=pos_[0:NS, :], lhsT=qmask_all[:, s_, blk, :], rhs=sn[:, blk, :], start=(s_ == 0 and blk == 0), stop=(s_ == NS - 1 and blk == 1)),
                                 reads=["qmask_all", snk], writes=[posk])

                ctxs = [None] * NWT
                ctxs[0] = make_ctx(h, 0, slabs)
                stageA_pe(ctxs[0], "qk")
                stageA_rot(ctxs[0])
                stageA_pe(ctxs[0], "v")
                stageA_v(ctxs[0])
                stageA_pe(ctxs[0], "gr")
                stageA_g(ctxs[0])
                prev = None
                for w in range(NWT):
                    c = ctxs[w]
                    nx = None
                    if w + 1 < NWT:
                        nx = ctxs[w + 1] = make_ctx(h, w + 1, slabs)
                    tb, tbk, vb, vbk = c.tb, c.tbk, c.vb, c.vbk
                    if c.is_main:
                        qt, qtk = qkT[c.par], "qkT%d" % c.par
                        for i6 in range(6):
                            S.op("pe", lambda e: e.transpose(out=ptr_v[:, i6, :], in_=tb[:, i6, :], identity=identb[:]), reads=[tbk, "identb"], writes=["ptr"])
                        S.op("act", lambda e: e.activation(out=qt[:], in_=ptr_v[:, 0:6, :], func=AF.Copy), reads=["ptr"], writes=[qtk])
                    if nx is not None:
                        stageA_pe(nx, "qk")
                        stageA_rot(nx)
                    if c.is_main:
                        for blk in range(2):
                            S.op("pe", lambda e: e.matmul(out=psc, lhsT=qt[:, 4 + blk, :], rhs=qt[:, blk, :], start=(blk == 0), stop=(blk == 1)), reads=[qtk], writes=[psck])
                        sc, sck = scs[c.par], "scs%d" % c.par
                        S.op("dve", lambda e: e.tensor_tensor(out=sc[:], in0=psc, in1=dcy[:], op=ALU.mult), reads=[psck, dcyk], writes=[sck])
                    stageB2(prev)
                    prev = c
                    for _ in range(SPI):
                        sample_A()
                    if nx is not None:
                        stageA_pe(nx, "v")
                        stageA_v(nx)
                    if c.is_main:
                        po, pok = banks[3], "bank3"
                        S.op("pe", lambda e: e.matmul(out=po[:], lhsT=sc[:], rhs=vb[:], start=True, stop=False), reads=[sck, vbk], writes=[pok])
                        for blk in range(2):
                            S.op("pe", lambda e: e.matmul(out=po[:], lhsT=qt[:, 2 + blk, :], rhs=Sbf[:, blk, :], start=False, stop=(blk == 1)), reads=[qtk, "Sbf"], writes=[pok])
                        gn_part1(c, po, pok, 128)
                    for blk in range(2):
                        S.op("pe", lambda e: e.matmul(out=bank45[:, blk, :], lhsT=tb[:, 6 + blk, :], rhs=vb[:], start=True, stop=True), reads=[tbk, vbk], writes=["bank45"])
                    S.op("dve", lambda e: e.scalar_tensor_tensor(out=Sst[:], in0=Sst[:], scalar=g128, in1=bank45[:], op0=ALU.mult, op1=ALU.add), reads=["bank45", "Sst"], writes=["Sst"])
                    S.op("act", lambda e: e.activation(out=Sbf[:], in_=Sst[:], func=AF.Copy), reads=["Sst"], writes=["Sbf"])
                    if w == NWT - 1:
                        outs.append(S.dma("sp", lambda e: e.dma_start(out=ret_p[h].rearrange("(b p) v -> p b v", p=128), in_=Sst[:]), reads=["Sst"]))
                    if nx is not None:
                        stageA_pe(nx, "gr")
                        stageA_g(nx)
                    sample_B()
                    if c.is_main:
                        gn_part2(c)
                while sdone[0] < NS:
                    sample_A()
                    sample_B()
                sample_B()
                stageB2(prev)
                stageA_pe(cs_, "gr")
                stageA_g(cs_)
                cs_.par = it[0] % 2
                it[0] += 1
                gn_chain(cs_, pos_, posk, NS)
                stageB2(cs_)
                S.dma("sp", lambda e: e.dma_start(out=actT_d[h * 4:(h + 1) * 4].rearrange("j p n -> p j n"), in_=astage[:]), reads=["astage"], writes=["actT_d%d" % h])
        slots.pop()
        slot_rr[0] = 0
        pop_scope()

        push_scope()
        GD = D // 4
        GC = GD // 128
        pT_d = dscr("pT_d", [KC, 128, NCOL], BF16)
        mix_d = dscr("mix_d", [KC, 128, NCOL], BF16)
        UB = 16 + NCOL
        ubuf = sb("ubuf", [128, UB], F32)
        pa = sb("pa", [128, UB], F32)
        pb2 = sb("pb2", [128, UB], F32)
        for bf_, k_ in ((ubuf, "ubuf"), (pa, "pa"), (pb2, "pb2")):
            S.op("dve", lambda e, bf_=bf_: e.memset(bf_[:, 0:16], 0.0), writes=[k_])
        gstage = sb("gstage", [128, 4, NCOL], BF16)
        ukeep = sb("ukeep", [128, KC, 32], F32)
        prevT = sb("prevT", [128, KC, NS], F32)
        prow = sb("prow", [NS, 15, GD], F32)
        ptok = sb("ptok", [NS, D], F32)
        utok = sb("utok", [32, D], F32)
        t16 = sb("t16", [128, 2, 16], F32)
        GRP = _ngroups(NCOL)
        for gi in range(4):
            w_ = 2 ** (gi + 1)
            nr = w_ - 1
            S.dma("sp", lambda e, gi=gi, nr=nr: e.dma_start(out=prow[:, 0:nr, :], in_=st_pool[:, 15 - nr:15, gi * GD:(gi + 1) * GD]), writes=["prow"])
            S.op("dve", lambda e, gi=gi: e.tensor_copy(out=ptok[:, gi * GD:(gi + 1) * GD], in_=prow[:, 0, :]), reads=["prow"], writes=["ptok"])
            for r_ in range(1, nr):
                S.op("dve", lambda e, gi=gi, r_=r_: e.tensor_tensor(out=ptok[:, gi * GD:(gi + 1) * GD], in0=ptok[:, gi * GD:(gi + 1) * GD], in1=prow[:, r_, :], op=ALU.add), reads=["prow", "ptok"], writes=["ptok"])
        def prev_transposes():
            pb, pk = bank()
            for kc in range(KC):
                S.op("pe", lambda e: e.transpose(out=pb[:, kc * NS:(kc + 1) * NS], in_=ptok[0:NS, kc * 128:(kc + 1) * 128], identity=ident[0:NS, 0:NS]), reads=["ptok", "ident"], writes=[pk])
            S.op("dve", lambda e: e.tensor_copy(out=prevT[:], in_=pb[:, 0:KC * NS].rearrange("p (k n) -> p k n", k=KC)), reads=[pk], writes=["prevT"])
        outs.append(S.dma("sp", lambda e: e.dma_start(out=pool_s[:, 0:14, :], in_=st_pool[:, 1:15, :])))

        def feat_mm(wv, wk, j, nk, rhs_fn, rkeys, evac):
            for (c0, wd) in GRP:
                pb, pk = bank()
                for kc in range(nk):
                    S.op("pe", lambda e, kc=kc, pb=pb, c0=c0, wd=wd: e.matmul(out=pb[:, 0:wd], lhsT=wv[:, kc, j * 128:(j + 1) * 128], rhs=rhs_fn(kc, c0, wd),
                         start=(kc == 0), stop=(kc == nk - 1)), reads=[wk] + rkeys, writes=[pk])
                evac(pb, pk, c0, wd)

        hrhs = lambda kc, c0, wd: hT[:, kc, c0:c0 + wd]
        modT2 = sb("modT2", [128, 4 * KC, NM], F32)
        A2 = sb("A2", [128, KC, NM], F32)
        gtmp = [sb("gtmp%d" % i, [NM, 512], F32) for i in range(2)]
        btmp = [sb("btmp%d" % i, [NM, 512], F32) for i in range(2)]
        gcnt = [0]

        def gt_fn():
            i = gcnt[0] % 2
            gcnt[0] += 1
            return gtmp[i], "gtmp%d" % i, btmp[i], "btmp%d" % i

        deferred = list(range(2 * SPM, 6 * SPM))
        g1_done = [0]
        N_G1 = 3 * (KC // 4)

        def run_deferred():
            g1_done[0] += 1
            target = (len(deferred_all) * g1_done[0] + N_G1 - 1) // N_G1
            while len(deferred_all) - len(deferred) < target and deferred:
                mod_slab(deferred.pop(0), modT2, 2 * KC, gt_fn)
            if not deferred and not fin[0]:
                fin[0] = True
                mod_finish(modT2, A2, "A2", n2T, 2 * KC, KC, 2)

        deferred_all = list(deferred)
        fin = [False]
        for i in range(KC // 4):
            wv, wk = load_slab([(w_in[:, cfg.offs[4] + i * 512:cfg.offs[4] + (i + 1) * 512], 0)], KC, 512)
            for j in range(4):
                c = i * 4 + j
                gi = c // GC
                w_ = 2 ** (gi + 1)
                feat_mm(wv, wk, j, KC, hrhs, HT_KEYS, lambda pb, pk, c0, wd: S.op("act", lambda e: e.activation(out=ubuf[:, 16 + c0:16 + c0 + wd], in_=pb[:, 0:wd], func=AF.Copy), reads=[pk], writes=["ubuf"]))
                S.op("act", lambda e, c=c: e.activation(out=ukeep[:, c, :], in_=ubuf[:, SC0:SC0 + 32], func=AF.Copy), reads=["ubuf"], writes=["ukeep"])
                if c == 0:
                    prev_transposes()
                src, srck = ubuf, "ubuf"
                for k_ in range(gi + 1):
                    sh = 2 ** k_
                    dst, dstk = (pa, "pa") if k_ % 2 == 0 else (pb2, "pb2")
                    S.op("dve", lambda e, src=src, dst=dst, sh=sh: e.tensor_tensor(out=dst[:, 16:16 + SC0], in0=src[:, 16:16 + SC0], in1=src[:, 16 - sh:16 + SC0 - sh], op=ALU.add), reads=[srck], writes=[dstk])
                    src, srck = dst, dstk
                S.op("dve", lambda e, src=src, j=j, w_=w_: e.scalar_tensor_tensor(out=gstage[:, j, 0:SC0], in0=src[:, 16:16 + SC0], scalar=1.0 / w_, in1=ubuf[:, 16:16 + SC0], op0=ALU.mult, op1=ALU.subtract), reads=[srck, "ubuf"], writes=["gstage"])
                S.op("dve", lambda e, src=src, gi=gi: e.tensor_tensor(out=t16[:, 0, :], in0=src[:, 144:160], in1=invc_s3[:, gi, :], op=ALU.mult), reads=[srck, "consts"], writes=["t16"])
                S.op("dve", lambda e, j=j: e.tensor_tensor(out=gstage[:, j, 128:144], in0=t16[:, 0, :], in1=ubuf[:, 144:160], op=ALU.subtract), reads=["t16", "ubuf"], writes=["gstage"])
                S.op("dve", lambda e, c=c: e.tensor_tensor(out=t16[:, 1, 0:NS], in0=ubuf[:, 16 + SC0:16 + SC0 + NS], in1=prevT[:, c, :], op=ALU.add), reads=["ubuf", "prevT"], writes=["t16b"])
                S.op("dve", lambda e, j=j, w_=w_: e.scalar_tensor_tensor(out=gstage[:, j, SC0:SC0 + NS], in0=t16[:, 1, 0:NS], scalar=1.0 / w_, in1=ubuf[:, 16 + SC0:16 + SC0 + NS], op0=ALU.mult, op1=ALU.subtract), reads=["t16b", "ubuf"], writes=["gstage"])
            S.dma("sp", lambda e, i=i: e.dma_start(out=pT_d[i * 4:(i + 1) * 4].rearrange("j p n -> p j n"), in_=gstage[:]), reads=["gstage"], writes=["pT_d%d" % i])
            run_deferred()
        for (oi, dst_d, dn) in ((5, sga_d, "sga_d"), (6, sgb_d, "sgb_d")):
            for i in range(KC // 4):
                wv, wk = load_slab([(w_in[:, cfg.offs[oi] + i * 512:cfg.offs[oi] + (i + 1) * 512], 0)], KC, 512)
                for j in range(4):
                    feat_mm(wv, wk, j, KC, hrhs, HT_KEYS, lambda pb, pk, c0, wd, j=j: S.op("act", lambda e: e.activation(out=gstage[:, j, c0:c0 + wd], in_=pb[:, 0:wd], func=AF.Sigmoid), reads=[pk], writes=["gstage"]))
                S.dma("sp", lambda e, i=i, dst_d=dst_d: e.dma_start(out=dst_d[i * 4:(i + 1) * 4].rearrange("j p n -> p j n"), in_=gstage[:]), reads=["gstage"], writes=["%s%d" % (dn, i)])
                run_deferred()
        for k0 in range(0, KC, 4):
            pb, pk = bank()
            for j in range(4):
                S.op("pe", lambda e, j=j, pb=pb, k0=k0: e.transpose(out=pb[0:32, j * 128:(j + 1) * 128], in_=ukeep[:, k0 + j, :], identity=ident[:]), reads=["ukeep", "ident"], writes=[pk])
            S.op("act", lambda e, pb=pb, k0=k0: e.activation(out=utok[:, k0 * 128:(k0 + 4) * 128], in_=pb[0:32, :], func=AF.Copy), reads=[pk], writes=["utok"])
        outs.append(S.dma("sp", lambda e: e.dma_start(out=pool_p, in_=utok[1:16, :]), reads=["utok"]))
        outs.append(S.dma("sp", lambda e: e.dma_start(out=pool_s[:, 14, :], in_=utok[16:32, :]), reads=["utok"]))
        pop_scope()
        pop_scope()

        push_scope()
        p2T = sb("p2T", [128, KC, NCOL], BF16)
        mixT = sb("mixT", [128, KC, NCOL], BF16)
        aTh = sb("aTh", [128, 2 * H, NCOL], BF16)
        pst = sb("pst", [128, 4, NCOL], BF16)
        tmpf = [sb("tmpf%d" % i, [128, 512], F32) for i in range(2)]
        tf_rr = [0]

        def tmp512():
            i = tf_rr[0] % len(tmpf)
            tf_rr[0] += 1
            return tmpf[i], "tmpf%d" % i

        for g in range(4):
            S.dma("sp", lambda e, g=g: e.dma_start(out=pst[:, 0:GC, :], in_=pT_d[g * GC:(g + 1) * GC].rearrange("j p n -> p j n")), reads=["pT_d%d" % i for i in range(KC // 4)], writes=["pst"])
            wv, wk = load_slab([(w_pg[g], 0)], GC, GD)
            for j in range(GC):
                c = g * GC + j
                feat_mm(wv, wk, j, GC, lambda kc, c0, wd: pst[:, kc, c0:c0 + wd], ["pst"],
                        lambda pb, pk, c0, wd, c=c: S.op("act", lambda e: e.activation(out=p2T[:, c, c0:c0 + wd], in_=pb[:, 0:wd], func=AF.Copy, scale=psc_s[:, c:c + 1]), reads=[pk, "consts"], writes=["p2T"]))
        for i in range(KC // 4):
            wv, wk = load_slab([(w_pb[:, i * 512:(i + 1) * 512], 0)], KC, 512)
            S.dma("sp", lambda e, i=i: e.dma_start(out=pst[:], in_=sgb_d[i * 4:(i + 1) * 4].rearrange("j p n -> p j n")), reads=["sgb_d%d" % i], writes=["pst"])
            for j in range(4):
                ob = i * 4 + j
                feat_mm(wv, wk, j, KC, lambda kc, c0, wd: p2T[:, kc, c0:c0 + wd], ["p2T"],
                        lambda pb, pk, c0, wd, j=j, ob=ob: S.op("dve", lambda e: e.tensor_tensor(out=mixT[:, ob, c0:c0 + wd], in0=pb[:, 0:wd], in1=pst[:, j, c0:c0 + wd], op=ALU.mult), reads=[pk, "pst"], writes=["mixT"]))
        for half in range(2):
            S.dma("sp", lambda e, half=half: e.dma_start(out=aTh[:], in_=actT_d[half * 2 * H:(half + 1) * 2 * H].rearrange("j p n -> p j n")), reads=["actT_d%d" % h for h in range(H)], writes=["aTh"])
            for i in range(KC // 4):
                wv, wk = load_slab([(w_pa[half * 2 * H * 128:(half + 1) * 2 * H * 128, i * 512:(i + 1) * 512], 0)], 2 * H, 512)
                S.dma("sp", lambda e, i=i: e.dma_start(out=pst[:], in_=sga_d[i * 4:(i + 1) * 4].rearrange("j p n -> p j n")), reads=["sga_d%d" % i], writes=["pst"])
                for j in range(4):
                    ob = i * 4 + j

                    def ev(pb, pk, c0, wd, j=j, ob=ob):
                        tf, tfk = tmp512()
                        S.op("dve", lambda e: e.tensor_tensor(out=tf[:, 0:wd], in0=pb[:, 0:wd], in1=pst[:, j, c0:c0 + wd], op=ALU.mult), reads=[pk, "pst"], writes=[tfk])
                        S.op("dve", lambda e: e.tensor_tensor(out=mixT[:, ob, c0:c0 + wd], in0=mixT[:, ob, c0:c0 + wd], in1=tf[:, 0:wd], op=ALU.add), reads=[tfk, "mixT"], writes=["mixT"])
                    feat_mm(wv, wk, j, 2 * H, lambda kc, c0, wd: aTh[:, kc, c0:c0 + wd], ["aTh"], ev)
        S.dma("sp", lambda e: e.dma_start(out=mix_d.rearrange("k p n -> p k n"), in_=mixT[:]), reads=["mixT"], writes=["mix_d"])
        pop_scope()

        push_scope()
        NHT = cfg.NHT
        x1 = sb("x1", [128, NHT, D], F32)
        samp = sb("samp", [NS, 1, D], F32)
        gpr = [sb("gpr%d" % i, [128, 512], F32) for i in range(2)]
        gsr = [sb("gsr%d" % i, [NS, 512], F32) for i in range(2)]
        g_rr = [0]

        def load_gates(goff, cb):
            i = g_rr[0] % 2
            g_rr[0] += 1
            S.dma("sp", lambda e: e.dma_start(out=gpr[i][:], in_=gate_d[0:1, goff + cb * 512:goff + (cb + 1) * 512].partition_broadcast(128)), reads=["gate_d"], writes=["gpr%d" % i])
            S.dma("sp", lambda e: e.dma_start(out=gsr[i][:], in_=gate_d[1:NM, goff + cb * 512:goff + (cb + 1) * 512]), reads=["gate_d"], writes=["gsr%d" % i])
            return gpr[i], gsr[i], ["gpr%d" % i, "gsr%d" % i]
        h2T = sb("h2T", [128, KC, NCOL], BF16)
        tmpf = [sb("tmpg%d" % i, [128, 512], F32) for i in range(3)]
        push_scope()
        xb = sb("xb", [128, D], F32)
        push_scope()
        mring = [sb("mring%d" % i, [128, KC, 128], BF16) for i in range(2)]

        def tile_info(tt):
            if tt == 0:
                return xb, "xb", 128, 0, lambda a, b: xb[:, a:b], None
            if tt == NT:
                return samp, "samp0", NS, SC0, lambda a, b: samp[:, 0, a:b], None
            return x1, "x1_%d" % tt, 128, tt * 128, lambda a, b, tt=tt: x1[:, tt - 1, a:b], None

        mr_rr = [0]
        for cb in range(D // 512):
            wv, wk = load_slab([(w_out[:, cb * 512:(cb + 1) * 512], 0)], KC, 512)
            gp_, gs_, gkeys = load_gates(0, cb)
            for tt in range(NT + 1):
                _, xk, n, c0_, xv, _g = tile_info(tt)
                mi_ = mr_rr[0] % 2
                mr_rr[0] += 1
                mr, mrk = mring[mi_], "mring%d" % mi_
                S.dma("sp", lambda e, mr=mr, c0_=c0_, n=n: e.dma_start(out=mr[:, :, 0:n], in_=mix_d[:, :, c0_:c0_ + n].rearrange("k p n -> p k n")), reads=["mix_d"], writes=[mrk])
                if cb == 0:
                    src = xs if tt == NT else xw[(NPRE + tt) * 128:(NPRE + tt + 1) * 128, :]
                    S.dma("sp", lambda e: e.dma_start(out=xv(0, D), in_=src), writes=[xk])
                pb, pk = bank()
                for kc in range(KC):
                    S.op("pe", lambda e, kc=kc, pb=pb, mr=mr, n=n, wv=wv: e.matmul(out=pb[0:n, :], lhsT=mr[:, kc, 0:n], rhs=wv[:, kc, :], start=(kc == 0), stop=(kc == KC - 1)), reads=[wk, mrk], writes=[pk])
                tf, tfk = tmp512()
                gsrc = gs_[:] if tt == NT else gp_[:]
                S.op("dve", lambda e, pb=pb, n=n, tf=tf, gsrc=gsrc: e.tensor_tensor(out=tf[0:n, :], in0=pb[0:n, :], in1=gsrc, op=ALU.mult), reads=[pk] + gkeys, writes=[tfk])
                S.op("dve", lambda e, xv=xv, tf=tf, n=n, cb=cb: e.tensor_tensor(out=xv(cb * 512, (cb + 1) * 512), in0=xv(cb * 512, (cb + 1) * 512), in1=tf[0:n, :], op=ALU.add), reads=[tfk, xk], writes=[xk])

        def norm_T2(xv, xk, n, junk, junkk):
            st, sk = stat4()
            S.op("act", lambda e: e.activation(out=junk, in_=xv(0, D).rearrange("p (a b) -> p a b", a=4), func=AF.Square, accum_out=st[0:n, 0:1]), reads=[xk], writes=[junkk, sk])
            S.op("dve", lambda e: e.tensor_scalar(out=st[0:n, 1:2], in0=st[0:n, 0:1], scalar1=1.0 / D, scalar2=1e-6, op0=ALU.mult, op1=ALU.add), reads=[sk], writes=[sk])
            S.op("act", lambda e: e.activation(out=st[0:n, 1:2], in_=st[0:n, 1:2], func=AF.Sqrt), reads=[sk], writes=[sk])
            S.op("dve", lambda e: e.reciprocal(out=st[0:n, 2:3], in_=st[0:n, 1:2]), reads=[sk], writes=[sk])
            return st, sk

        pop_scope()
        push_scope()
        xnb = [sb("xnb%d" % i, [128, D], F32) for i in range(2)]
        evt2 = [sb("evt2_%d" % i, [128, 4, 128], F32) for i in range(2)]
        pmask = sb("pmask2", [128, 128], F32)
        S.dma("sp", lambda e: e.dma_start(out=pmask[:], in_=premask.partition_broadcast(128)), writes=["pmask2"])
        mB = sb("mB", [128, 2, KC, NM], F32)
        S.dma("sp", lambda e: e.dma_start(out=mB[:], in_=modk_d[:, 2:4, :].rearrange("p a (k n) -> p a k n", k=KC)), reads=["modk_d"], writes=["mB"])
        tiles2 = []
        for tt in range(NT + 1):
            _, xk, n, c0_, xv, _g = tile_info(tt)
            st, sk = stat4()
            tiles2.append((tt, xk, n, c0_, xv, st, sk))
        for (tt, xk, n, c0_, xv, st, sk) in tiles2:
            xt, xtk = xnb[tt % 2], "xnb%d" % (tt % 2)
            S.op("act", lambda e: e.activation(out=xt[0:n, :].rearrange("p (a b) -> p a b", a=4), in_=xv(0, D).rearrange("p (a b) -> p a b", a=4), func=AF.Square, accum_out=st[0:n, 0:1]), reads=[xk], writes=[xtk, sk])
        for (tt, xk, n, c0_, xv, st, sk) in tiles2:
            S.op("dve", lambda e: e.tensor_scalar(out=st[0:n, 1:2], in0=st[0:n, 0:1], scalar1=1.0 / D, scalar2=1e-6, op0=ALU.mult, op1=ALU.add), reads=[sk], writes=[sk])
        for (tt, xk, n, c0_, xv, st, sk) in tiles2:
            S.op("act", lambda e: e.activation(out=st[0:n, 1:2], in_=st[0:n, 1:2], func=AF.Sqrt), reads=[sk], writes=[sk])
        for (tt, xk, n, c0_, xv, st, sk) in tiles2:
            S.op("dve", lambda e: e.reciprocal(out=st[0:n, 2:3], in_=st[0:n, 1:2]), reads=[sk], writes=[sk])
        H2_KEYS = ["h2T%d" % tt for tt in range(NT + 1)]
        for (tt, xk, n, c0_, xv, st, sk) in tiles2:
            xt, xtk = xnb[tt % 2], "xnb%d" % (tt % 2)
            hk2 = "h2T%d" % tt
            S.op("dve", lambda e: e.tensor_scalar(out=xt[0:n, :], in0=xv(0, D), scalar1=st[0:n, 2:3], scalar2=None, op0=ALU.mult), reads=[xk, sk], writes=[xtk])
            for k0 in range(0, KC, 4):
                pb, pk = bank()
                for j in range(4):
                    kc = k0 + j
                    S.op("pe", lambda e: e.transpose(out=pb[:, j * 128:j * 128 + n], in_=xt[0:n, kc * 128:(kc + 1) * 128], identity=ident[0:n, 0:n]), reads=[xtk, "ident"], writes=[pk])
                if n == 128 and (k0 // 4) % 2 == _EVPAR:
                    et, etk = evt2[(k0 // 8) % 2], "evt2_%d" % ((k0 // 8) % 2)
                    S.op("dve", lambda e: e.tensor_tensor(out=et[:], in0=pb[:].rearrange("p (a n) -> p a n", a=4), in1=mB[:, 0, k0:k0 + 4, 0:1].to_broadcast([128, 4, 128]), op=ALU.mult), reads=[pk, "mB"], writes=[etk])
                    S.op("dve", lambda e: e.tensor_tensor(out=h2T[:, k0:k0 + 4, c0_:c0_ + 128], in0=et[:], in1=mB[:, 1, k0:k0 + 4, 0:1].to_broadcast([128, 4, 128]), op=ALU.add), reads=[etk, "mB"], writes=[hk2])
                    continue
                for j in range(4):
                    kc = k0 + j
                    if n == 128:
                        S.op("act", lambda e: e.activation(out=h2T[:, kc, c0_:c0_ + 128], in_=pb[:, j * 128:(j + 1) * 128], func=AF.Identity,
                             scale=mB[:, 0, kc, 0:1], bias=mB[:, 1, kc, 0:1]), reads=[pk, "mB"], writes=[hk2])
                    else:
                        S.op("dve", lambda e: e.tensor_tensor(out=stmp[:, j, :], in0=pb[:, j * 128:j * 128 + n], in1=mB[:, 0, kc, 1:NM], op=ALU.mult), reads=[pk, "mB"], writes=["stmp%d" % j])
                        S.op("dve", lambda e: e.tensor_tensor(out=h2T[:, kc, SC0:SC0 + NS], in0=stmp[:, j, :], in1=mB[:, 1, kc, 1:NM], op=ALU.add), reads=["stmp%d" % j, "mB"], writes=[hk2])
        S.op("dve", lambda e: e.tensor_tensor(out=h2T[:, :, 0:128], in0=h2T[:, :, 0:128], in1=pmask[:].unsqueeze(1).to_broadcast([128, KC, 128]), op=ALU.mult), reads=["h2T0", "pmask2"], writes=["h2T0"])
        pop_scope()
        pop_scope()

        push_scope()
        gT = sb("gT", [128, 4, NCOL], BF16)
        upbuf = sb("upbuf", [128, NCOL], F32)
        cab = [sb("cab%d" % i, [128, NCOL], F32) for i in range(2)]
        cst = sb("cst", [NS, 512], F32)
        cstT = sb("cstT", [128, 16, NS], F32)
        upk = sb("upk", [128, 8, 18], F32)
        uptok = sb("uptok", [32, 512], F32)
        outs.append(S.dma("sp", lambda e: e.dma_start(out=conv_s[:, 0, :], in_=st_conv[:, 1, :])))
        pre_ab = None
        ev_cnt = [0]
        for g in range(NG):
            if pre_ab is None:
                wa, wak = load_slab([(w_up[:, g * 512:(g + 1) * 512], 0)], KC, 512)
                wb, wbk = load_slab([(w_up[:, DFF + g * 512:DFF + (g + 1) * 512], 0)], KC, 512)
            else:
                (wa, wak), (wb, wbk) = pre_ab
            wd_, wdk = load_slab([(w_dn[g * 512:(g + 1) * 512, :], 0)], 4, D)
            pb, pk = bank()
            for r_ in range(2):
                for ab in range(2):
                    S.dma("sp", lambda e, ab=ab, g=g, r_=r_: e.dma_start(out=cst[:], in_=st_conv[:, r_, ab * DFF + g * 512:ab * DFF + (g + 1) * 512]), writes=["cst"])
                    for j in range(4):
                        idx = r_ * 8 + ab * 4 + j
                        S.op("pe", lambda e, j=j, idx=idx, pb=pb: e.transpose(out=pb[:, idx * NS:(idx + 1) * NS], in_=cst[0:NS, j * 128:(j + 1) * 128], identity=ident[0:NS, 0:NS]), reads=["cst", "ident"], writes=[pk])
            S.op("dve", lambda e, pb=pb: e.tensor_copy(out=cstT[:], in_=pb[:, 0:16 * NS].rearrange("p (k n) -> p k n", k=16)), reads=[pk], writes=["cstT"])
            for j in range(4):
                for ab in range(2):
                    W, Wk = (wa, wak) if ab == 0 else (wb, wbk)
                    fidx = ab * FC + g * 4 + j
                    dst, dstk = cab[ab], "cab%d" % ab
                    feat_mm(W, Wk, j, KC, lambda kc, c0, wd: h2T[:, kc, c0:c0 + wd], H2_KEYS,
                            lambda pb, pk, c0, wd: S.op("act", lambda e: e.activation(out=upbuf[:, c0:c0 + wd], in_=pb[:, 0:wd], func=AF.Copy), reads=[pk], writes=["upbuf"]))
                    S.op("act", lambda e, dst=dst, fidx=fidx: e.activation(out=dst[:], in_=upbuf[:], func=AF.Identity, scale=cw_s3[:, 2, fidx:fidx + 1], bias=cb_s[:, fidx:fidx + 1]), reads=["upbuf", "consts"], writes=[dstk])
                    S.op("dve", lambda e, dst=dst, fidx=fidx: e.scalar_tensor_tensor(out=dst[:, 1:SC0], in0=upbuf[:, 0:SC0 - 1], scalar=cw_s3[:, 1, fidx:fidx + 1], in1=dst[:, 1:SC0], op0=ALU.mult, op1=ALU.add), reads=["upbuf", dstk, "consts"], writes=[dstk])
                    S.op("dve", lambda e, dst=dst, fidx=fidx: e.scalar_tensor_tensor(out=dst[:, 2:SC0], in0=upbuf[:, 0:SC0 - 2], scalar=cw_s3[:, 0, fidx:fidx + 1], in1=dst[:, 2:SC0], op0=ALU.mult, op1=ALU.add), reads=["upbuf", dstk, "consts"], writes=[dstk])
                    for r_ in range(2):
                        S.op("dve", lambda e, dst=dst, fidx=fidx, r_=r_, ab=ab, j=j: e.scalar_tensor_tensor(out=dst[:, SC0:SC0 + NS], in0=cstT[:, r_ * 8 + ab * 4 + j, :], scalar=cw_s3[:, r_, fidx:fidx + 1], in1=dst[:, SC0:SC0 + NS], op0=ALU.mult, op1=ALU.add), reads=["cstT", dstk, "consts"], writes=[dstk])
                    S.op("act", lambda e, ab=ab, j=j: e.activation(out=upk[:, ab * 4 + j, :], in_=upbuf[:, SC0 - 2:SC0 + NS], func=AF.Copy), reads=["upbuf"], writes=["upk"])
                S.op("act", lambda e: e.activation(out=cab[0][:], in_=cab[0][:], func=AF.Silu), reads=["cab0"], writes=["cab0"])
                S.op("dve", lambda e, j=j: e.tensor_tensor(out=gT[:, j, :], in0=cab[0][:], in1=cab[1][:], op=ALU.mult), reads=["cab0", "cab1"], writes=["gT"])
            if g + 1 < NG:
                pre_ab = (load_slab([(w_up[:, (g + 1) * 512:(g + 2) * 512], 0)], KC, 512),
                          load_slab([(w_up[:, DFF + (g + 1) * 512:DFF + (g + 2) * 512], 0)], KC, 512))
            else:
                pre_ab = None
            for ab in range(2):
                pb, pk = bank()
                for j in range(4):
                    S.op("pe", lambda e, ab=ab, j=j, pb=pb: e.transpose(out=pb[0:18, j * 128:(j + 1) * 128], in_=upk[:, ab * 4 + j, :], identity=ident[:]), reads=["upk", "ident"], writes=[pk])
                S.op("act", lambda e, pb=pb: e.activation(out=uptok[0:18, :], in_=pb[0:18, :], func=AF.Copy), reads=[pk], writes=["uptok"])
                outs.append(S.dma("sp", lambda e, ab=ab, g=g: e.dma_start(out=conv_p[:, ab * DFF + g * 512:ab * DFF + (g + 1) * 512], in_=uptok[0:2, :]), reads=["uptok"]))
                outs.append(S.dma("sp", lambda e, ab=ab, g=g: e.dma_start(out=conv_s[:, 1, ab * DFF + g * 512:ab * DFF + (g + 1) * 512], in_=uptok[2:18, :]), reads=["uptok"]))
            for cb in range(D // 512):
                gp_, gs_, gkeys = load_gates(D, cb)
                for tt in range(1, NT + 1):
                    _, xk, n, c0_, xv, _g = tile_info(tt)
                    pb, pk = bank()
                    for j in range(4):
                        S.op("pe", lambda e, j=j, pb=pb, n=n, c0_=c0_, cb=cb, wd_=wd_: e.matmul(out=pb[0:n, :], lhsT=gT[:, j, c0_:c0_ + n], rhs=wd_[:, j, cb * 512:(cb + 1) * 512], start=(j == 0), stop=(j == 3)), reads=["gT", wdk], writes=[pk])
                    tf, tfk = tmp512()
                    gsrc = gs_[:] if tt == NT else gp_[:]
                    S.op("dve", lambda e, pb=pb, n=n, tf=tf, gsrc=gsrc: e.tensor_tensor(out=tf[0:n, :], in0=pb[0:n, :], in1=gsrc, op=ALU.mult), reads=[pk] + gkeys, writes=[tfk])
                    ev_cnt[0] += 1
                    eng2 = "pool" if (tt != NT and ev_cnt[0] % 3 != 0) else "dve"
                    S.op(eng2, lambda e, xv=xv, tf=tf, n=n, cb=cb: e.tensor_tensor(out=xv(cb * 512, (cb + 1) * 512), in0=xv(cb * 512, (cb + 1) * 512), in1=tf[0:n, :], op=ALU.add), reads=[tfk, xk], writes=[xk])
        pop_scope()

        push_scope()
        gbc = sb("gbc", [128, D], F32)
        S.dma("sp", lambda e: e.dma_start(out=gbc[:], in_=normf.partition_broadcast(128)), writes=["gbc"])
        fin = []
        NJ = max(KC // 4, 1)
        for tt in range(1, NT + 1):
            _, xk, n, c0_, xv, _g = tile_info(tt)
            st, sk = stat4()
            jr = (tt % NJ) * 4
            fin.append((tt, xk, n, xv, st, sk, h2T[0:n, jr:jr + 4, 0:D // 4], "junk%d" % (tt % NJ)))
        for (tt, xk, n, xv, st, sk, junk, jk) in fin:
            S.op("act", lambda e: e.activation(out=junk, in_=xv(0, D).rearrange("p (a b) -> p a b", a=4), func=AF.Square, accum_out=st[0:n, 0:1]), reads=[xk], writes=[jk, sk])
        for (tt, xk, n, xv, st, sk, junk, jk) in fin:
            S.op("dve", lambda e: e.tensor_scalar(out=st[0:n, 1:2], in0=st[0:n, 0:1], scalar1=1.0 / D, scalar2=1e-6, op0=ALU.mult, op1=ALU.add), reads=[sk], writes=[sk])
        for (tt, xk, n, xv, st, sk, junk, jk) in fin:
            S.op("act", lambda e: e.activation(out=st[0:n, 1:2], in_=st[0:n, 1:2], func=AF.Sqrt), reads=[sk], writes=[sk])
        for (tt, xk, n, xv, st, sk, junk, jk) in fin:
            S.op("dve", lambda e: e.reciprocal(out=st[0:n, 2:3], in_=st[0:n, 1:2]), reads=[sk], writes=[sk])
        for (tt, xk, n, xv, st, sk, junk, jk) in fin:
            S.op("act", lambda e: e.activation(out=xv(0, D), in_=xv(0, D), func=AF.Copy, scale=st[0:n, 2:3]), reads=[xk, sk], writes=[xk])
        for (tt, xk, n, xv, st, sk, junk, jk) in fin:
            S.op("dve", lambda e: e.tensor_tensor(out=xv(0, D), in0=xv(0, D), in1=gbc[0:n, :], op=ALU.mult), reads=[xk, "gbc"], writes=[xk])
            dst = y_s if tt == NT else y_main[(tt - 1) * 128:tt * 128, :]
            outs.append(S.dma("sp", lambda e: e.dma_start(out=dst, in_=xv(0, D)), reads=[xk]))
        S.emit(nc, outs)
        while len(scopes) > 1:
            scopes.pop().close()
    return nc


def _tables(cfg, half):
    H, NT, NPRE, NS = cfg.H, cfg.NT, cfg.NPRE, cfg.NS
    NCS = NT + NPRE + 1
    halfd = 128
    inv = (np.float32(10000.0) ** (-np.arange(halfd, dtype=np.float32) / np.float32(halfd))).astype(np.float32)
    pos = np.zeros((NCS, 128), np.float32)
    base = (half * cfg.NHT - cfg.NHT) * 128
    for w in range(NCS - 1):
        pos[w] = np.maximum(base + w * 128 + np.arange(128), 0)
    pos[NCS - 1] = cfg.PAST
    ang = pos[:, :, None].astype(np.float32) * inv[None, None, :]
    cos_t = np.ascontiguousarray(np.cos(ang).astype(np.float32).transpose(1, 0, 2))
    sin_t = np.ascontiguousarray(np.sin(ang).astype(np.float32).transpose(1, 0, 2))
    log_g = np.log(np.float32(1.0) - np.float32(2.0) ** (np.float32(-5.0) - np.arange(H, dtype=np.float32))).astype(np.float32)
    idx = np.arange(128, dtype=np.float32)
    diff = idx[None, :] - idx[:, None]
    dec = np.where(diff[:, None, :] >= 0, np.exp(np.maximum(diff, 0.0)[:, None, :] * log_g[None, :, None]), 0.0).astype(np.float32)
    gq = np.exp((idx[:, None] + 1.0) * log_g[None, :]).astype(np.float32)
    kd = np.exp((127.0 - idx[:, None]) * log_g[None, :]).astype(np.float32)
    invc = np.zeros((128, 4, 16), np.float32)
    p0 = half * cfg.NHT * 128
    for gi in range(4):
        w_ = 2 ** (gi + 1)
        invc[:, gi, :] = 1.0 / np.minimum(p0 + np.arange(16) + 1, w_).astype(np.float32)
    m16 = np.zeros((128, NS, NS), np.float32)
    for s in range(NS):
        m16[:, s, s] = 1.0
    return dict(cos_t=cos_t, sin_t=sin_t, decay_t=np.ascontiguousarray(dec), gq=gq, kd=kd, invc=invc,
                ident=np.eye(128, dtype=np.float32), m16=m16)


def _colT(v):
    return np.ascontiguousarray(v.reshape(-1, 128).T)


_PROG = {}


def run_cfg(cfg, inp):
    key = (cfg.D, cfg.H, cfg.DFF, cfg.NHT, cfg.NB)
    if key not in _PROG:
        _PROG[key] = build_program(cfg)
    nc = _PROG[key]
    D, H, NS, NHT, DFF = cfg.D, cfg.H, cfg.NS, cfg.NHT, cfg.DFF
    ncores = 2 * cfg.NB
    f = lambda a: np.ascontiguousarray(np.asarray(a, dtype=np.float32))
    shared = dict(
        norm1T=_colT(f(inp["norm1_g"][0])), norm2T=_colT(f(inp["norm2_g"][0])), bmodT=_colT(f(inp["b_mod"][0])),
        bmod_row=f(inp["b_mod"][0])[None, :], gn_g=f(inp["gn_g"][0])[None, :], pscaleT=_colT(f(inp["pool_scale"][0])),
        convwT=np.ascontiguousarray(f(inp["conv_w"][0]).reshape(3, -1, 128).transpose(2, 0, 1)), convbT=_colT(f(inp["conv_b"][0])),
        normf=f(inp["norm_f_g"])[None, :], w_mod=f(inp["w_mod"][0]), w_in=f(inp["w_in"][0]), w_proj_a=f(inp["w_proj_a"][0]),
        w_pool_grp=f(inp["w_pool_grp"][0]), w_proj_b=f(inp["w_proj_b"][0]), w_out=f(inp["w_out"][0]), w_up=f(inp["w_up"][0]),
        w_down=f(inp["w_down"][0]))
    tabs = [_tables(cfg, 0), _tables(cfg, 1)]
    xp, xsmp = f(inp["x_prompt"]), f(inp["x_sample"])
    in_maps = []
    for c in range(ncores):
        b, half = c // 2, c % 2
        xwin = np.zeros((2 * NHT * 128, D), np.float32)
        if half == 0:
            xwin[NHT * 128:] = xp[b, 0:NHT * 128]
        else:
            xwin[:] = xp[b]
        ss = slice(c * NS, (c + 1) * NS)
        m = dict(shared)
        m.update(tabs[half])
        m.update(xw=xwin, xs=f(xsmp[ss, 0]), c17=np.concatenate([f(inp["c_prompt"])[b:b + 1], f(inp["c_sample"])[ss]], 0),
                 premask=np.full((1, 128), float(half), np.float32),
                 st_ret=f(inp["state_ret"][0][ss]), st_pool=f(inp["state_pool"][0][ss]), st_conv=f(inp["state_conv"][0][ss]))
        in_maps.append(m)
    res = run_bass_kernel_spmd(nc, in_maps, core_ids=list(range(ncores)))
    R = res.results
    NB = cfg.NB
    y_p = np.stack([np.concatenate([R[2 * b]["y_main"], R[2 * b + 1]["y_main"]], 0) for b in range(NB)], 0)
    y_s = np.concatenate([R[c]["y_s"] for c in range(ncores)], 0)[:, None, :]
    ret_p = np.stack([R[2 * b + 1]["ret_p"] for b in range(NB)], 0)[None]
    ret_s = np.concatenate([R[c]["ret_s"] for c in range(ncores)], 0)[None]
    pool_p = np.stack([R[2 * b + 1]["pool_p"] for b in range(NB)], 0)[None]
    pool_s = np.concatenate([R[c]["pool_s"] for c in range(ncores)], 0)[None]
    conv_p = np.stack([R[2 * b + 1]["conv_p"] for b in range(NB)], 0)[None]
    conv_s = np.concatenate([R[c]["conv_s"] for c in range(ncores)], 0)[None]
    return tuple(np.ascontiguousarray(a.astype(np.float32)) for a in (y_p, y_s, ret_p, ret_s, pool_p, pool_s, conv_p, conv_s))


def kernel(**inputs):
    return run_cfg(Cfg(), inputs)
```
